# Optimizing a Trainium2 kernel written in Bass

```python
import math
import jax, jax.numpy as jnp
from jax import lax
import numpy as np

D_MODEL = 2048
BATCH = 4
SEQ = 4096
DEPTH = 1

CHUNK = 64
SB_BLOCK = 128
SB_HEADS = 8
SB_HEAD_DIM = 128
GDN_HEADS = 8
GDN_DK = 128
GDN_DV = 128
CONV_W = 4
D_FF = 5504
EPS = 1e-6

D_SB = SB_HEADS * SB_HEAD_DIM
D_GDN_QK = GDN_HEADS * GDN_DK
D_GDN_V = GDN_HEADS * GDN_DV
D_GDN_CONV = 2 * D_GDN_QK + D_GDN_V
D_MIX = D_SB + D_GDN_V
D_IN = 3 * D_SB + D_GDN_CONV + 2 * GDN_HEADS + D_GDN_V

kernel_name = "hybrid_stickbreak_gdn_macaron"


def rms_norm(x, gain):
    x32 = x.astype(jnp.float32)
    y = x32 * lax.rsqrt(jnp.mean(x32 * x32, axis=-1, keepdims=True) + EPS)
    return (y * gain.astype(jnp.float32)).astype(x.dtype)


def l2_norm(x):
    x32 = x.astype(jnp.float32)
    return x32 * lax.rsqrt(jnp.sum(x32 * x32, axis=-1, keepdims=True) + EPS)


def swiglu(x, w_gate, w_up, w_down):
    return (jax.nn.silu(x @ w_gate) * (x @ w_up)) @ w_down


def stick_breaking_attention(q, k, v):
    bsz, nh, s_len, d = q.shape
    nblk = s_len // SB_BLOCK
    qb = q.reshape(bsz, nh, nblk, SB_BLOCK, d).transpose(2, 0, 1, 3, 4)
    key_pos = jnp.arange(s_len)
    scale = 1.0 / math.sqrt(d)

    def block(args):
        q_blk, i = args
        q_pos = i * SB_BLOCK + jnp.arange(SB_BLOCK)
        z = jnp.einsum('bhqd,bhkd->bhqk', q_blk, k,
                       preferred_element_type=jnp.float32) * scale
        mask = key_pos[None, :] < q_pos[:, None]
        log_beta = jax.nn.log_sigmoid(z)
        log_keep = jnp.where(mask, jax.nn.log_sigmoid(-z), 0.0)
        suffix = lax.cumsum(log_keep, axis=3, reverse=True) - log_keep
        w = jnp.where(mask, jnp.exp(log_beta + suffix), 0.0)
        return jnp.einsum('bhqk,bhkd->bhqd', w.astype(v.dtype), v)

    o = lax.map(block, (qb, jnp.arange(nblk)))
    return o.transpose(1, 2, 0, 3, 4).reshape(bsz, nh, s_len, d)


def causal_depthwise_conv(x, w):
    c = x.shape[-1]
    return lax.conv_general_dilated(
        x, w[:, None, :].astype(x.dtype), window_strides=(1,), padding=[(CONV_W - 1, 0)],
        dimension_numbers=('NWC', 'WIO', 'NWC'), feature_group_count=c)


def gated_delta_rule(q, k, v, g, beta):
    out_dtype = v.dtype
    bsz, nh, s_len, dk = q.shape
    dv = v.shape[-1]
    n = s_len // CHUNK
    f32 = jnp.float32
    q = q.astype(f32).reshape(bsz, nh, n, CHUNK, dk)
    k = k.astype(f32).reshape(bsz, nh, n, CHUNK, dk)
    v = v.astype(f32).reshape(bsz, nh, n, CHUNK, dv)
    beta = beta.astype(f32).reshape(bsz, nh, n, CHUNK)
    g = jnp.cumsum(g.astype(f32).reshape(bsz, nh, n, CHUNK), axis=-1)

    incl = jnp.tril(jnp.ones((CHUNK, CHUNK), dtype=bool))
    strict = jnp.tril(jnp.ones((CHUNK, CHUNK), dtype=bool), k=-1)
    decay = jnp.exp(jnp.where(incl, g[..., :, None] - g[..., None, :], -jnp.inf))

    kb = k * beta[..., None]
    a_strict = jnp.where(strict, jnp.einsum('bhncd,bhnsd->bhncs', kb, k) * decay, 0.0)
    t_lhs = a_strict + jnp.eye(CHUNK, dtype=f32)
    u = lax.linalg.triangular_solve(t_lhs, v * beta[..., None],
                                    left_side=True, lower=True, unit_diagonal=True)
    w = lax.linalg.triangular_solve(t_lhs, kb * jnp.exp(g)[..., None],
                                    left_side=True, lower=True, unit_diagonal=True)
    attn_qk = jnp.einsum('bhncd,bhnsd->bhncs', q, k) * decay
    g_last = g[..., -1]
    k_to_end = k * jnp.exp(g_last[..., None] - g)[..., None]
    q_dec = q * jnp.exp(g)[..., None]

    def mv(t):
        return jnp.moveaxis(t, 2, 0)

    def step(state, xs):
        u_c, w_c, qd_c, aqk_c, kend_c, gl_c = xs
        v_new = u_c - jnp.einsum('bhck,bhkv->bhcv', w_c, state)
        o_c = (jnp.einsum('bhck,bhkv->bhcv', qd_c, state)
               + jnp.einsum('bhcs,bhsv->bhcv', aqk_c, v_new))
        state = (state * jnp.exp(gl_c)[..., None, None]
                 + jnp.einsum('bhck,bhcv->bhkv', kend_c, v_new))
        return state, o_c

    state0 = jnp.zeros((bsz, nh, dk, dv), dtype=f32)
    _, o = lax.scan(step, state0, (mv(u), mv(w), mv(q_dec), mv(attn_qk), mv(k_to_end), mv(g_last)))
    o = jnp.moveaxis(o, 0, 2).reshape(bsz, nh, s_len, dv)
    return o.astype(out_dtype)


def hybrid_mixer(h, w_in, sb_out_norm, conv_w, a_log, dt_bias, gdn_out_norm, w_out):
    bsz, s_len, _ = h.shape
    proj = h @ w_in
    offs = np.cumsum([0, D_SB, D_SB, D_SB, D_GDN_CONV, GDN_HEADS, GDN_HEADS, D_GDN_V])
    q_sb, k_sb, v_sb, qkv_g, a_g, b_g, gate_g = [proj[..., offs[i]:offs[i + 1]] for i in range(7)]

    def heads(t, nh, d):
        return t.reshape(bsz, s_len, nh, d).transpose(0, 2, 1, 3)

    o_sb = stick_breaking_attention(heads(q_sb, SB_HEADS, SB_HEAD_DIM),
                                    heads(k_sb, SB_HEADS, SB_HEAD_DIM),
                                    heads(v_sb, SB_HEADS, SB_HEAD_DIM))
    o_sb = rms_norm(o_sb, sb_out_norm).transpose(0, 2, 1, 3).reshape(bsz, s_len, D_SB)

    qkv = jax.nn.silu(causal_depthwise_conv(qkv_g, conv_w))
    q_g = qkv[..., :D_GDN_QK]
    k_g = qkv[..., D_GDN_QK:2 * D_GDN_QK]
    v_g = qkv[..., 2 * D_GDN_QK:]
    q_g = l2_norm(heads(q_g, GDN_HEADS, GDN_DK)) * (1.0 / math.sqrt(GDN_DK))
    k_g = l2_norm(heads(k_g, GDN_HEADS, GDN_DK))
    v_g = heads(v_g, GDN_HEADS, GDN_DV)
    decay_log = -jnp.exp(a_log.astype(jnp.float32)) * jax.nn.softplus(
        a_g.astype(jnp.float32) + dt_bias.astype(jnp.float32))
    beta = jax.nn.sigmoid(b_g.astype(jnp.float32))
    o_g = gated_delta_rule(q_g, k_g, v_g, decay_log.transpose(0, 2, 1), beta.transpose(0, 2, 1))
    o_g = o_g.transpose(0, 2, 1, 3)
    gate = gate_g.reshape(bsz, s_len, GDN_HEADS, GDN_DV)
    o_g = (rms_norm(o_g, gdn_out_norm) * jax.nn.silu(gate)).reshape(bsz, s_len, D_GDN_V)

    y = jnp.concatenate([o_sb, o_g], axis=-1)
    return y @ w_out


def setup_inputs(seed: int = 0) -> dict:
    key = jax.random.key(seed)
    ks = jax.random.split(key, 20)
    f32 = jnp.float32

    def nrm(k, shape, fan_in):
        return jax.random.normal(k, shape, f32) * (fan_in ** -0.5)

    def gain(k, shape):
        return 1.0 + 0.02 * jax.random.normal(k, shape, f32)

    L = DEPTH
    x = jax.random.normal(ks[0], (BATCH, SEQ, D_MODEL), f32)
    dt = jnp.exp(jax.random.uniform(ks[9], (L, GDN_HEADS), f32, math.log(1e-3), math.log(1e-1)))
    dt_bias = dt + jnp.log(-jnp.expm1(-dt))
    a_log = jnp.log(jax.random.uniform(ks[8], (L, GDN_HEADS), f32, 1.0, 16.0))
    return {
        "x": x,
        "ffn1_norm": gain(ks[1], (L, D_MODEL)),
        "ffn1_w_gate": nrm(ks[2], (L, D_MODEL, D_FF), D_MODEL),
        "ffn1_w_up": nrm(ks[3], (L, D_MODEL, D_FF), D_MODEL),
        "ffn1_w_down": nrm(ks[4], (L, D_FF, D_MODEL), D_FF),
        "mix_norm": gain(ks[5], (L, D_MODEL)),
        "w_in": nrm(ks[6], (L, D_MODEL, D_IN), D_MODEL),
        "sb_out_norm": gain(ks[7], (L, SB_HEAD_DIM)),
        "conv_w": nrm(ks[10], (L, CONV_W, D_GDN_CONV), CONV_W),
        "a_log": a_log,
        "dt_bias": dt_bias,
        "gdn_out_norm": gain(ks[11], (L, GDN_DV)),
        "w_out": nrm(ks[12], (L, D_MIX, D_MODEL), D_MIX),
        "ffn2_norm": gain(ks[13], (L, D_MODEL)),
        "ffn2_w_gate": nrm(ks[14], (L, D_MODEL, D_FF), D_MODEL),
        "ffn2_w_up": nrm(ks[15], (L, D_MODEL, D_FF), D_MODEL),
        "ffn2_w_down": nrm(ks[16], (L, D_FF, D_MODEL), D_FF),
        "final_norm": gain(ks[17], (D_MODEL,)),
    }


def reference(x, ffn1_norm, ffn1_w_gate, ffn1_w_up, ffn1_w_down, mix_norm, w_in,
              sb_out_norm, conv_w, a_log, dt_bias, gdn_out_norm, w_out,
              ffn2_norm, ffn2_w_gate, ffn2_w_up, ffn2_w_down, final_norm):
    for l in range(DEPTH):
        x = x + 0.5 * swiglu(rms_norm(x, ffn1_norm[l]), ffn1_w_gate[l], ffn1_w_up[l], ffn1_w_down[l])
        h = rms_norm(x, mix_norm[l])
        x = x + hybrid_mixer(h, w_in[l], sb_out_norm[l], conv_w[l], a_log[l], dt_bias[l],
                             gdn_out_norm[l], w_out[l])
        x = x + 0.5 * swiglu(rms_norm(x, ffn2_norm[l]), ffn2_w_gate[l], ffn2_w_up[l], ffn2_w_down[l])
    return rms_norm(x, final_norm)
```

```python
from contextlib import ExitStack
import numpy as np
import concourse.bass as bass
import concourse.mybir as mybir
from concourse.bass_utils import run_bass_kernel_spmd

F32 = mybir.dt.float32
BF16 = mybir.dt.bfloat16
AF = mybir.ActivationFunctionType
ALU = mybir.AluOpType
AX = mybir.AxisListType

D = 2048
DFF = 5504
NF = DFF // 128
NKC = D // 128
SEQ = 4096
EPS = 1e-6
ENGS = ("tensor", "scalar", "vector", "gpsimd", "sync")


class Buf:
    __slots__ = ("name", "w", "r", "rd", "dsem")

    def __init__(self, name):
        self.name = name
        self.w = None
        self.r = {}
        self.rd = []
        self.dsem = None


class Op:
    __slots__ = ("eng", "fn", "deps", "sig", "sem", "val", "inc", "is_dma", "ph")

    def __init__(self, eng, fn, is_dma):
        self.eng = eng
        self.fn = fn
        self.deps = []
        self.sig = False
        self.sem = None
        self.val = None
        self.inc = 1
        self.is_dma = is_dma


class Prog:
    def __init__(self, nc, stack, n_dma_sems=40):
        self.nc = nc
        self.stack = stack
        self.dma_sems = []
        for i in range(n_dma_sems):
            h = stack.enter_context(nc.semaphore(f"dq{i}"))
            self.dma_sems.append([h, 0])
        self.n_dma_sems = n_dma_sems

    def new_phase(self, name):
        self.pname = name
        self.ops = {e: [] for e in ENGS}
        self.phase_dma_ops = []
        self.prog_sems = {}
        for e in ENGS[:4]:
            self.prog_sems[e] = self.stack.enter_context(self.nc.semaphore(f"pg_{name}_{e}"))
        self.free_dma = {"sync": list(range(12, self.n_dma_sems)), "gpsimd": list(range(0, 12)),
                         "scalar": []}

    def op(self, eng, fn, reads=(), writes=(), same_ok=False):
        o = Op(eng, fn, False)
        self._deps(o, reads, writes, same_ok)
        self.ops[eng].append(o)
        return o

    def dma(self, eng, fn, reads=(), writes=(), sembuf=None):
        o = Op(eng, fn, True)
        self._deps(o, reads, writes, False)
        if sembuf.dsem is None or sembuf.dsem[0] != self.pname:
            sembuf.dsem = (self.pname, self.free_dma[eng].pop())
        ent = self.dma_sems[sembuf.dsem[1]]
        ent[1] += 16
        o.sem = ent[0]
        o.val = ent[1]
        o.inc = 16
        o.sig = True
        self.ops[eng].append(o)
        self.phase_dma_ops.append(o)
        return o

    def _deps(self, o, reads, writes, same_ok):
        o.ph = self.pname
        deps = []
        for b in reads:
            if b.w is not None:
                deps.append(b.w)
        for b in writes:
            if b.w is not None:
                deps.append(b.w)
            deps.extend(b.r.values())
            deps.extend(b.rd)
        for d in deps:
            if d is o or d.ph != o.ph:
                continue
            if same_ok and (not d.is_dma) and d.eng == o.eng:
                continue
            o.deps.append(d)
        for b in reads:
            if o.is_dma:
                b.rd.append(o)
            else:
                b.r[o.eng] = o
        for b in writes:
            b.w = o
            b.r = {}
            b.rd = []

    def emit(self):
        nc = self.nc
        for e in ENGS:
            for o in self.ops[e]:
                for d in o.deps:
                    d.sig = True
        for e in ENGS[:4]:
            cnt = 0
            for o in self.ops[e]:
                if o.is_dma:
                    continue
                if o.sig:
                    cnt += 1
                    o.sem = self.prog_sems[e]
                    o.val = cnt
        finals = {e: {} for e in ENGS}
        for o in self.phase_dma_ops:
            k = id(o.sem)
            cur = finals[o.eng].get(k)
            if cur is None or cur[1] < o.val:
                finals[o.eng][k] = (o.sem, o.val)
        ops = self.ops

        def replay(e, eng):
            waited = {}
            for o in ops[e]:
                need = {}
                for d in o.deps:
                    k = id(d.sem)
                    if waited.get(k, 0) >= d.val:
                        continue
                    if k not in need or need[k][1] < d.val:
                        need[k] = (d.sem, d.val)
                for k, (s, v) in need.items():
                    eng.wait_ge(s, v)
                    waited[k] = v
                ins = o.fn(eng)
                if o.sig:
                    ins.then_inc(o.sem, o.inc)
            for k, (s, v) in finals[e].items():
                if waited.get(k, 0) < v:
                    eng.wait_ge(s, v)

        with nc.Block() as block:
            if ops["sync"]:
                @block.sync
                def _(eng):
                    replay("sync", eng)
            if ops["gpsimd"]:
                @block.gpsimd
                def _(eng):
                    replay("gpsimd", eng)
            if ops["tensor"]:
                @block.tensor
                def _(eng):
                    replay("tensor", eng)
            if ops["scalar"]:
                @block.scalar
                def _(eng):
                    replay("scalar", eng)
            if ops["vector"]:
                @block.vector
                def _(eng):
                    replay("vector", eng)
        self.ops = None
        self.phase_dma_ops = None


class Tile:
    def __init__(self, t, name, nbuf=1):
        self.t = t
        self.b = Buf(name)


_UID = [0]


def sb(nc, stack, name, shape, dtype):
    _UID[0] += 1
    t = stack.enter_context(nc.sbuf_tensor(f"s{_UID[0]}_{name}", list(shape), dtype))
    return Tile(t, name)


def ps(nc, stack, name, shape, dtype=F32):
    _UID[0] += 1
    t = stack.enter_context(nc.psum_tensor(f"p{_UID[0]}_{name}", list(shape), dtype))
    return Tile(t, name)


def bcast_last(ap, n):
    a = ap.ap
    return bass.AP(ap.tensor, ap.offset, [list(a[0]), list(a[1]), [0, n]])


def ffn_phase(P, nc, name, NT, src, dst, gain_d, wg, wu, wd, consts):
    TT = min(1024, NT)
    n_tt = NT // TT
    NSUB = TT // 128
    P.new_phase(name)
    with ExitStack() as st:
        hT = sb(nc, st, "hT", [128, NKC, TT], BF16)
        act = sb(nc, st, "act", [128, NF, TT], BF16)
        NWS = 3
        wgu = [sb(nc, st, f"wgu{i}", [128, 2, NKC, 128], BF16) for i in range(NWS)]
        FG = 4
        NDS = 3
        wds = [sb(nc, st, f"wds{i}", [128, FG, 512], BF16) for i in range(NDS)]
        xs = [sb(nc, st, f"xs{i}", [128, D], F32) for i in range(2)]
        xn = [sb(nc, st, f"xn{i}", [128, D], F32) for i in range(1)]
        sg = [sb(nc, st, f"sg{i}", [128, 512], F32) for i in range(2)]
        xres = [sb(nc, st, f"xres{i}", [128, 512], F32) for i in range(4)]
        ost = [sb(nc, st, f"ost{i}", [128, 512], F32) for i in range(4)]
        ssq = [sb(nc, st, f"ssq{i}", [128, 1], F32) for i in range(2)]
        rstd = [sb(nc, st, f"rstd{i}", [128, 1], F32) for i in range(2)]
        gain = sb(nc, st, "gain", [128, NKC], F32)
        pbank = [ps(nc, st, f"pb{i}", [128, 512]) for i in range(8)]
        ident = consts["ident_f32"]
        mhalf = consts["mhalf"]

        P.dma("sync", lambda e: e.dma_start(out=gain.t[:], in_=gain_d[:, :]), writes=[gain.b], sembuf=gain.b)

        wg_v = wg.rearrange("(c p) n -> p c n", p=128)
        wu_v = wu.rearrange("(c p) n -> p c n", p=128)
        wslot = 0
        dslot = 0
        xslot = 0
        rslot = 0
        for tt in range(n_tt):
            t0 = tt * TT
            for s in range(NSUB):
                r0 = t0 + s * 128
                X = xs[xslot % 2]
                SS = ssq[xslot % 2]
                RS = rstd[xslot % 2]
                XN = xn[0]
                xslot += 1
                P.dma("sync", lambda e, X=X, r0=r0: e.dma_start(out=X.t[:], in_=src[r0:r0 + 128, :]),
                      writes=[X.b], sembuf=X.b)
                P.op("scalar", lambda e, X=X, XN=XN, SS=SS: e.activation(out=XN.t[:], in_=X.t[:], func=AF.Square,
                                                                        accum_out=SS.t[:]),
                     reads=[X.b], writes=[XN.b, SS.b])
                P.op("vector", lambda e, SS=SS: e.tensor_scalar(out=SS.t[:], in0=SS.t[:], scalar1=1.0 / D, scalar2=EPS,
                                                               op0=ALU.mult, op1=ALU.add),
                     reads=[SS.b], writes=[SS.b])
                P.op("gpsimd", lambda e, SS=SS, RS=RS: e.tensor_tensor(out=RS.t[:], in0=SS.t[:], in1=mhalf.t[:], op=ALU.pow),
                     reads=[SS.b, mhalf.b], writes=[RS.b])
                P.op("vector", lambda e, X=X, XN=XN, RS=RS: e.tensor_scalar(out=XN.t[:], in0=X.t[:], scalar1=RS.t[:, 0:1],
                                                                           scalar2=None, op0=ALU.mult),
                     reads=[X.b, RS.b], writes=[XN.b])
                for cg in range(4):
                    PB = pbank[6 + (cg % 2)]
                    for ci in range(4):
                        c = cg * 4 + ci
                        P.op("tensor", lambda e, PB=PB, XN=XN, c=c, ci=ci: e.transpose(
                            out=PB.t[:, ci * 128:(ci + 1) * 128], in_=XN.t[:, c * 128:(c + 1) * 128], identity=ident.t[:]),
                            reads=[XN.b, ident.b], writes=[PB.b], same_ok=True)
                    P.op("vector", lambda e, PB=PB, cg=cg, s=s: e.tensor_tensor(
                        out=hT.t[:, cg * 4:(cg + 1) * 4, s * 128:(s + 1) * 128],
                        in0=PB.t[:, :].rearrange("p (c n) -> p c n", c=4),
                        in1=bcast_last(gain.t[:, cg * 4:(cg + 1) * 4], 128), op=ALU.mult),
                        reads=[PB.b, gain.b], writes=[hT.b])
            for f in range(NF):
                W = wgu[wslot % NWS]
                wslot += 1
                P.dma("gpsimd", lambda e, W=W, f=f: e.dma_start(out=W.t[:, 0, :, :], in_=wg_v[:, :, f * 128:(f + 1) * 128]),
                      writes=[W.b], sembuf=W.b)
                P.dma("gpsimd", lambda e, W=W, f=f: e.dma_start(out=W.t[:, 1, :, :], in_=wu_v[:, :, f * 128:(f + 1) * 128]),
                      writes=[W.b], sembuf=W.b)
                for half in range(TT // 512):
                    pset = (f * 2 + half) % 3
                    PG = pbank[2 * pset]
                    PU = pbank[2 * pset + 1]
                    for gi, PBK in ((0, PG), (1, PU)):
                        for k in range(NKC):
                            P.op("tensor", lambda e, PBK=PBK, W=W, gi=gi, k=k, half=half: e.matmul(
                                PBK.t[:, :], lhsT=W.t[:, gi, k, :], rhs=hT.t[:, k, half * 512:(half + 1) * 512],
                                start=(k == 0), stop=(k == NKC - 1)),
                                reads=[W.b, hT.b], writes=[PBK.b], same_ok=True)
                    SG = sg[(f * 2 + half) % 2]
                    P.op("scalar", lambda e, SG=SG, PG=PG: e.activation(out=SG.t[:], in_=PG.t[:, :], func=AF.Silu),
                         reads=[PG.b], writes=[SG.b])
                    P.op("vector", lambda e, SG=SG, PU=PU, f=f, half=half: e.tensor_tensor(
                        out=act.t[:, f, half * 512:(half + 1) * 512], in0=PU.t[:, :], in1=SG.t[:], op=ALU.mult),
                        reads=[SG.b, PU.b], writes=[act.b])
            for n in range(D // 512):
                f = 0
                while f < NF:
                    g = min(FG, NF - f)
                    WD = wds[dslot % NDS]
                    dslot += 1
                    P.dma("gpsimd", lambda e, WD=WD, f=f, g=g, n=n: e.dma_start(
                        out=WD.t[:, 0:g, :],
                        in_=wd[f * 128:(f + g) * 128, n * 512:(n + 1) * 512].rearrange("(g p) n -> p g n", p=128)),
                        writes=[WD.b], sembuf=WD.b)
                    for gi in range(g):
                        ff = f + gi
                        for s in range(NSUB):
                            P.op("tensor", lambda e, s=s, WD=WD, gi=gi, ff=ff: e.matmul(
                                pbank[s].t[:, :], lhsT=act.t[:, ff, s * 128:(s + 1) * 128], rhs=WD.t[:, gi, :],
                                start=(ff == 0), stop=(ff == NF - 1)),
                                reads=[act.b, WD.b], writes=[pbank[s].b], same_ok=True)
                    f += g
                for s in range(NSUB):
                    r0 = t0 + s * 128
                    XR = xres[rslot % 4]
                    OS = ost[rslot % 4]
                    rslot += 1
                    P.dma("sync", lambda e, XR=XR, r0=r0, n=n: e.dma_start(out=XR.t[:], in_=src[r0:r0 + 128, n * 512:(n + 1) * 512]),
                          writes=[XR.b], sembuf=XR.b)
                    P.op("vector", lambda e, OS=OS, XR=XR, s=s: e.scalar_tensor_tensor(
                        out=OS.t[:], in0=pbank[s].t[:, :], scalar=0.5, in1=XR.t[:], op0=ALU.mult, op1=ALU.add),
                        reads=[pbank[s].b, XR.b], writes=[OS.b])
                    P.dma("sync", lambda e, OS=OS, r0=r0, n=n: e.dma_start(out=dst[r0:r0 + 128, n * 512:(n + 1) * 512], in_=OS.t[:]),
                          reads=[OS.b], sembuf=OS.b)
        P.emit()


def norm_transpose(P, nc, src_rows, X, XN, SS, RS, hT, s, gain, consts, pbanks):
    ident = consts["ident_f32"]
    mhalf = consts["mhalf"]
    P.dma("sync", lambda e: e.dma_start(out=X.t[:], in_=src_rows), writes=[X.b], sembuf=X.b)
    P.op("scalar", lambda e: e.activation(out=XN.t[:], in_=X.t[:], func=AF.Square, accum_out=SS.t[:]),
         reads=[X.b], writes=[XN.b, SS.b])
    P.op("vector", lambda e: e.tensor_scalar(out=SS.t[:], in0=SS.t[:], scalar1=1.0 / D, scalar2=EPS,
                                             op0=ALU.mult, op1=ALU.add), reads=[SS.b], writes=[SS.b])
    P.op("gpsimd", lambda e: e.tensor_tensor(out=RS.t[:], in0=SS.t[:], in1=mhalf.t[:], op=ALU.pow),
         reads=[SS.b, mhalf.b], writes=[RS.b])
    P.op("vector", lambda e: e.tensor_scalar(out=XN.t[:], in0=X.t[:], scalar1=RS.t[:, 0:1], scalar2=None, op0=ALU.mult),
         reads=[X.b, RS.b], writes=[XN.b])
    for cg in range(4):
        PB = pbanks[cg % 2]
        for ci in range(4):
            c = cg * 4 + ci
            P.op("tensor", lambda e, PB=PB, c=c, ci=ci: e.transpose(
                out=PB.t[:, ci * 128:(ci + 1) * 128], in_=XN.t[:, c * 128:(c + 1) * 128], identity=ident.t[:]),
                reads=[XN.b, ident.b], writes=[PB.b], same_ok=True)
        P.op("vector", lambda e, PB=PB, cg=cg: e.tensor_tensor(
            out=hT.t[:, cg * 4:(cg + 1) * 4, s * 128:(s + 1) * 128],
            in0=PB.t[:, :].rearrange("p (c n) -> p c n", c=4),
            in1=bcast_last(gain.t[:, cg * 4:(cg + 1) * 4], 128), op=ALU.mult),
            reads=[PB.b, gain.b], writes=[hT.b])


def proj_phase(P, nc, NT, NPREV, src, gain_d, w_in, sc, consts):
    TT = min(1024, NT)
    n_tt = NT // TT
    NSUB = TT // 128
    P.new_phase("proj")
    qscale = 1.0 / float(np.sqrt(128.0))
    with ExitStack() as st:
        hT = sb(nc, st, "hT", [128, NKC, TT], BF16)
        wf = [sb(nc, st, f"wf{i}", [128, NKC, 128], BF16) for i in range(3)]
        wt = [sb(nc, st, f"wt{i}", [128, NKC, 512], BF16) for i in range(2)]
        xs = [sb(nc, st, f"xs{i}", [128, D], F32) for i in range(2)]
        xn = [sb(nc, st, f"xn{i}", [128, D], F32) for i in range(1)]
        ssq = [sb(nc, st, f"ssq{i}", [128, 1], F32) for i in range(2)]
        rstd = [sb(nc, st, f"rstd{i}", [128, 1], F32) for i in range(2)]
        gain = sb(nc, st, "gain", [128, NKC], F32)
        obf = [sb(nc, st, f"obf{i}", [128, 512], BF16) for i in range(4)]
        of32 = [sb(nc, st, f"of32{i}", [128, 512], F32) for i in range(4)]
        pbank = [ps(nc, st, f"pb{i}", [128, 512]) for i in range(8)]
        P.dma("sync", lambda e: e.dma_start(out=gain.t[:], in_=gain_d[:, :]), writes=[gain.b], sembuf=gain.b)
        w_v = w_in.rearrange("(c p) n -> p c n", p=128)
        xslot = 0
        fslot = 0
        tslot = 0
        oslot = 0
        pslot = 0
        tm_blocks = []
        for j in range(2):
            tm_blocks.append((2048 + 512 * j, 512, (lambda r0, j=j: sc["v"][NPREV + r0:NPREV + r0 + 128, 512 * j:512 * (j + 1)]), BF16))
        for j in range(6):
            tm_blocks.append((3072 + 512 * j, 512, (lambda r0, j=j: sc["graw"][3 + r0:3 + r0 + 128, 512 * j:512 * (j + 1)]), F32))
        tm_blocks.append((6144, 16, (lambda r0: sc["ab"][r0:r0 + 128, :]), F32))
        for j in range(2):
            tm_blocks.append((6160 + 512 * j, 512, (lambda r0, j=j: sc["gate"][r0:r0 + 128, 512 * j:512 * (j + 1)]), F32))
        for tt in range(n_tt):
            t0 = tt * TT
            for s in range(NSUB):
                r0 = t0 + s * 128
                i = xslot % 2
                xslot += 1
                norm_transpose(P, nc, src[r0:r0 + 128, :], xs[i], xn[0], ssq[i], rstd[i], hT, s, gain, consts, pbank[6:8])
            for c in range(16):
                W = wf[fslot % 3]
                fslot += 1
                P.dma("gpsimd", lambda e, W=W, c=c: e.dma_start(out=W.t[:], in_=w_v[:, :, c * 128:(c + 1) * 128]),
                      writes=[W.b], sembuf=W.b)
                for half in range(TT // 512):
                    PB = pbank[pslot % 6]
                    pslot += 1
                    for k in range(NKC):
                        P.op("tensor", lambda e, PB=PB, W=W, k=k, half=half: e.matmul(
                            PB.t[:, :], lhsT=W.t[:, k, :], rhs=hT.t[:, k, half * 512:(half + 1) * 512],
                            start=(k == 0), stop=(k == NKC - 1)), reads=[W.b, hT.b], writes=[PB.b], same_ok=True)
                    O = obf[oslot % 4]
                    oslot += 1
                    if c < 8:
                        P.op("scalar", lambda e, O=O, PB=PB: e.activation(out=O.t[:], in_=PB.t[:, :], func=AF.Copy, scale=qscale),
                             reads=[PB.b], writes=[O.b])
                        dstap = sc["qT"][c, :, t0 + half * 512:t0 + (half + 1) * 512]
                    else:
                        P.op("vector", lambda e, O=O, PB=PB: e.tensor_copy(out=O.t[:], in_=PB.t[:, :]),
                             reads=[PB.b], writes=[O.b])
                        dstap = sc["kT"][c - 8, :, NPREV + t0 + half * 512:NPREV + t0 + (half + 1) * 512]
                    P.dma("sync", lambda e, O=O, dstap=dstap: e.dma_start(out=dstap, in_=O.t[:]), reads=[O.b], sembuf=O.b)
            for bi, (c0, ncol, dfn, odt) in enumerate(tm_blocks):
                W = wt[tslot % 2]
                tslot += 1
                P.dma("gpsimd", lambda e, W=W, c0=c0, ncol=ncol: e.dma_start(out=W.t[:, :, 0:ncol], in_=w_v[:, :, c0:c0 + ncol]),
                      writes=[W.b], sembuf=W.b)
                for s in range(NSUB):
                    r0 = t0 + s * 128
                    PB = pbank[pslot % 6]
                    pslot += 1
                    for k in range(NKC):
                        P.op("tensor", lambda e, PB=PB, W=W, k=k, s=s, ncol=ncol: e.matmul(
                            PB.t[:, 0:ncol], lhsT=hT.t[:, k, s * 128:(s + 1) * 128], rhs=W.t[:, k, 0:ncol],
                            start=(k == 0), stop=(k == NKC - 1)), reads=[W.b, hT.b], writes=[PB.b], same_ok=True)
                    O = (obf if odt == BF16 else of32)[oslot % 4]
                    oslot += 1
                    eng = "scalar" if (s % 2 == 0) else "vector"
                    if eng == "scalar":
                        P.op("scalar", lambda e, O=O, PB=PB, ncol=ncol: e.activation(out=O.t[:, 0:ncol], in_=PB.t[:, 0:ncol], func=AF.Copy),
                             reads=[PB.b], writes=[O.b])
                    else:
                        P.op("vector", lambda e, O=O, PB=PB, ncol=ncol: e.tensor_copy(out=O.t[:, 0:ncol], in_=PB.t[:, 0:ncol]),
                             reads=[PB.b], writes=[O.b])
                    P.dma("sync", lambda e, O=O, dstap=dfn(r0), ncol=ncol: e.dma_start(out=dstap, in_=O.t[:, 0:ncol]),
                          reads=[O.b], sembuf=O.b)
        P.emit()


def attn_phase(P, nc, NT, NPREV, sc, cin, consts, xg=None):
    NK = NPREV + NT
    NKB = NK // 128
    NPB = NPREV // 128
    NG = NT // 512
    P.new_phase("attn")
    with ExitStack() as st:
        NHB = 4
        KT = [sb(nc, st, f"KT{i}", [128, NK], BF16) for i in range(NHB)]
        VV = [sb(nc, st, f"VV{i}", [128, NKB, 128], BF16) for i in range(NHB)]
        QT = [sb(nc, st, f"QT{i}", [128, NT], BF16) for i in range(NHB)]
        trineg = sb(nc, st, "trineg", [128, 128], BF16)
        tricomp = sb(nc, st, "tricomp", [128, 128], BF16)
        ones32 = sb(nc, st, "ones32", [128, 128], F32)
        masks = sb(nc, st, "masks", [128, 4, 512], F32)
        negb = sb(nc, st, "negb", [128, 1], F32)
        gsb = sb(nc, st, "gsb", [128, 1], F32)
        NS = 2
        E = [[sb(nc, st, f"E{s_}_{i}", [128, 512], F32) for i in range(3)] for s_ in range(NS)]
        SP = [[sb(nc, st, f"SP{s_}_{i}", [128, 512], BF16) for i in range(2)] for s_ in range(NS)]
        EC = [[sb(nc, st, f"EC{s_}_{i}", [128, 512], F32) for i in range(2)] for s_ in range(NS)]
        W = [[sb(nc, st, f"W{s_}_{i}", [128, 512], BF16) for i in range(3)] for s_ in range(NS)]
        SQ = [sb(nc, st, f"SQ{s_}", [128, 512], F32) for s_ in range(NS)]
        R = [sb(nc, st, f"R{s_}", [128, 512], F32) for s_ in range(NS)]
        Y = [[sb(nc, st, f"Y{s_}_{i}", [128, 512], BF16) for i in range(2)] for s_ in range(NS)]
        Zp = [[ps(nc, st, f"Zp{s_}_{i}", [128, 512]) for i in range(2)] for s_ in range(NS)]
        Cp = [ps(nc, st, f"Cp{s_}", [128, 512]) for s_ in range(NS)]
        OT = [ps(nc, st, f"OT{s_}", [128, 512]) for s_ in range(NS)]

        P.dma("gpsimd", lambda e: e.dma_start(out=trineg.t[:], in_=cin["trineg"][:, :]), writes=[trineg.b], sembuf=trineg.b)
        P.dma("gpsimd", lambda e: e.dma_start(out=tricomp.t[:], in_=cin["tricomp"][:, :]), writes=[tricomp.b], sembuf=tricomp.b)
        P.dma("sync", lambda e: e.dma_start(out=ones32.t[:], in_=cin["ones32"][:, :]), writes=[ones32.b], sembuf=ones32.b)
        P.dma("sync", lambda e: e.dma_start(out=masks.t[:], in_=cin["dmask"].rearrange("r p n -> p r n")), writes=[masks.b], sembuf=masks.b)
        P.dma("sync", lambda e: e.dma_start(out=negb.t[:], in_=cin["negbias"][:, :]), writes=[negb.b], sembuf=negb.b)
        P.dma("sync", lambda e: e.dma_start(out=gsb.t[:], in_=cin["sb_out_norm"][:, :]), writes=[gsb.b], sembuf=gsb.b)

        def load_head(h):
            K_, V_, Q_ = KT[h % NHB], VV[h % NHB], QT[h % NHB]
            if xg is None:
                P.dma("sync", lambda e: e.dma_start(out=K_.t[:], in_=sc["kT"][h, :, :]), writes=[K_.b], sembuf=K_.b)
                P.dma("sync", lambda e: e.dma_start(
                    out=V_.t[:], in_=sc["v"][:, h * 128:(h + 1) * 128].rearrange("(b s) d -> s b d", s=128)),
                    writes=[V_.b], sembuf=V_.b)
            else:
                P.dma("sync", lambda e: e.dma_start(out=K_.t[:, 0:NPREV], in_=xg["kd"][h // 4][(h % 4) * 128:(h % 4 + 1) * 128, :]), writes=[K_.b], sembuf=K_.b)
                P.dma("sync", lambda e: e.dma_start(out=K_.t[:, NPREV:NK], in_=sc["kT"][h, :, :]), writes=[K_.b], sembuf=K_.b)
                HVB = xg["HV"] // 128
                for i in range(2):
                    P.dma("sync", lambda e, i=i: e.dma_start(
                        out=V_.t[:, i * HVB:(i + 1) * HVB, :], in_=xg["vd"][i][0:xg["HV"], h * 128:(h + 1) * 128].rearrange("(b s) d -> s b d", s=128)),
                        writes=[V_.b], sembuf=V_.b)
                P.dma("sync", lambda e: e.dma_start(
                    out=V_.t[:, NPB:NKB, :], in_=sc["v"][:, h * 128:(h + 1) * 128].rearrange("(b s) d -> s b d", s=128)),
                    writes=[V_.b], sembuf=V_.b)
            P.dma("sync", lambda e: e.dma_start(out=Q_.t[:], in_=sc["qT"][h, :, :]), writes=[Q_.b], sembuf=Q_.b)

        class Stream:
            pass

        def mk_stream(sid, h, G):
            S_ = Stream()
            S_.sid, S_.h, S_.G = sid, h, G
            S_.K, S_.V, S_.Q = KT[h % NHB], VV[h % NHB], QT[h % NHB]
            S_.g0 = G * 512
            steps = []
            for r in (3, 2, 1, 0):
                steps.append((NPB + G * 4 + r, r, False))
            for m in range(G * 4 - 1, -1, -1):
                steps.append((NPB + m, None, False))
            for m in range(NPB - 1, -1, -1):
                steps.append((m, None, True))
            S_.steps = steps
            S_.ns = len(steps)
            S_.bufs = {}
            S_.cz = S_.ce = S_.csp = S_.cec = S_.cw = 0
            return S_

        def S1z(S_, i):
            kb, r, isprev = S_.steps[i]
            sid = S_.sid
            Z = Zp[sid][S_.cz % 2]; S_.cz += 1
            K_, Q_, g0 = S_.K, S_.Q, S_.g0
            P.op("tensor", lambda e: e.matmul(Z.t[:, :], lhsT=K_.t[:, kb * 128:(kb + 1) * 128], rhs=Q_.t[:, g0:g0 + 512],
                                              start=True, stop=True), reads=[K_.b, Q_.b], writes=[Z.b], same_ok=True)
            S_.bufs[i] = [None, None, None, Z]

        def S1e(S_, i):
            kb, r, isprev = S_.steps[i]
            sid = S_.sid
            Z = S_.bufs[i][3]
            Ei = E[sid][S_.ce % 3]; S_.ce += 1
            if isprev:
                P.op("scalar", lambda e: e.activation(out=Ei.t[:], in_=Z.t[:, :], func=AF.Exp, bias=negb.t[:, 0:1]),
                     reads=[Z.b, negb.b], writes=[Ei.b])
            else:
                P.op("scalar", lambda e: e.activation(out=Ei.t[:], in_=Z.t[:, :], func=AF.Exp), reads=[Z.b], writes=[Ei.b])
            if r is not None:
                P.op("vector", lambda e: e.tensor_tensor(out=Ei.t[:], in0=Ei.t[:], in1=masks.t[:, r, :], op=ALU.mult),
                     reads=[Ei.b, masks.b], writes=[Ei.b])
            S_.bufs[i][0] = Ei

        def S1sp(S_, i):
            sid = S_.sid
            Ei = S_.bufs[i][0]
            SPi = SP[sid][S_.csp % 2]; S_.csp += 1
            P.op("scalar", lambda e: e.activation(out=SPi.t[:], in_=Ei.t[:], func=AF.Ln, bias=1.0),
                 reads=[Ei.b], writes=[SPi.b])
            S_.bufs[i][1] = SPi

        def S2a(S_, i):
            Ei, SPi = S_.bufs[i][0], S_.bufs[i][1]
            sid = S_.sid
            C = Cp[sid]
            ECi = EC[sid][S_.cec % 2]; S_.cec += 1
            Wi = W[sid][S_.cw % 3]; S_.cw += 1
            last = (i == S_.ns - 1)
            P.op("tensor", lambda e: e.matmul(C.t[:, :], lhsT=trineg.t[:], rhs=SPi.t[:], start=(i == 0), stop=last),
                 reads=[trineg.b, SPi.b], writes=[C.b], same_ok=True)
            P.op("scalar", lambda e: e.activation(out=ECi.t[:], in_=C.t[:, :], func=AF.Exp), reads=[C.b], writes=[ECi.b])
            P.op("vector", lambda e: e.tensor_tensor(out=Wi.t[:], in0=Ei.t[:], in1=ECi.t[:], op=ALU.mult),
                 reads=[Ei.b, ECi.b], writes=[Wi.b])
            S_.bufs[i][2] = Wi

        def S2b(S_, i):
            if i == S_.ns - 1:
                return
            Ei, SPi = S_.bufs[i][0], S_.bufs[i][1]
            C = Cp[S_.sid]
            P.op("tensor", lambda e: e.matmul(C.t[:, :], lhsT=tricomp.t[:], rhs=SPi.t[:], start=False, stop=False),
                 reads=[tricomp.b, SPi.b], writes=[C.b], same_ok=True)

        def S3(S_, i):
            kb, r, isprev = S_.steps[i]
            Wi = S_.bufs[i][2]
            OTg, V_, ns = OT[S_.sid], S_.V, S_.ns
            P.op("tensor", lambda e: e.matmul(OTg.t[:, :], lhsT=V_.t[:, kb, :], rhs=Wi.t[:], start=(i == 0), stop=(i == ns - 1)),
                 reads=[V_.b, Wi.b], writes=[OTg.b], same_ok=True)
            del S_.bufs[i]

        def finish(S_):
            sid, h, g0, G = S_.sid, S_.h, S_.g0, S_.G
            OTg = OT[sid]
            Yg = Y[sid][G % 2]
            Zs = Zp[sid][S_.cz % 2]; S_.cz += 1
            SQ_, R_ = SQ[sid], R[sid]
            P.op("scalar", lambda e: e.activation(out=SQ_.t[:], in_=OTg.t[:, :], func=AF.Square), reads=[OTg.b], writes=[SQ_.b])
            P.op("tensor", lambda e: e.matmul(Zs.t[:, :], lhsT=ones32.t[:], rhs=SQ_.t[:], start=True, stop=True),
                 reads=[ones32.b, SQ_.b], writes=[Zs.b], same_ok=True)
            P.op("vector", lambda e: e.tensor_scalar(out=R_.t[:], in0=Zs.t[:, :], scalar1=1.0 / 128.0, scalar2=EPS,
                                                     op0=ALU.mult, op1=ALU.add), reads=[Zs.b], writes=[R_.b])
            P.op("scalar", lambda e: e.activation(out=R_.t[:], in_=R_.t[:], func=AF.Ln), reads=[R_.b], writes=[R_.b])
            P.op("scalar", lambda e: e.activation(out=R_.t[:], in_=R_.t[:], func=AF.Exp, scale=-0.5), reads=[R_.b], writes=[R_.b])
            P.op("vector", lambda e: e.scalar_tensor_tensor(
                out=Yg.t[:], in0=OTg.t[:, :], scalar=gsb.t[:, 0:1], in1=R_.t[:], op0=ALU.mult, op1=ALU.mult),
                reads=[OTg.b, gsb.b, R_.b], writes=[Yg.b])
            P.dma("sync", lambda e: e.dma_start(out=sc["yT"][h, :, g0:g0 + 512], in_=Yg.t[:]), reads=[Yg.b], sembuf=Yg.b)

        load_head(0)
        load_head(1)
        for hp in range(4):
            if hp + 1 < 4:
                load_head(2 * hp + 2)
                load_head(2 * hp + 3)
            for G in range(NG):
                strs = [mk_stream(0, 2 * hp, G), mk_stream(1, 2 * hp + 1, G)]
                ns = strs[0].ns
                for S_ in strs:
                    S1z(S_, 0)
                if ns > 1:
                    for S_ in strs:
                        S1z(S_, 1)
                for S_ in strs:
                    S1e(S_, 0)
                for i in range(ns):
                    if i + 2 < ns:
                        for S_ in strs:
                            S1z(S_, i + 2)
                    if i >= 1:
                        for S_ in strs:
                            S3(S_, i - 1)
                    for S_ in strs:
                        S1sp(S_, i)
                    for S_ in strs:
                        S2a(S_, i)
                    for S_ in strs:
                        S2b(S_, i)
                    if i + 1 < ns:
                        for S_ in strs:
                            S1e(S_, i + 1)
                for S_ in strs:
                    S3(S_, ns - 1)
                for S_ in strs:
                    finish(S_)
        P.emit()


FP32R = False


def mm32(e, out, lhsT, rhs, **kw):
    if FP32R and lhsT.dtype == F32 and rhs.dtype == F32:
        lhsT = lhsT.bitcast(mybir.dt.float32r)
        rhs = rhs.bitcast(mybir.dt.float32r)
    return e.matmul(out, lhsT=lhsT, rhs=rhs, **kw)


class PQ:
    def __init__(self, bank, q, buf):
        self.bank = bank
        self.q = q
        self.b = buf

    def ap(self, rows=128, cols=128):
        return self.bank[0:rows, self.q * 128:self.q * 128 + cols]


def gdn_prep_phase(P, nc, NT, sc, cin, consts, egl_all, xg=None):
    NTILE = NT // 128
    P.new_phase("gprep")
    ident = consts["ident_f32"]
    with ExitStack() as st:
        convw = sb(nc, st, "convw", [128, 4, 3072], F32)
        alog = sb(nc, st, "alog", [128, 8], F32)
        dtb = sb(nc, st, "dtb", [128, 8], F32)
        nega = sb(nc, st, "nega", [128, 8], F32)
        BT = sb(nc, st, "BT", [128, 128], F32)
        BL = sb(nc, st, "BL", [128, 128], F32)
        selC = sb(nc, st, "selC", [128, 2, 128], F32)
        zero3 = sb(nc, st, "zero3", [3, 3072], F32)
        XJ = [[sb(nc, st, f"XJ{i}_{j}", [128, 1024], F32) for j in range(4)] for i in range(2)]
        acc = sb(nc, st, "acc", [128, 1024], F32)
        tmp = sb(nc, st, "tmp", [128, 1024], F32)
        tmp2 = sb(nc, st, "tmp2", [128, 1024], F32)
        acc2 = sb(nc, st, "acc2", [128, 1024], F32)
        qn2 = [sb(nc, st, f"qn{i}", [128, 8, 128], F32) for i in range(2)]
        kn2 = [sb(nc, st, f"kn{i}", [128, 8, 128], F32) for i in range(2)]
        vs2 = [sb(nc, st, f"vs{i}", [128, 8, 128], F32) for i in range(2)]
        qg = sb(nc, st, "qg", [128, 8, 128], F32)
        kbg = sb(nc, st, "kbg", [128, 8, 128], BF16)
        vb = sb(nc, st, "vb", [128, 8, 128], BF16)
        kT = sb(nc, st, "kT", [128, 8, 128], F32)
        qT = sb(nc, st, "qT", [128, 8, 128], F32)
        qgT = sb(nc, st, "qgT", [128, 8, 128], BF16)
        A_all = sb(nc, st, "A_all", [128, 8, 128], F32)
        aqk_all = sb(nc, st, "aqk_all", [64, 2, 8, 64], BF16)
        kend_all = sb(nc, st, "kend_all", [64, 2, 8, 128], BF16)
        DmS = [sb(nc, st, f"DmS{i}", [128, 128], F32) for i in range(3)]
        DmT = [sb(nc, st, f"DmT{i}", [64, 64], F32) for i in range(3)]
        Gbb = sb(nc, st, "Gbb", [128, 8, 128], F32)
        MS01 = sb(nc, st, "MS01", [128, 128], F32)
        ML01 = sb(nc, st, "ML01", [64, 64], F32)
        gcrow_buf = Buf("gcrow")
        abt = sb(nc, st, "abt", [128, 16], F32)
        sm = {n: sb(nc, st, n, [128, 8], F32) for n in ("g", "beta", "gc", "eg", "ekend", "bke", "ssq", "rq", "rk", "t8")}
        smC = {n: sb(nc, st, n, [64, 2, 8], F32) for n in ("ngcC", "ekendC", "tC")}
        gcT = sb(nc, st, "gcT", [8, 128], F32)
        mh8 = sb(nc, st, "mh8", [128, 8], F32)
        banks = [st.enter_context(nc.psum_tensor(f"gp_bank{i}", [128, 512], F32)) for i in range(8)]
        bankbufs = [Buf(f"bank{i}") for i in range(8)]
        pq = [PQ(banks[i % 8], i // 8, bankbufs[i % 8]) for i in range(32)]
        pqi = [0]

        def nextpq():
            p = pq[pqi[0] % 32]
            pqi[0] += 1
            return p

        ld = lambda t, src, eng="sync": P.dma(eng, lambda e: e.dma_start(out=t.t[:], in_=src), writes=[t.b], sembuf=t.b)
        ld(convw, cin["convw_b"].rearrange("p (j c) -> p j c", j=4))
        ld(alog, cin["alog_b"][:, :])
        ld(dtb, cin["dtb_b"][:, :])
        ld(BT, cin["BT"][:, :])
        ld(BL, cin["BL"][:, :])
        ld(selC, cin["selC"].rearrange("p (c n) -> p c n", c=2))
        ld(MS01, cin["MS01"][:, :])
        ld(ML01, cin["ML01"][:, :])
        if xg is None:
            P.op("vector", lambda e: e.memset(zero3.t[:], 0.0), writes=[zero3.b])
        P.op("vector", lambda e: e.memset(mh8.t[:], -0.5), writes=[mh8.b])
        hist = Buf("hist")
        if xg is None:
            P.dma("sync", lambda e: e.dma_start(out=sc["graw"][0:3, :], in_=zero3.t[:]), reads=[zero3.b], writes=[hist], sembuf=zero3.b)
        else:
            flag3 = sb(nc, st, "flag3", [3, 1], F32)
            P.dma("sync", lambda e: e.dma_start(out=flag3.t[:], in_=cin["flag01"][0:3, :]), writes=[flag3.b], sembuf=flag3.b)
            P.dma("sync", lambda e: e.dma_start(out=zero3.t[:], in_=xg["hist_dst"][0:3, :]), writes=[zero3.b], sembuf=zero3.b)
            P.op("vector", lambda e: e.tensor_scalar(out=zero3.t[:], in0=zero3.t[:], scalar1=flag3.t[:, 0:1], scalar2=None, op0=ALU.mult),
                 reads=[zero3.b, flag3.b], writes=[zero3.b])
            P.dma("sync", lambda e: e.dma_start(out=sc["graw"][0:3, :], in_=zero3.t[:]), reads=[zero3.b], writes=[hist], sembuf=zero3.b)
        P.op("scalar", lambda e: e.activation(out=nega.t[:], in_=alog.t[:], func=AF.Exp), reads=[alog.b], writes=[nega.b])
        P.op("vector", lambda e: e.tensor_scalar(out=nega.t[:], in0=nega.t[:], scalar1=-1.0, scalar2=None, op0=ALU.mult),
             reads=[nega.b], writes=[nega.b])
        qs = 1.0 / float(np.sqrt(128.0))
        P.emit_barrier_needed = True
        g, beta, gc, eg, ekend, bke, t8 = (sm[n] for n in ("g", "beta", "gc", "eg", "ekend", "bke", "t8"))
        ngcC, ekendC, tC = smC["ngcC"], smC["ekendC"], smC["tC"]

        def s1(tb, third, dstt):
            r0 = tb * 128
            X = XJ[(tb * 3 + third) % 2]
            c0 = third * 1024
            for j in range(4):
                P.dma("sync", lambda e, X=X, j=j, r0=r0, c0=c0: e.dma_start(out=X[j].t[:], in_=sc["graw"][r0 + j:r0 + j + 128, c0:c0 + 1024]),
                      reads=([hist] if tb == 0 else []), writes=[X[j].b], sembuf=X[j].b)
            P.op("gpsimd", lambda e, X=X, c0=c0: e.tensor_tensor(out=tmp.t[:], in0=X[1].t[:], in1=convw.t[:, 1, c0:c0 + 1024], op=ALU.mult),
                 reads=[X[1].b, convw.b], writes=[tmp.b])
            P.op("gpsimd", lambda e, X=X, c0=c0: e.tensor_tensor(out=tmp2.t[:], in0=X[2].t[:], in1=convw.t[:, 2, c0:c0 + 1024], op=ALU.mult),
                 reads=[X[2].b, convw.b], writes=[tmp2.b])
            P.op("vector", lambda e, X=X, c0=c0: e.tensor_tensor(out=acc.t[:], in0=X[0].t[:], in1=convw.t[:, 0, c0:c0 + 1024], op=ALU.mult),
                 reads=[X[0].b, convw.b], writes=[acc.b])
            P.op("vector", lambda e, X=X, c0=c0: e.tensor_tensor(out=acc2.t[:], in0=X[3].t[:], in1=convw.t[:, 3, c0:c0 + 1024], op=ALU.mult),
                 reads=[X[3].b, convw.b], writes=[acc2.b])
            P.op("vector", lambda e: e.tensor_tensor(out=acc.t[:], in0=acc.t[:], in1=acc2.t[:], op=ALU.add), reads=[acc.b, acc2.b], writes=[acc.b])
            P.op("vector", lambda e: e.tensor_tensor(out=acc.t[:], in0=acc.t[:], in1=tmp.t[:], op=ALU.add), reads=[acc.b, tmp.b], writes=[acc.b])
            P.op("vector", lambda e: e.tensor_tensor(out=acc.t[:], in0=acc.t[:], in1=tmp2.t[:], op=ALU.add), reads=[acc.b, tmp2.b], writes=[acc.b])
            dflat = dstt.t[:].rearrange("p h d -> p (h d)")
            P.op("scalar", lambda e, dflat=dflat: e.activation(out=dflat, in_=acc.t[:], func=AF.Silu), reads=[acc.b], writes=[dstt.b])
            if third < 2:
                rr = sm["rq"] if third == 0 else sm["rk"]
                P.op("gpsimd", lambda e, dflat=dflat: e.tensor_tensor(out=tmp.t[:], in0=dflat, in1=dflat, op=ALU.mult),
                     reads=[dstt.b], writes=[tmp.b])
                P.op("vector", lambda e: e.tensor_reduce(out=sm["ssq"].t[:], in_=tmp.t[:].rearrange("p (h d) -> p h d", h=8),
                                                         axis=AX.X, op=ALU.add), reads=[tmp.b], writes=[sm["ssq"].b])
                P.op("vector", lambda e: e.tensor_scalar(out=sm["ssq"].t[:], in0=sm["ssq"].t[:], scalar1=EPS, scalar2=None, op0=ALU.add),
                     reads=[sm["ssq"].b], writes=[sm["ssq"].b])
                P.op("gpsimd", lambda e, rr=rr: e.tensor_tensor(out=rr.t[:], in0=sm["ssq"].t[:], in1=mh8.t[:], op=ALU.pow),
                     reads=[sm["ssq"].b, mh8.b], writes=[rr.b])
                if third == 0:
                    P.op("vector", lambda e, rr=rr: e.tensor_scalar(out=rr.t[:], in0=rr.t[:], scalar1=qs, scalar2=None, op0=ALU.mult),
                         reads=[rr.b], writes=[rr.b])
                P.op("vector", lambda e, dstt=dstt, rr=rr: e.tensor_tensor(out=dstt.t[:], in0=dstt.t[:], in1=bcast_last(rr.t[:, :], 128), op=ALU.mult),
                     reads=[dstt.b, rr.b], writes=[dstt.b])

        def s2a(tb, qn, kn, vs):
            r0 = tb * 128
            P.dma("sync", lambda e, r0=r0: e.dma_start(out=abt.t[:], in_=sc["ab"][r0:r0 + 128, :]), writes=[abt.b], sembuf=abt.b)
            g, beta, gc, eg, ekend, bke, t8 = (sm[n] for n in ("g", "beta", "gc", "eg", "ekend", "bke", "t8"))
            P.op("vector", lambda e: e.tensor_tensor(out=t8.t[:], in0=abt.t[:, 0:8], in1=dtb.t[:], op=ALU.add), reads=[abt.b, dtb.b], writes=[t8.b])
            P.op("scalar", lambda e: e.activation(out=t8.t[:], in_=t8.t[:], func=AF.Exp), reads=[t8.b], writes=[t8.b])
            P.op("scalar", lambda e: e.activation(out=t8.t[:], in_=t8.t[:], func=AF.Ln, bias=1.0), reads=[t8.b], writes=[t8.b])
            P.op("vector", lambda e: e.tensor_tensor(out=g.t[:], in0=t8.t[:], in1=nega.t[:], op=ALU.mult), reads=[t8.b, nega.b], writes=[g.b])
            P.op("scalar", lambda e: e.activation(out=beta.t[:], in_=abt.t[:, 8:16], func=AF.Exp, scale=-1.0), reads=[abt.b], writes=[beta.b])
            P.op("vector", lambda e: e.tensor_scalar(out=beta.t[:], in0=beta.t[:], scalar1=1.0, scalar2=None, op0=ALU.add), reads=[beta.b], writes=[beta.b])
            P.op("vector", lambda e: e.reciprocal(out=beta.t[:], in_=beta.t[:]), reads=[beta.b], writes=[beta.b])
            p_gc, p_gl, p_gcT = nextpq(), nextpq(), nextpq()
            P.op("tensor", lambda e, p=p_gc: mm32(e, p.ap(128, 8), lhsT=BT.t[:], rhs=g.t[:], start=True, stop=True), reads=[BT.b, g.b], writes=[p_gc.b], same_ok=True)
            P.op("tensor", lambda e, p=p_gl: mm32(e, p.ap(128, 8), lhsT=BL.t[:], rhs=g.t[:], start=True, stop=True), reads=[BL.b, g.b], writes=[p_gl.b], same_ok=True)
            P.op("tensor", lambda e, p=p_gcT: mm32(e, p.ap(8, 128), lhsT=g.t[:], rhs=BT.t[:], start=True, stop=True), reads=[BT.b, g.b], writes=[p_gcT.b], same_ok=True)
            P.op("vector", lambda e, p=p_gc: e.tensor_copy(out=gc.t[:], in_=p.ap(128, 8)), reads=[p_gc.b], writes=[gc.b])
            P.op("scalar", lambda e, p=p_gc: e.activation(out=eg.t[:], in_=p.ap(128, 8), func=AF.Exp), reads=[p_gc.b], writes=[eg.b])
            P.op("vector", lambda e, p=p_gl: e.tensor_tensor(out=ekend.t[:], in0=p.ap(128, 8), in1=gc.t[:], op=ALU.subtract), reads=[p_gl.b, gc.b], writes=[ekend.b])
            P.op("scalar", lambda e: e.activation(out=ekend.t[:], in_=ekend.t[:], func=AF.Exp), reads=[ekend.b], writes=[ekend.b])
            P.op("vector", lambda e, p=p_gcT: e.tensor_copy(out=gcT.t[:], in_=p.ap(8, 128)), reads=[p_gcT.b], writes=[gcT.b])
            P.dma("sync", lambda e, tb=tb: e.dma_start(out=sc["gcrow"][tb], in_=gcT.t[:]), reads=[gcT.b], writes=[gcrow_buf], sembuf=gcrow_buf)

            def ldb(e, tb=tb):
                src = sc["gcrow"][tb]
                bsrc = bass.AP(src.tensor, src.offset, [[0, 128], [128, 8], [1, 128]])
                return e.dma_start(out=Gbb.t[:], in_=bsrc)
            P.dma("sync", ldb, reads=[gcrow_buf], writes=[Gbb.b], sembuf=Gbb.b)
            P.op("vector", lambda e: e.tensor_tensor(out=bke.t[:], in0=beta.t[:], in1=eg.t[:], op=ALU.mult), reads=[beta.b, eg.b], writes=[bke.b])
            ngcC, ekendC, tC = smC["ngcC"], smC["ekendC"], smC["tC"]
            for c in range(2):
                pc1, pc2, pc3 = nextpq(), nextpq(), nextpq()
                P.op("tensor", lambda e, p=pc1, c=c: mm32(e, p.ap(64, 8), lhsT=BT.t[:, c * 64:(c + 1) * 64], rhs=g.t[:], start=True, stop=True),
                     reads=[BT.b, g.b], writes=[pc1.b], same_ok=True)
                P.op("tensor", lambda e, p=pc2, c=c: mm32(e, p.ap(64, 8), lhsT=BL.t[:, c * 64:(c + 1) * 64], rhs=g.t[:], start=True, stop=True),
                     reads=[BL.b, g.b], writes=[pc2.b], same_ok=True)
                P.op("tensor", lambda e, p=pc3, c=c: mm32(e, p.ap(128, 8), lhsT=selC.t[:, c, :], rhs=g.t[:], start=True, stop=True),
                     reads=[selC.b, g.b], writes=[pc3.b], same_ok=True)
                P.op("vector", lambda e, p=pc1, c=c: e.tensor_scalar(out=ngcC.t[:, c, :], in0=p.ap(64, 8), scalar1=-1.0, scalar2=None, op0=ALU.mult),
                     reads=[pc1.b], writes=[ngcC.b])
                P.op("vector", lambda e, p=pc2, c=c: e.tensor_tensor(out=tC.t[:, c, :], in0=p.ap(64, 8), in1=ngcC.t[:, c, :], op=ALU.add),
                     reads=[pc2.b, ngcC.b], writes=[tC.b])
                P.op("scalar", lambda e, c=c: e.activation(out=ekendC.t[:, c, :], in_=tC.t[:, c, :], func=AF.Exp), reads=[tC.b], writes=[ekendC.b])
                P.op("scalar", lambda e, p=pc3, c=c, tb=tb: e.activation(out=egl_all.t[:, tb * 2 + c, :], in_=p.ap(128, 8), func=AF.Exp),
                     reads=[pc3.b], writes=[egl_all.b])
            P.op("vector", lambda e: e.tensor_tensor(out=kbg.t[:], in0=kn.t[:], in1=bcast_last(bke.t[:, :], 128), op=ALU.mult), reads=[kn.b, bke.b], writes=[kbg.b])
            P.op("gpsimd", lambda e: e.tensor_tensor(out=vb.t[:], in0=vs.t[:], in1=bcast_last(beta.t[:, :], 128), op=ALU.mult), reads=[vs.b, beta.b], writes=[vb.b])
            P.op("vector", lambda e: e.tensor_tensor(out=qg.t[:], in0=qn.t[:], in1=bcast_last(eg.t[:, :], 128), op=ALU.mult), reads=[qn.b, eg.b], writes=[qg.b])
            P.dma("sync", lambda e, r0=r0: e.dma_start(out=sc["kbg"][r0:r0 + 128, :], in_=kbg.t[:].rearrange("p h d -> p (h d)")), reads=[kbg.b], sembuf=kbg.b)
            P.dma("sync", lambda e, r0=r0: e.dma_start(out=sc["vb"][r0:r0 + 128, :], in_=vb.t[:].rearrange("p h d -> p (h d)")), reads=[vb.b], sembuf=vb.b)
            for srct, dstT in ((kn, kT), (qn, qT), (qg, qgT)):
                for h in range(8):
                    p = nextpq()
                    P.op("tensor", lambda e, p=p, srct=srct, h=h: e.transpose(out=p.ap(), in_=srct.t[:, h, :], identity=ident.t[:]),
                         reads=[srct.b, ident.b], writes=[p.b], same_ok=True)
                    eng = "scalar" if h % 2 == 0 else "vector"
                    if eng == "scalar":
                        P.op("scalar", lambda e, p=p, dstT=dstT, h=h: e.activation(out=dstT.t[:, h, :], in_=p.ap(), func=AF.Copy), reads=[p.b], writes=[dstT.b])
                    else:
                        P.op("vector", lambda e, p=p, dstT=dstT, h=h: e.tensor_copy(out=dstT.t[:, h, :], in_=p.ap()), reads=[p.b], writes=[dstT.b])
            P.dma("sync", lambda e, tb=tb: e.dma_start(out=sc["qgT"][tb], in_=qgT.t[:]), reads=[qgT.b], sembuf=qgT.b)

        def s2h(tb, h, qn, kn, vs):
            r0 = tb * 128
            pkk = nextpq()
            P.op("tensor", lambda e, p=pkk, h=h: mm32(e, p.ap(), lhsT=kT.t[:, h, :], rhs=kT.t[:, h, :], start=True, stop=True),
                 reads=[kT.b], writes=[pkk.b], same_ok=True)
            DS = DmS[h % 3]
            P.op("vector", lambda e, DS=DS, h=h: e.tensor_scalar(out=DS.t[:], in0=Gbb.t[:, h, :], scalar1=gc.t[:, h:h + 1], scalar2=0.0,
                                                                op0=ALU.subtract, op1=ALU.max),
                 reads=[Gbb.b, gc.b], writes=[DS.b])
            P.op("scalar", lambda e, DS=DS: e.activation(out=DS.t[:], in_=DS.t[:], func=AF.Exp, scale=-1.0), reads=[DS.b], writes=[DS.b])
            P.op("gpsimd", lambda e, DS=DS: e.tensor_tensor(out=DS.t[:], in0=DS.t[:], in1=MS01.t[:], op=ALU.mult), reads=[DS.b, MS01.b], writes=[DS.b])
            P.op("vector", lambda e, p=pkk, DS=DS, h=h: e.scalar_tensor_tensor(out=A_all.t[:, h, :], in0=p.ap(), scalar=beta.t[:, h:h + 1], in1=DS.t[:],
                                                                             op0=ALU.mult, op1=ALU.mult),
                 reads=[pkk.b, DS.b, beta.b], writes=[A_all.b])
            for c in range(2):
                pkq, pkc = nextpq(), nextpq()
                cs = slice(c * 64, (c + 1) * 64)
                P.op("tensor", lambda e, p=pkq, h=h, cs=cs: mm32(e, p.ap(64, 64), lhsT=kT.t[:, h, cs], rhs=qT.t[:, h, cs], start=True, stop=True),
                     reads=[kT.b, qT.b], writes=[pkq.b], same_ok=True)
                DT = DmT[(h * 2 + c) % 3]
                P.op("vector", lambda e, DT=DT, h=h, c=c, cs=cs: e.tensor_scalar(out=DT.t[:], in0=Gbb.t[0:64, h, cs], scalar1=ngcC.t[:, c, h:h + 1], scalar2=0.0,
                                                                                op0=ALU.add, op1=ALU.min),
                     reads=[Gbb.b, ngcC.b], writes=[DT.b])
                P.op("scalar", lambda e, DT=DT: e.activation(out=DT.t[:], in_=DT.t[:], func=AF.Exp), reads=[DT.b], writes=[DT.b])
                P.op("gpsimd", lambda e, DT=DT: e.tensor_tensor(out=DT.t[:], in0=DT.t[:], in1=ML01.t[:], op=ALU.mult), reads=[DT.b, ML01.b], writes=[DT.b])
                P.op("vector", lambda e, p=pkq, DT=DT, h=h, c=c: e.tensor_tensor(out=aqk_all.t[:, c, h, :], in0=p.ap(64, 64), in1=DT.t[:], op=ALU.mult),
                     reads=[pkq.b, DT.b], writes=[aqk_all.b])
                P.op("tensor", lambda e, p=pkc, h=h, cs=cs: e.transpose(out=p.ap(64, 128), in_=kT.t[:, h, cs], identity=ident.t[:]),
                     reads=[kT.b, ident.b], writes=[pkc.b], same_ok=True)
                P.op("vector", lambda e, p=pkc, h=h, c=c: e.tensor_scalar(out=kend_all.t[:, c, h, :], in0=p.ap(64, 128), scalar1=ekendC.t[:, c, h:h + 1],
                                                                         scalar2=None, op0=ALU.mult),
                     reads=[pkc.b, ekendC.b], writes=[kend_all.b])

        def s2z(tb):
            r0 = tb * 128
            for c in range(2):
                cg = tb * 2 + c
                P.dma("sync", lambda e, c=c, cg=cg: e.dma_start(out=sc["A"][cg].rearrange("h i j -> i h j"), in_=A_all.t[c * 64:(c + 1) * 64, :, c * 64:(c + 1) * 64]),
                      reads=[A_all.b], sembuf=A_all.b)
            P.dma("sync", lambda e, tb=tb: e.dma_start(out=sc["aqkT"][tb * 2:tb * 2 + 2].rearrange("c j h i -> j c h i"), in_=aqk_all.t[:]),
                  reads=[aqk_all.b], sembuf=aqk_all.b)
            P.dma("sync", lambda e, tb=tb: e.dma_start(out=sc["kend"][tb * 2:tb * 2 + 2].rearrange("c j h d -> j c h d"), in_=kend_all.t[:]),
                  reads=[kend_all.b], sembuf=kend_all.b)

        def s1_all(tb, third):
            s1(tb, third, (qn2[tb % 2], kn2[tb % 2], vs2[tb % 2])[third])

        for third in range(3):
            s1_all(0, third)
        for tb in range(NTILE):
            q_, k_, v_ = qn2[tb % 2], kn2[tb % 2], vs2[tb % 2]
            nxt = tb + 1 < NTILE
            if nxt:
                s1_all(tb + 1, 0)
            s2a(tb, q_, k_, v_)
            if nxt:
                s1_all(tb + 1, 1)
            for h in range(4):
                s2h(tb, h, q_, k_, v_)
            if nxt:
                s1_all(tb + 1, 2)
            for h in range(4, 8):
                s2h(tb, h, q_, k_, v_)
            s2z(tb)
        P.emit()


def gdn_solve_phase(P, nc, NT, sc, cin):
    NCH = NT // 64
    NSYS = NCH * 8
    NGRP = (NSYS + 127) // 128
    P.new_phase("gsolve")
    A_v = sc["A"].rearrange("c h i j -> (c h) (i j)")
    Tt_v = sc["Tt"].rearrange("c h j i -> (c h) (j i)")
    with ExitStack() as st:
        nset = min(2, NGRP)
        As = [sb(nc, st, f"As{i}", [128, 64, 64], F32) for i in range(nset)]
        Ts = [sb(nc, st, f"Ts{i}", [128, 64, 64], F32) for i in range(nset)]
        Tm = [sb(nc, st, f"Tm{i}", [128, 63, 63], F32) for i in range(nset)]
        Tb = [sb(nc, st, f"Tb{i}", [128, 64, 64], BF16) for i in range(nset)]
        for g0 in range(0, NGRP, nset):
            grp = list(range(g0, min(NGRP, g0 + nset)))
            Rs = {}
            for gi in grp:
                k = gi % nset
                A_, T_ = As[k], Ts[k]
                R = min(128, NSYS - gi * 128)
                Rs[gi] = R
                P.dma("sync", lambda e, A_=A_, gi=gi, R=R: e.dma_start(out=A_.t[0:R].rearrange("p i j -> p (i j)"), in_=A_v[gi * 128:gi * 128 + R, :]),
                      writes=[A_.b], sembuf=A_.b)
                P.dma("sync", lambda e, T_=T_: e.dma_start(out=T_.t[:].rearrange("p i j -> p (i j)"), in_=cin["I64"][:, :]),
                      writes=[T_.b], sembuf=T_.b)
            for i in range(1, 64):
                for gi in grp:
                    k = gi % nset
                    A_, T_, M_, R = As[k], Ts[k], Tm[k], Rs[gi]

                    def mul(e, A_=A_, T_=T_, M_=M_, i=i, R=R):
                        a = A_.t[0:R, i, 0:i]
                        in1 = bass.AP(a.tensor, a.offset, [list(a.ap[0]), [0, i], list(a.ap[1])])
                        in0 = T_.t[0:R, 0:i, 0:i].rearrange("p j c -> p c j")
                        return e.tensor_tensor(out=M_.t[0:R, 0:i, 0:i], in0=in0, in1=in1, op=ALU.mult)
                    P.op("gpsimd" if k == 1 else "vector", mul, reads=[A_.b, T_.b], writes=[M_.b])
                for gi in grp:
                    k = gi % nset
                    T_, M_, R = Ts[k], Tm[k], Rs[gi]
                    P.op("vector", lambda e, T_=T_, M_=M_, i=i, R=R: e.tensor_reduce(out=T_.t[0:R, i, 0:i], in_=M_.t[0:R, 0:i, 0:i], axis=AX.X, op=ALU.add, negate=True),
                         reads=[M_.b], writes=[T_.b])
            for gi in grp:
                k = gi % nset
                B_, T_, R = Tb[k], Ts[k], Rs[gi]
                P.op("vector", lambda e, B_=B_, T_=T_, R=R: e.tensor_copy(out=B_.t[0:R], in_=T_.t[0:R].rearrange("p i j -> p j i")), reads=[T_.b], writes=[B_.b])
                P.dma("sync", lambda e, B_=B_, gi=gi, R=R: e.dma_start(out=Tt_v[gi * 128:gi * 128 + R, :], in_=B_.t[0:R].rearrange("p j i -> p (j i)")),
                      reads=[B_.b], sembuf=B_.b)
        P.emit()


def gdn_scan_phase(P, nc, NT, sc, cin, consts, egl_all, pname="gscan", with_out=True, init_ap=None, final_ap=None):
    NTILE = NT // 128
    P.new_phase(pname)
    ident = consts["ident_f32"]
    with ExitStack() as st:
        kbg = [sb(nc, st, f"kbg{i}", [128, 8, 128], BF16) for i in range(2)]
        vb = [sb(nc, st, f"vb{i}", [128, 8, 128], BF16) for i in range(2)]
        qgT = [sb(nc, st, f"qgT{i}", [128, 8, 128], BF16) for i in range(2)]
        aqk = [sb(nc, st, f"aqk{i}", [64, 2, 8, 64], BF16) for i in range(2)]
        kend = [sb(nc, st, f"kend{i}", [64, 2, 8, 128], BF16) for i in range(2)]
        TtBD = [sb(nc, st, f"TtBD{i}", [128, 8, 128], BF16) for i in range(2)]
        gate = [sb(nc, st, f"gate{i}", [64, 2, 8, 128], F32) for i in range(2)]
        wT = sb(nc, st, "wT", [128, 8, 128], BF16)
        u = sb(nc, st, "u", [64, 2, 8, 128], F32)
        o = sb(nc, st, "o", [64, 2, 8, 128], F32)
        sq = sb(nc, st, "sq", [64, 2, 8, 128], F32)
        S = [sb(nc, st, f"S{h}", [128, 128], F32) for h in range(8)]
        vnew = [sb(nc, st, f"vnew{i}", [64, 128], BF16) for i in range(8)]
        Sb = [sb(nc, st, f"Sb{h}", [128, 128], BF16) for h in range(8)]
        ssq = sb(nc, st, "ssq", [64, 16], F32)
        mh16 = sb(nc, st, "mh16", [64, 16], F32)
        gnb = sb(nc, st, "gnb", [128, 128], F32)
        yTt = [sb(nc, st, f"yTt{i}", [128, 8, 128], BF16) for i in range(2)]
        banks = [st.enter_context(nc.psum_tensor(f"gs_{pname}_bank{i}", [128, 512], F32)) for i in range(8)]
        bankbufs = [Buf(f"bank{i}") for i in range(8)]
        pq = [PQ(banks[i % 8], i // 8, bankbufs[i % 8]) for i in range(32)]
        pqi = [0]

        def nextpq():
            p = pq[pqi[0] % 32]
            pqi[0] += 1
            return p

        P.op("vector", lambda e: e.memset(mh16.t[:], -0.5), writes=[mh16.b])
        P.dma("sync", lambda e: e.dma_start(out=gnb.t[:], in_=cin["gdn_norm_b"][:, :]), writes=[gnb.b], sembuf=gnb.b)
        for i in range(2):
            P.op("gpsimd", lambda e, i=i: e.memset(TtBD[i].t[:], 0.0), writes=[TtBD[i].b])
        for h in range(8):
            if init_ap is None:
                P.op("vector", lambda e, h=h: e.memset(S[h].t[:], 0.0), writes=[S[h].b])
            else:
                if h == 0:
                    flagS = sb(nc, st, "flagS", [128, 1], F32)
                    P.dma("sync", lambda e: e.dma_start(out=flagS.t[:], in_=cin["flag01"][:, :]), writes=[flagS.b], sembuf=flagS.b)
                P.dma("sync", lambda e, h=h: e.dma_start(out=S[h].t[:], in_=init_ap[h]), writes=[S[h].b], sembuf=S[h].b)
                P.op("vector", lambda e, h=h: e.tensor_scalar(out=S[h].t[:], in0=S[h].t[:], scalar1=flagS.t[:, 0:1], scalar2=None, op0=ALU.mult),
                     reads=[S[h].b, flagS.b], writes=[S[h].b])
        for h in range(8):
            P.op("scalar", lambda e, h=h: e.activation(out=Sb[h].t[:], in_=S[h].t[:], func=AF.Copy), reads=[S[h].b], writes=[Sb[h].b])
        for tb in range(NTILE):
            r0 = tb * 128
            k_ = tb % 2
            KB, VB, QG, AQ, KE, TB_, GT, YT = kbg[k_], vb[k_], qgT[k_], aqk[k_], kend[k_], TtBD[k_], gate[k_], yTt[k_]
            P.dma("sync", lambda e, KB=KB, r0=r0: e.dma_start(out=KB.t[:].rearrange("p h d -> p (h d)"), in_=sc["kbg"][r0:r0 + 128, :]), writes=[KB.b], sembuf=KB.b)
            P.dma("sync", lambda e, VB=VB, r0=r0: e.dma_start(out=VB.t[:].rearrange("p h d -> p (h d)"), in_=sc["vb"][r0:r0 + 128, :]), writes=[VB.b], sembuf=VB.b)
            P.dma("sync", lambda e, AQ=AQ, tb=tb: e.dma_start(out=AQ.t[:], in_=sc["aqkT"][tb * 2:tb * 2 + 2].rearrange("c j h i -> j c h i")), writes=[AQ.b], sembuf=AQ.b)
            P.dma("sync", lambda e, KE=KE, tb=tb: e.dma_start(out=KE.t[:], in_=sc["kend"][tb * 2:tb * 2 + 2].rearrange("c j h d -> j c h d")), writes=[KE.b], sembuf=KE.b)
            for c in range(2):
                P.dma("sync", lambda e, TB_=TB_, tb=tb, c=c: e.dma_start(out=TB_.t[c * 64:(c + 1) * 64, :, c * 64:(c + 1) * 64],
                                                                       in_=sc["Tt"][tb * 2 + c].rearrange("h j i -> j h i")), writes=[TB_.b], sembuf=TB_.b)
            if with_out:
                P.dma("sync", lambda e, QG=QG, tb=tb: e.dma_start(out=QG.t[:], in_=sc["qgT"][tb]), writes=[QG.b], sembuf=QG.b)
                P.dma("sync", lambda e, GT=GT, r0=r0: e.dma_start(out=GT.t[:], in_=sc["gate"][r0:r0 + 128, :].rearrange("(c p) (h d) -> p c h d", c=2, h=8)),
                      writes=[GT.b], sembuf=GT.b)
            for h in range(8):
                p = nextpq()
                P.op("tensor", lambda e, p=p, h=h, KB=KB, TB_=TB_: mm32(e, p.ap(), lhsT=KB.t[:, h, :], rhs=TB_.t[:, h, :], start=True, stop=True),
                     reads=[KB.b, TB_.b], writes=[p.b], same_ok=True)
                P.op("scalar", lambda e, p=p, h=h: e.activation(out=wT.t[:, h, :], in_=p.ap(), func=AF.Copy), reads=[p.b], writes=[wT.b])
                for c in range(2):
                    p2 = nextpq()
                    P.op("tensor", lambda e, p=p2, h=h, c=c, VB=VB, TB_=TB_: mm32(e, p.ap(64, 128), lhsT=TB_.t[:, h, c * 64:(c + 1) * 64], rhs=VB.t[:, h, :], start=True, stop=True),
                         reads=[VB.b, TB_.b], writes=[p2.b], same_ok=True)
                    P.op("vector", lambda e, p=p2, h=h, c=c: e.tensor_copy(out=u.t[:, c, h, :], in_=p.ap(64, 128)), reads=[p2.b], writes=[u.b])
            for c in range(2):
                cg = tb * 2 + c
                pws = [nextpq() for _ in range(8)]
                for h in range(8):
                    P.op("tensor", lambda e, p=pws[h], h=h, c=c: mm32(e, p.ap(64, 128), lhsT=wT.t[:, h, c * 64:(c + 1) * 64], rhs=Sb[h].t[:], start=True, stop=True),
                         reads=[wT.b, Sb[h].b], writes=[pws[h].b], same_ok=True)
                for h in range(8):
                    P.op("vector", lambda e, p=pws[h], h=h, c=c: e.tensor_tensor(out=vnew[h].t[:], in0=u.t[:, c, h, :], in1=p.ap(64, 128), op=ALU.subtract),
                         reads=[u.b, pws[h].b], writes=[vnew[h].b])
                for h in range(8):
                    if with_out:
                        po = nextpq()
                        P.op("tensor", lambda e, p=po, h=h, c=c, QG=QG: mm32(e, p.ap(64, 128), lhsT=QG.t[:, h, c * 64:(c + 1) * 64], rhs=Sb[h].t[:], start=True, stop=False),
                             reads=[QG.b, Sb[h].b], writes=[po.b], same_ok=True)
                        P.op("tensor", lambda e, p=po, h=h, c=c, AQ=AQ: mm32(e, p.ap(64, 128), lhsT=AQ.t[:, c, h, :], rhs=vnew[h].t[:], start=False, stop=True),
                             reads=[AQ.b, vnew[h].b], writes=[po.b], same_ok=True)
                        P.op("scalar", lambda e, p=po, h=h, c=c: e.activation(out=o.t[:, c, h, :], in_=p.ap(64, 128), func=AF.Copy), reads=[po.b], writes=[o.b])
                    psu = nextpq()
                    P.op("tensor", lambda e, p=psu, h=h, c=c, KE=KE: mm32(e, p.ap(), lhsT=KE.t[:, c, h, :], rhs=vnew[h].t[:], start=True, stop=True),
                         reads=[KE.b, vnew[h].b], writes=[psu.b], same_ok=True)
                    P.op("vector", lambda e, p=psu, h=h, cg=cg: e.scalar_tensor_tensor(out=S[h].t[:], in0=S[h].t[:], scalar=egl_all.t[:, cg, h:h + 1], in1=p.ap(),
                                                                                     op0=ALU.mult, op1=ALU.add),
                         reads=[S[h].b, psu.b, egl_all.b], writes=[S[h].b])
                    P.op("scalar", lambda e, h=h: e.activation(out=Sb[h].t[:], in_=S[h].t[:], func=AF.Copy), reads=[S[h].b], writes=[Sb[h].b])
            if with_out:
                o16 = o.t[:].rearrange("p c h d -> p (c h) d")
                sq16 = sq.t[:].rearrange("p c h d -> p (c h) d")
                g16 = GT.t[:].rearrange("p c h d -> p (c h) d")
                P.op("gpsimd", lambda e, o16=o16, sq16=sq16: e.tensor_tensor(out=sq16, in0=o16, in1=o16, op=ALU.mult), reads=[o.b], writes=[sq.b])
                P.op("vector", lambda e, sq16=sq16: e.tensor_reduce(out=ssq.t[:], in_=sq16, axis=AX.X, op=ALU.add), reads=[sq.b], writes=[ssq.b])
                P.op("vector", lambda e: e.tensor_scalar(out=ssq.t[:], in0=ssq.t[:], scalar1=1.0 / 128.0, scalar2=EPS, op0=ALU.mult, op1=ALU.add),
                     reads=[ssq.b], writes=[ssq.b])
                P.op("gpsimd", lambda e: e.tensor_tensor(out=ssq.t[:], in0=ssq.t[:], in1=mh16.t[:], op=ALU.pow), reads=[ssq.b, mh16.b], writes=[ssq.b])
                P.op("vector", lambda e, o16=o16: e.tensor_tensor(out=o16, in0=o16, in1=bcast_last(ssq.t[:, :], 128), op=ALU.mult), reads=[o.b, ssq.b], writes=[o.b])
                gn = gnb.t[0:64, :]
                gnb16 = bass.AP(gn.tensor, gn.offset, [list(gn.ap[0]), [0, 16], list(gn.ap[1])])
                P.op("gpsimd", lambda e, o16=o16, gnb16=gnb16: e.tensor_tensor(out=o16, in0=o16, in1=gnb16, op=ALU.mult), reads=[o.b, gnb.b], writes=[o.b])
                P.op("scalar", lambda e, g16=g16: e.activation(out=g16, in_=g16, func=AF.Silu), reads=[GT.b], writes=[GT.b])
                P.op("vector", lambda e, o16=o16, g16=g16: e.tensor_tensor(out=o16, in0=o16, in1=g16, op=ALU.mult), reads=[o.b, GT.b], writes=[o.b])
                for h in range(8):
                    p = nextpq()
                    for c in range(2):
                        P.op("tensor", lambda e, p=p, h=h, c=c: e.transpose(out=p.bank[0:128, p.q * 128 + c * 64:p.q * 128 + (c + 1) * 64], in_=o.t[:, c, h, :],
                                                                           identity=ident.t[0:64, 0:64]),
                             reads=[o.b, ident.b], writes=[p.b], same_ok=True)
                    if h % 2 == 0:
                        P.op("scalar", lambda e, p=p, h=h, YT=YT: e.activation(out=YT.t[:, h, :], in_=p.ap(), func=AF.Copy), reads=[p.b], writes=[YT.b])
                    else:
                        P.op("vector", lambda e, p=p, h=h, YT=YT: e.tensor_copy(out=YT.t[:, h, :], in_=p.ap()), reads=[p.b], writes=[YT.b])
                P.dma("sync", lambda e, YT=YT, r0=r0: e.dma_start(out=sc["yT"][8:16, :, r0:r0 + 128].rearrange("h d t -> d h t"), in_=YT.t[:]),
                      reads=[YT.b], sembuf=YT.b)
        if final_ap is not None:
            for h in range(8):
                P.dma("sync", lambda e, h=h: e.dma_start(out=final_ap[h], in_=S[h].t[:]), reads=[S[h].b], sembuf=S[h].b)
        P.emit()


def wout_phase(P, nc, NT, sc, w_out):
    NTILE = NT // 128
    P.new_phase("wout")
    with ExitStack() as st:
        wo = sb(nc, st, "wo", [128, 16, D], BF16)
        yt = [sb(nc, st, f"yt{i}", [128, 16, 128], BF16) for i in range(2)]
        xr = [sb(nc, st, f"xr{i}", [128, D], F32) for i in range(2)]
        xo = [sb(nc, st, f"xo{i}", [128, D], F32) for i in range(2)]
        pbank = [ps(nc, st, f"pb{i}", [128, 512]) for i in range(8)]
        w_v = w_out.rearrange("(c p) n -> p c n", p=128)
        for q4 in range(4):
            P.dma("gpsimd", lambda e, q4=q4: e.dma_start(out=wo.t[:, q4 * 4:(q4 + 1) * 4, :], in_=w_v[:, q4 * 4:(q4 + 1) * 4, :]), writes=[wo.b], sembuf=wo.b)
        for tb in range(NTILE):
            r0 = tb * 128
            YT, XR, XO = yt[tb % 2], xr[tb % 2], xo[tb % 2]
            P.dma("sync", lambda e, YT=YT, r0=r0: e.dma_start(out=YT.t[:], in_=sc["yT"][:, :, r0:r0 + 128].rearrange("c d t -> d c t")), writes=[YT.b], sembuf=YT.b)
            P.dma("sync", lambda e, XR=XR, r0=r0: e.dma_start(out=XR.t[:], in_=sc["x1"][r0:r0 + 128, :]), writes=[XR.b], sembuf=XR.b)
            for n in range(4):
                PB = pbank[(tb * 4 + n) % 8]
                for c in range(16):
                    P.op("tensor", lambda e, PB=PB, YT=YT, c=c, n=n: e.matmul(PB.t[:, :], lhsT=YT.t[:, c, :], rhs=wo.t[:, c, n * 512:(n + 1) * 512],
                                                                           start=(c == 0), stop=(c == 15)),
                         reads=[YT.b, wo.b], writes=[PB.b], same_ok=True)
                P.op("vector", lambda e, PB=PB, XR=XR, XO=XO, n=n: e.tensor_tensor(out=XO.t[:, n * 512:(n + 1) * 512], in0=PB.t[:, :], in1=XR.t[:, n * 512:(n + 1) * 512], op=ALU.add),
                     reads=[PB.b, XR.b], writes=[XO.b])
            P.dma("sync", lambda e, XO=XO, r0=r0: e.dma_start(out=sc["x2"][r0:r0 + 128, :], in_=XO.t[:]), reads=[XO.b], sembuf=XO.b)
        P.emit()


def fnorm_phase(P, nc, NT, src, dst, gain_b_d, consts):
    NTILE = NT // 128
    P.new_phase("fnorm")
    mhalf = consts["mhalf"]
    with ExitStack() as st:
        gb = sb(nc, st, "gb", [128, D], F32)
        xs = [sb(nc, st, f"xs{i}", [128, D], F32) for i in range(3)]
        xq = sb(nc, st, "xq", [128, D], F32)
        ssq = [sb(nc, st, f"ssq{i}", [128, 1], F32) for i in range(3)]
        P.dma("sync", lambda e: e.dma_start(out=gb.t[:], in_=gain_b_d[:, :]), writes=[gb.b], sembuf=gb.b)
        for tb in range(NTILE):
            r0 = tb * 128
            X, SS = xs[tb % 3], ssq[tb % 3]
            P.dma("sync", lambda e, X=X, r0=r0: e.dma_start(out=X.t[:], in_=src[r0:r0 + 128, :]), writes=[X.b], sembuf=X.b)
            P.op("scalar", lambda e, X=X, SS=SS: e.activation(out=xq.t[:], in_=X.t[:], func=AF.Square, accum_out=SS.t[:]), reads=[X.b], writes=[xq.b, SS.b])
            P.op("vector", lambda e, SS=SS: e.tensor_scalar(out=SS.t[:], in0=SS.t[:], scalar1=1.0 / D, scalar2=EPS, op0=ALU.mult, op1=ALU.add), reads=[SS.b], writes=[SS.b])
            P.op("gpsimd", lambda e, SS=SS: e.tensor_tensor(out=SS.t[:], in0=SS.t[:], in1=mhalf.t[:], op=ALU.pow), reads=[SS.b, mhalf.b], writes=[SS.b])
            P.op("vector", lambda e, X=X, SS=SS: e.scalar_tensor_tensor(out=X.t[:], in0=X.t[:], scalar=SS.t[:, 0:1], in1=gb.t[:], op0=ALU.mult, op1=ALU.mult),
                 reads=[X.b, SS.b, gb.b], writes=[X.b])
            P.dma("sync", lambda e, X=X, r0=r0: e.dma_start(out=dst[r0:r0 + 128, :], in_=X.t[:]), reads=[X.b], sembuf=X.b)
        P.emit()


PAIRS = [[0, 1], [2, 3], [4, 5], [6, 7]]


def xchg_phase(P, nc, name, ccsem, cccount, items, pre_copy=None):
    P.new_phase(name)
    if pre_copy is not None:
        dummy = Buf("cp")
        for (src_ap, dst_ap) in pre_copy:
            P.dma("gpsimd", lambda e, src_ap=src_ap, dst_ap=dst_ap: e.dma_start(out=dst_ap, in_=src_ap), writes=[dummy], sembuf=dummy)
    prev = [None]
    for (src, dst) in items:
        cccount[0] += 1
        n = cccount[0]

        def fn(e, src=src, dst=dst, n=n):
            e.collective_compute("AllGather", ALU.bypass, replica_groups=PAIRS,
                                 ins=[src.ap().opt()], outs=[dst.ap().opt()]).then_inc(ccsem)
            return e.wait_ge(ccsem, n)
        b = Buf("cc")
        reads = [dummy] if pre_copy is not None else []
        P.op("gpsimd", fn, reads=reads, writes=[b])
    P.emit()


class LazyIn(dict):
    def __init__(self, nc, shapes):
        super().__init__()
        self.nc = nc
        self.shapes = shapes

    def __missing__(self, name):
        ap = self.nc.dram_tensor(name, list(self.shapes[name]), F32, kind="ExternalInput").ap()
        self[name] = ap
        return ap


def input_shapes(NT):
    sh = {"x": [NT, D], "w_in": [D, 7184], "w_out": [D, D]}
    for nm in ("ffn1_norm", "mix_norm", "ffn2_norm", "final_norm_t"):
        sh[nm] = [128, NKC]
    for pre in ("ffn1", "ffn2"):
        sh[pre + "_w_gate"] = [D, DFF]
        sh[pre + "_w_up"] = [D, DFF]
        sh[pre + "_w_down"] = [DFF, D]
    sh.update({"ident_f32": [128, 128], "trineg": [128, 128], "tricomp": [128, 128], "negones": [128, 128], "ones32": [128, 128],
               "dmask": [4, 128, 512], "negbias": [128, 1], "sb_out_norm": [128, 1],
               "final_norm_b": [128, D], "convw_b": [128, 4 * 3072], "alog_b": [128, 8], "dtb_b": [128, 8],
               "BT": [128, 128], "BL": [128, 128], "selC": [128, 256], "selH": [8, 1024], "MUs": [128, 128],
               "ML64": [64, 64], "MS01": [128, 128], "ML01": [64, 64], "I64": [128, 4096], "gdn_norm_b": [128, 128], "flag01": [128, 1]})
    return sh


def const_inputs(j):
    c = {}
    c["ident_f32"] = np.eye(128, dtype=np.float32)
    jj = np.arange(128)[:, None]
    ss = np.arange(128)[None, :]
    c["trineg"] = np.where(jj >= ss, -1.0, 0.0).astype(np.float32)
    c["negones"] = -np.ones((128, 128), np.float32)
    c["tricomp"] = np.where(jj < ss, -1.0, 0.0).astype(np.float32)
    c["ones32"] = np.ones((128, 128), np.float32)
    t = np.arange(512)[None, None, :]
    r = np.arange(4)[:, None, None]
    sp = np.arange(128)[None, :, None]
    c["dmask"] = ((r * 128 + sp) < t).astype(np.float32)
    c["negbias"] = np.full((128, 1), 0.0 if j == 1 else -1.0e4, np.float32)
    c["flag01"] = np.full((128, 1), 1.0 if j == 1 else 0.0, np.float32)
    ch = np.arange(128) // 64
    same = ch[:, None] == ch[None, :]
    ii = np.arange(128)
    c["BT"] = (same & (ii[:, None] <= ii[None, :])).astype(np.float32)
    c["BL"] = same.astype(np.float32)
    selC = np.zeros((128, 2, 128), np.float32)
    selC[:64, 0, :] = 1.0
    selC[64:, 1, :] = 1.0
    c["selC"] = selC.reshape(128, 256)
    selH = np.zeros((8, 8, 128), np.float32)
    for h in range(8):
        selH[h, h, :] = 1.0
    c["selH"] = selH.reshape(8, 1024)
    c["MUs"] = np.where(same & (ii[None, :] < ii[:, None]), 0.0, 1.0e4).astype(np.float32)
    i64 = np.arange(64)
    c["ML64"] = np.where(i64[:, None] <= i64[None, :], 0.0, -1.0e4).astype(np.float32)
    c["MS01"] = (same & (ii[None, :] < ii[:, None])).astype(np.float32)
    c["ML01"] = (i64[:, None] <= i64[None, :]).astype(np.float32)
    c["I64"] = np.tile(np.eye(64, dtype=np.float32).reshape(1, 4096), (128, 1))
    return c


def build_program(NT=2048, NPREV=0, phases=("ffn1", "proj", "attn", "gdn", "wout", "ffn2", "fnorm"), dbg=(), exchange=False):
    nc = bass.Bass("TRN2", target_bir_lowering=False)
    cin = LazyIn(nc, input_shapes(NT))

    def dsc(name, shape, dtype=F32, ap=True):
        kind = "ExternalOutput" if name in dbg else "Internal"
        t = nc.dram_tensor(name, list(shape), dtype, kind=kind)
        return t.ap() if ap else t

    NK = NPREV + NT
    NCH = NT // 64
    out = nc.dram_tensor("out", [NT, D], F32, kind="ExternalOutput").ap()
    sc = {}
    sc["x1"] = dsc("x1", [NT, D]) if "ffn1" in phases else cin["x"]
    sc["x2"] = dsc("x2", [NT, D])
    sc["x3"] = dsc("x3", [NT, D])
    sc["qT"] = dsc("qT", [8, 128, NT], BF16)
    xg = None
    if exchange:
        sc["kT"] = dsc("kT", [8, 128, NT], BF16)
        sc["v"] = dsc("v", [NT, 1024], BF16)
        HK = 4 * 128
        HV = NT // 2
        ks_t = [dsc(f"ksrc{i}", [HK, NT], BF16, ap=False) for i in range(2)]
        kd_t = [dsc(f"kdst{i}", [2 * HK, NT], BF16, ap=False) for i in range(2)]
        vs_t = [dsc(f"vsrc{i}", [HV, 1024], BF16, ap=False) for i in range(2)]
        vd_t = [dsc(f"vdst{i}", [2 * HV, 1024], BF16, ap=False) for i in range(2)]
        hs_t = dsc("hist_src", [3, 3072], F32, ap=False)
        hd_t = dsc("hist_dst", [6, 3072], F32, ap=False)
        ss_t = dsc("st_src", [8 * 128, 128], F32, ap=False)
        sd_t = dsc("st_dst", [2 * 8 * 128, 128], F32, ap=False)
        xg = {"kd": [t.ap() for t in kd_t], "vd": [t.ap() for t in vd_t], "hist_dst": hd_t.ap(), "HV": HV}
    else:
        sc["kT"] = dsc("kT", [8, 128, NK], BF16)
        sc["v"] = dsc("v", [NK, 1024], BF16)
    sc["graw"] = dsc("graw", [3 + NT, 3072])
    sc["ab"] = dsc("ab", [NT, 16])
    sc["gate"] = dsc("gate", [NT, 1024])
    sc["yT"] = dsc("yT", [16, 128, NT], BF16)
    sc["gcrow"] = dsc("gcrow", [NT // 128, 8, 128])
    sc["kbg"] = dsc("kbg", [NT, 1024], BF16)
    sc["vb"] = dsc("vb", [NT, 1024], BF16)
    sc["qgT"] = dsc("qgT", [NT // 128, 128, 8, 128], BF16)
    sc["aqkT"] = dsc("aqkT", [NCH, 64, 8, 64], BF16)
    sc["kend"] = dsc("kend", [NCH, 64, 8, 128], BF16)
    sc["A"] = dsc("A", [NCH, 8, 64, 64])
    sc["Tt"] = dsc("Tt", [NCH, 8, 64, 64], BF16)

    with ExitStack() as stack:
        P = Prog(nc, stack)
        ccsem = stack.enter_context(nc.semaphore("ccsem"))
        cccount = [0]
        consts = {}
        consts["ident_f32"] = sb(nc, stack, "ident_f32", [128, 128], F32)
        consts["mhalf"] = sb(nc, stack, "mhalf", [128, 1], F32)
        P.new_phase("const")
        c = consts["ident_f32"]
        P.dma("sync", lambda e: e.dma_start(out=c.t[:], in_=cin["ident_f32"][:, :]), writes=[c.b], sembuf=c.b)
        m = consts["mhalf"]
        P.op("vector", lambda e: e.memset(m.t[:], -0.5), writes=[m.b])
        P.emit()

        if "ffn1" in phases:
            ffn_phase(P, nc, "ffn1", NT, cin["x"], sc["x1"], cin["ffn1_norm"], cin["ffn1_w_gate"], cin["ffn1_w_up"],
                      cin["ffn1_w_down"], consts)
        if "proj" in phases:
            proj_phase(P, nc, NT, 0 if exchange else NPREV, sc["x1"], cin["mix_norm"], cin["w_in"], sc, consts)
        if exchange:
            kflat = sc["kT"].rearrange("h d n -> (h d) n")
            pre = [(kflat[i * HK:(i + 1) * HK, :], ks_t[i].ap()) for i in range(2)]
            pre += [(sc["v"][i * HV:(i + 1) * HV, :], vs_t[i].ap()) for i in range(2)]
            pre += [(sc["graw"][NT:NT + 3, :], hs_t.ap())]
            xchg_phase(P, nc, "xchg1", ccsem, cccount,
                       [(ks_t[0], kd_t[0]), (ks_t[1], kd_t[1]), (vs_t[0], vd_t[0]), (vs_t[1], vd_t[1]), (hs_t, hd_t)], pre_copy=pre)
        if "attn" in phases:
            attn_phase(P, nc, NT, NPREV, sc, cin, consts, xg=xg)
        if "gdn" in phases:
            egl_all = sb(nc, stack, "egl_all", [128, NCH, 8], F32)
            gdn_prep_phase(P, nc, NT, sc, cin, consts, egl_all, xg=xg)
            if "nosolve" not in phases:
                gdn_solve_phase(P, nc, NT, sc, cin)
            if "noscan" not in phases:
                if exchange:
                    ss_v = ss_t.ap().rearrange("(h k) v -> h k v", h=8)
                    sd_v = sd_t.ap().rearrange("(r h k) v -> r h k v", r=2, h=8)
                    gdn_scan_phase(P, nc, NT, sc, cin, consts, egl_all, pname="gscanA", with_out=False, final_ap=ss_v)
                    xchg_phase(P, nc, "xchg2", ccsem, cccount, [(ss_t, sd_t)])
                    gdn_scan_phase(P, nc, NT, sc, cin, consts, egl_all, pname="gscanB", with_out=True, init_ap=sd_v[0])
                else:
                    gdn_scan_phase(P, nc, NT, sc, cin, consts, egl_all)
        if "wout" in phases:
            wout_phase(P, nc, NT, sc, cin["w_out"])
        if "ffn2" in phases:
            ffn_phase(P, nc, "ffn2", NT, sc["x2"], sc["x3"], cin["ffn2_norm"], cin["ffn2_w_gate"], cin["ffn2_w_up"],
                      cin["ffn2_w_down"], consts)
        if "fnorm" in phases:
            fnorm_phase(P, nc, NT, sc["x3"], out, cin["final_norm_b"], consts)
    return nc


MODE = "T"
_CACHE = {}


def kernel(**inputs):
    f32 = lambda a: np.ascontiguousarray(np.asarray(a, dtype=np.float32))
    x = f32(inputs["x"])
    B, S, _ = x.shape
    NT = S if MODE == "A" else S // 2
    NPREV = 0 if MODE == "A" else S // 2
    key = (MODE, NT, NPREV)
    if key not in _CACHE:
        _CACHE[key] = build_program(NT=NT, NPREV=NPREV, exchange=(MODE == "T"))
    nc = _CACHE[key]

    def nt(v):
        return np.ascontiguousarray(f32(v).reshape(NKC, 128).T)

    shared = {}
    shared["ffn1_norm"] = nt(inputs["ffn1_norm"][0])
    shared["mix_norm"] = nt(inputs["mix_norm"][0])
    shared["ffn2_norm"] = nt(inputs["ffn2_norm"][0])
    shared["final_norm_b"] = np.ascontiguousarray(np.tile(f32(inputs["final_norm"]).reshape(1, D), (128, 1)))
    for pre in ("ffn1", "ffn2"):
        for w in ("w_gate", "w_up", "w_down"):
            shared[f"{pre}_{w}"] = f32(inputs[f"{pre}_{w}"][0])
    shared["w_in"] = f32(inputs["w_in"][0])
    shared["w_out"] = f32(inputs["w_out"][0])
    shared["sb_out_norm"] = f32(inputs["sb_out_norm"][0]).reshape(128, 1)
    shared["convw_b"] = np.ascontiguousarray(np.tile(f32(inputs["conv_w"][0]).reshape(1, -1), (128, 1)))
    shared["alog_b"] = np.ascontiguousarray(np.tile(f32(inputs["a_log"][0]).reshape(1, 8), (128, 1)))
    shared["dtb_b"] = np.ascontiguousarray(np.tile(f32(inputs["dt_bias"][0]).reshape(1, 8), (128, 1)))
    shared["gdn_norm_b"] = np.ascontiguousarray(np.tile(f32(inputs["gdn_out_norm"][0]).reshape(1, 128), (128, 1)))
    consts = [const_inputs(0), const_inputs(1)]
    in_maps = []
    for c in range(8):
        b, j = c // 2, c % 2
        m = dict(shared)
        m.update(consts[j])
        if MODE == "A":
            m["x"] = x[b]
        else:
            m["x"] = np.ascontiguousarray(x[b, j * NT:(j + 1) * NT])
        in_maps.append(m)
    res = run_bass_kernel_spmd(nc, in_maps, core_ids=list(range(8)))
    out = np.empty((B, S, D), np.float32)
    for c in range(8):
        b, j = c // 2, c % 2
        if MODE == "A":
            if j == 0:
                out[b] = res.results[c]["out"]
        else:
            out[b, j * NT:(j + 1) * NT] = res.results[c]["out"]
    return out
```

```python
from contextlib import ExitStack
import numpy as np
import concourse.bass as bass
import concourse.mybir as mybir
from concourse.bass_utils import run_bass_kernel_spmd

F32 = mybir.dt.float32
BF16 = mybir.dt.bfloat16
AF = mybir.ActivationFunctionType
ALU = mybir.AluOpType
AX = mybir.AxisListType

D = 2048
DFF = 5504
NF = DFF // 128
NKC = D // 128
SEQ = 4096
EPS = 1e-6
ENGS = ("tensor", "scalar", "vector", "gpsimd", "sync")


class Buf:
    __slots__ = ("name", "w", "r", "rd", "dsem")

    def __init__(self, name):
        self.name = name
        self.w = None
        self.r = {}
        self.rd = []
        self.dsem = None


class Op:
    __slots__ = ("eng", "fn", "deps", "sig", "sem", "val", "inc", "is_dma", "ph")

    def __init__(self, eng, fn, is_dma):
        self.eng = eng
        self.fn = fn
        self.deps = []
        self.sig = False
        self.sem = None
        self.val = None
        self.inc = 1
        self.is_dma = is_dma


class Prog:
    def __init__(self, nc, stack, n_dma_sems=40):
        self.nc = nc
        self.stack = stack
        self.dma_sems = []
        for i in range(n_dma_sems):
            h = stack.enter_context(nc.semaphore(f"dq{i}"))
            self.dma_sems.append([h, 0])
        self.n_dma_sems = n_dma_sems

    def new_phase(self, name):
        self.pname = name
        self.ops = {e: [] for e in ENGS}
        self.phase_dma_ops = []
        self.prog_sems = {}
        for e in ENGS[:4]:
            self.prog_sems[e] = self.stack.enter_context(self.nc.semaphore(f"pg_{name}_{e}"))
        self.free_dma = {"sync": list(range(12, self.n_dma_sems)), "gpsimd": list(range(0, 12)),
                         "scalar": []}

    def op(self, eng, fn, reads=(), writes=(), same_ok=False):
        o = Op(eng, fn, False)
        self._deps(o, reads, writes, same_ok)
        self.ops[eng].append(o)
        return o

    def dma(self, eng, fn, reads=(), writes=(), sembuf=None):
        o = Op(eng, fn, True)
        self._deps(o, reads, writes, False)
        if sembuf.dsem is None or sembuf.dsem[0] != self.pname:
            sembuf.dsem = (self.pname, self.free_dma[eng].pop())
        ent = self.dma_sems[sembuf.dsem[1]]
        ent[1] += 16
        o.sem = ent[0]
        o.val = ent[1]
        o.inc = 16
        o.sig = True
        self.ops[eng].append(o)
        self.phase_dma_ops.append(o)
        return o

    def _deps(self, o, reads, writes, same_ok):
        o.ph = self.pname
        deps = []
        for b in reads:
            if b.w is not None:
                deps.append(b.w)
        for b in writes:
            if b.w is not None:
                deps.append(b.w)
            deps.extend(b.r.values())
            deps.extend(b.rd)
        for d in deps:
            if d is o or d.ph != o.ph:
                continue
            if same_ok and (not d.is_dma) and d.eng == o.eng:
                continue
            o.deps.append(d)
        for b in reads:
            if o.is_dma:
                b.rd.append(o)
            else:
                b.r[o.eng] = o
        for b in writes:
            b.w = o
            b.r = {}
            b.rd = []

    def emit(self):
        nc = self.nc
        for e in ENGS:
            for o in self.ops[e]:
                for d in o.deps:
                    d.sig = True
        for e in ENGS[:4]:
            cnt = 0
            for o in self.ops[e]:
                if o.is_dma:
                    continue
                if o.sig:
                    cnt += 1
                    o.sem = self.prog_sems[e]
                    o.val = cnt
        finals = {e: {} for e in ENGS}
        for o in self.phase_dma_ops:
            k = id(o.sem)
            cur = finals[o.eng].get(k)
            if cur is None or cur[1] < o.val:
                finals[o.eng][k] = (o.sem, o.val)
        ops = self.ops

        def replay(e, eng):
            waited = {}
            for o in ops[e]:
                need = {}
                for d in o.deps:
                    k = id(d.sem)
                    if waited.get(k, 0) >= d.val:
                        continue
                    if k not in need or need[k][1] < d.val:
                        need[k] = (d.sem, d.val)
                for k, (s, v) in need.items():
                    eng.wait_ge(s, v)
                    waited[k] = v
                ins = o.fn(eng)
                if o.sig:
                    ins.then_inc(o.sem, o.inc)
            for k, (s, v) in finals[e].items():
                if waited.get(k, 0) < v:
                    eng.wait_ge(s, v)

        with nc.Block() as block:
            if ops["sync"]:
                @block.sync
                def _(eng):
                    replay("sync", eng)
            if ops["gpsimd"]:
                @block.gpsimd
                def _(eng):
                    replay("gpsimd", eng)
            if ops["tensor"]:
                @block.tensor
                def _(eng):
                    replay("tensor", eng)
            if ops["scalar"]:
                @block.scalar
                def _(eng):
                    replay("scalar", eng)
            if ops["vector"]:
                @block.vector
                def _(eng):
                    replay("vector", eng)
        self.ops = None
        self.phase_dma_ops = None


class Tile:
    def __init__(self, t, name, nbuf=1):
        self.t = t
        self.b = Buf(name)


_UID = [0]


def sb(nc, stack, name, shape, dtype):
    _UID[0] += 1
    t = stack.enter_context(nc.sbuf_tensor(f"s{_UID[0]}_{name}", list(shape), dtype))
    return Tile(t, name)


def ps(nc, stack, name, shape, dtype=F32):
    _UID[0] += 1
    t = stack.enter_context(nc.psum_tensor(f"p{_UID[0]}_{name}", list(shape), dtype))
    return Tile(t, name)


def bcast_last(ap, n):
    a = ap.ap
    return bass.AP(ap.tensor, ap.offset, [list(a[0]), list(a[1]), [0, n]])


def ffn_phase(P, nc, name, NT, src, dst, gain_d, wg, wu, wd, consts):
    TT = min(1024, NT)
    n_tt = NT // TT
    NSUB = TT // 128
    P.new_phase(name)
    with ExitStack() as st:
        hT = sb(nc, st, "hT", [128, NKC, TT], BF16)
        act = sb(nc, st, "act", [128, NF, TT], BF16)
        NWS = 3
        wgu = [sb(nc, st, f"wgu{i}", [128, 2, NKC, 128], BF16) for i in range(NWS)]
        FG = 4
        NDS = 3
        wds = [sb(nc, st, f"wds{i}", [128, FG, 512], BF16) for i in range(NDS)]
        xs = [sb(nc, st, f"xs{i}", [128, D], F32) for i in range(2)]
        xn = [sb(nc, st, f"xn{i}", [128, D], F32) for i in range(1)]
        sg = [sb(nc, st, f"sg{i}", [128, 512], F32) for i in range(2)]
        xres = [sb(nc, st, f"xres{i}", [128, 512], F32) for i in range(4)]
        ost = [sb(nc, st, f"ost{i}", [128, 512], F32) for i in range(4)]
        ssq = [sb(nc, st, f"ssq{i}", [128, 1], F32) for i in range(2)]
        rstd = [sb(nc, st, f"rstd{i}", [128, 1], F32) for i in range(2)]
        gain = sb(nc, st, "gain", [128, NKC], F32)
        pbank = [ps(nc, st, f"pb{i}", [128, 512]) for i in range(8)]
        ident = consts["ident_f32"]
        mhalf = consts["mhalf"]

        P.dma("sync", lambda e: e.dma_start(out=gain.t[:], in_=gain_d[:, :]), writes=[gain.b], sembuf=gain.b)

        wg_v = wg.rearrange("(c p) n -> p c n", p=128)
        wu_v = wu.rearrange("(c p) n -> p c n", p=128)
        wslot = 0
        dslot = 0
        xslot = 0
        rslot = 0
        for tt in range(n_tt):
            t0 = tt * TT
            for s in range(NSUB):
                r0 = t0 + s * 128
                X = xs[xslot % 2]
                SS = ssq[xslot % 2]
                RS = rstd[xslot % 2]
                XN = xn[0]
                xslot += 1
                P.dma("sync", lambda e, X=X, r0=r0: e.dma_start(out=X.t[:], in_=src[r0:r0 + 128, :]),
                      writes=[X.b], sembuf=X.b)
                P.op("scalar", lambda e, X=X, XN=XN, SS=SS: e.activation(out=XN.t[:], in_=X.t[:], func=AF.Square,
                                                                        accum_out=SS.t[:]),
                     reads=[X.b], writes=[XN.b, SS.b])
                P.op("vector", lambda e, SS=SS: e.tensor_scalar(out=SS.t[:], in0=SS.t[:], scalar1=1.0 / D, scalar2=EPS,
                                                               op0=ALU.mult, op1=ALU.add),
                     reads=[SS.b], writes=[SS.b])
                P.op("gpsimd", lambda e, SS=SS, RS=RS: e.tensor_tensor(out=RS.t[:], in0=SS.t[:], in1=mhalf.t[:], op=ALU.pow),
                     reads=[SS.b, mhalf.b], writes=[RS.b])
                P.op("vector", lambda e, X=X, XN=XN, RS=RS: e.tensor_scalar(out=XN.t[:], in0=X.t[:], scalar1=RS.t[:, 0:1],
                                                                           scalar2=None, op0=ALU.mult),
                     reads=[X.b, RS.b], writes=[XN.b])
                for cg in range(4):
                    PB = pbank[6 + (cg % 2)]
                    for ci in range(4):
                        c = cg * 4 + ci
                        P.op("tensor", lambda e, PB=PB, XN=XN, c=c, ci=ci: e.transpose(
                            out=PB.t[:, ci * 128:(ci + 1) * 128], in_=XN.t[:, c * 128:(c + 1) * 128], identity=ident.t[:]),
                            reads=[XN.b, ident.b], writes=[PB.b], same_ok=True)
                    P.op("vector", lambda e, PB=PB, cg=cg, s=s: e.tensor_tensor(
                        out=hT.t[:, cg * 4:(cg + 1) * 4, s * 128:(s + 1) * 128],
                        in0=PB.t[:, :].rearrange("p (c n) -> p c n", c=4),
                        in1=bcast_last(gain.t[:, cg * 4:(cg + 1) * 4], 128), op=ALU.mult),
                        reads=[PB.b, gain.b], writes=[hT.b])
            for f in range(NF):
                W = wgu[wslot % NWS]
                wslot += 1
                P.dma("gpsimd", lambda e, W=W, f=f: e.dma_start(out=W.t[:, 0, :, :], in_=wg_v[:, :, f * 128:(f + 1) * 128]),
                      writes=[W.b], sembuf=W.b)
                P.dma("gpsimd", lambda e, W=W, f=f: e.dma_start(out=W.t[:, 1, :, :], in_=wu_v[:, :, f * 128:(f + 1) * 128]),
                      writes=[W.b], sembuf=W.b)
                for half in range(TT // 512):
                    pset = (f * 2 + half) % 3
                    PG = pbank[2 * pset]
                    PU = pbank[2 * pset + 1]
                    for gi, PBK in ((0, PG), (1, PU)):
                        for k in range(NKC):
                            P.op("tensor", lambda e, PBK=PBK, W=W, gi=gi, k=k, half=half: e.matmul(
                                PBK.t[:, :], lhsT=W.t[:, gi, k, :], rhs=hT.t[:, k, half * 512:(half + 1) * 512],
                                start=(k == 0), stop=(k == NKC - 1)),
                                reads=[W.b, hT.b], writes=[PBK.b], same_ok=True)
                    SG = sg[(f * 2 + half) % 2]
                    P.op("scalar", lambda e, SG=SG, PG=PG: e.activation(out=SG.t[:], in_=PG.t[:, :], func=AF.Silu),
                         reads=[PG.b], writes=[SG.b])
                    P.op("vector", lambda e, SG=SG, PU=PU, f=f, half=half: e.tensor_tensor(
                        out=act.t[:, f, half * 512:(half + 1) * 512], in0=PU.t[:, :], in1=SG.t[:], op=ALU.mult),
                        reads=[SG.b, PU.b], writes=[act.b])
            for n in range(D // 512):
                f = 0
                while f < NF:
                    g = min(FG, NF - f)
                    WD = wds[dslot % NDS]
                    dslot += 1
                    P.dma("gpsimd", lambda e, WD=WD, f=f, g=g, n=n: e.dma_start(
                        out=WD.t[:, 0:g, :],
                        in_=wd[f * 128:(f + g) * 128, n * 512:(n + 1) * 512].rearrange("(g p) n -> p g n", p=128)),
                        writes=[WD.b], sembuf=WD.b)
                    for gi in range(g):
                        ff = f + gi
                        for s in range(NSUB):
                            P.op("tensor", lambda e, s=s, WD=WD, gi=gi, ff=ff: e.matmul(
                                pbank[s].t[:, :], lhsT=act.t[:, ff, s * 128:(s + 1) * 128], rhs=WD.t[:, gi, :],
                                start=(ff == 0), stop=(ff == NF - 1)),
                                reads=[act.b, WD.b], writes=[pbank[s].b], same_ok=True)
                    f += g
                for s in range(NSUB):
                    r0 = t0 + s * 128
                    XR = xres[rslot % 4]
                    OS = ost[rslot % 4]
                    rslot += 1
                    P.dma("sync", lambda e, XR=XR, r0=r0, n=n: e.dma_start(out=XR.t[:], in_=src[r0:r0 + 128, n * 512:(n + 1) * 512]),
                          writes=[XR.b], sembuf=XR.b)
                    P.op("vector", lambda e, OS=OS, XR=XR, s=s: e.scalar_tensor_tensor(
                        out=OS.t[:], in0=pbank[s].t[:, :], scalar=0.5, in1=XR.t[:], op0=ALU.mult, op1=ALU.add),
                        reads=[pbank[s].b, XR.b], writes=[OS.b])
                    P.dma("sync", lambda e, OS=OS, r0=r0, n=n: e.dma_start(out=dst[r0:r0 + 128, n * 512:(n + 1) * 512], in_=OS.t[:]),
                          reads=[OS.b], sembuf=OS.b)
        P.emit()


def norm_transpose(P, nc, src_rows, X, XN, SS, RS, hT, s, gain, consts, pbanks):
    ident = consts["ident_f32"]
    mhalf = consts["mhalf"]
    P.dma("sync", lambda e: e.dma_start(out=X.t[:], in_=src_rows), writes=[X.b], sembuf=X.b)
    P.op("scalar", lambda e: e.activation(out=XN.t[:], in_=X.t[:], func=AF.Square, accum_out=SS.t[:]),
         reads=[X.b], writes=[XN.b, SS.b])
    P.op("vector", lambda e: e.tensor_scalar(out=SS.t[:], in0=SS.t[:], scalar1=1.0 / D, scalar2=EPS,
                                             op0=ALU.mult, op1=ALU.add), reads=[SS.b], writes=[SS.b])
    P.op("gpsimd", lambda e: e.tensor_tensor(out=RS.t[:], in0=SS.t[:], in1=mhalf.t[:], op=ALU.pow),
         reads=[SS.b, mhalf.b], writes=[RS.b])
    P.op("vector", lambda e: e.tensor_scalar(out=XN.t[:], in0=X.t[:], scalar1=RS.t[:, 0:1], scalar2=None, op0=ALU.mult),
         reads=[X.b, RS.b], writes=[XN.b])
    for cg in range(4):
        PB = pbanks[cg % 2]
        for ci in range(4):
            c = cg * 4 + ci
            P.op("tensor", lambda e, PB=PB, c=c, ci=ci: e.transpose(
                out=PB.t[:, ci * 128:(ci + 1) * 128], in_=XN.t[:, c * 128:(c + 1) * 128], identity=ident.t[:]),
                reads=[XN.b, ident.b], writes=[PB.b], same_ok=True)
        P.op("vector", lambda e, PB=PB, cg=cg: e.tensor_tensor(
            out=hT.t[:, cg * 4:(cg + 1) * 4, s * 128:(s + 1) * 128],
            in0=PB.t[:, :].rearrange("p (c n) -> p c n", c=4),
            in1=bcast_last(gain.t[:, cg * 4:(cg + 1) * 4], 128), op=ALU.mult),
            reads=[PB.b, gain.b], writes=[hT.b])


def proj_phase(P, nc, NT, NPREV, src, gain_d, w_in, sc, consts):
    TT = min(1024, NT)
    n_tt = NT // TT
    NSUB = TT // 128
    P.new_phase("proj")
    qscale = 1.0 / float(np.sqrt(128.0))
    with ExitStack() as st:
        hT = sb(nc, st, "hT", [128, NKC, TT], BF16)
        wf = [sb(nc, st, f"wf{i}", [128, NKC, 128], BF16) for i in range(3)]
        wt = [sb(nc, st, f"wt{i}", [128, NKC, 512], BF16) for i in range(2)]
        xs = [sb(nc, st, f"xs{i}", [128, D], F32) for i in range(2)]
        xn = [sb(nc, st, f"xn{i}", [128, D], F32) for i in range(1)]
        ssq = [sb(nc, st, f"ssq{i}", [128, 1], F32) for i in range(2)]
        rstd = [sb(nc, st, f"rstd{i}", [128, 1], F32) for i in range(2)]
        gain = sb(nc, st, "gain", [128, NKC], F32)
        obf = [sb(nc, st, f"obf{i}", [128, 512], BF16) for i in range(4)]
        of32 = [sb(nc, st, f"of32{i}", [128, 512], F32) for i in range(4)]
        pbank = [ps(nc, st, f"pb{i}", [128, 512]) for i in range(8)]
        P.dma("sync", lambda e: e.dma_start(out=gain.t[:], in_=gain_d[:, :]), writes=[gain.b], sembuf=gain.b)
        w_v = w_in.rearrange("(c p) n -> p c n", p=128)
        xslot = 0
        fslot = 0
        tslot = 0
        oslot = 0
        pslot = 0
        tm_blocks = []
        for j in range(2):
            tm_blocks.append((2048 + 512 * j, 512, (lambda r0, j=j: sc["v"][NPREV + r0:NPREV + r0 + 128, 512 * j:512 * (j + 1)]), BF16))
        for j in range(6):
            tm_blocks.append((3072 + 512 * j, 512, (lambda r0, j=j: sc["graw"][3 + r0:3 + r0 + 128, 512 * j:512 * (j + 1)]), F32))
        tm_blocks.append((6144, 16, (lambda r0: sc["ab"][r0:r0 + 128, :]), F32))
        for j in range(2):
            tm_blocks.append((6160 + 512 * j, 512, (lambda r0, j=j: sc["gate"][r0:r0 + 128, 512 * j:512 * (j + 1)]), F32))
        for tt in range(n_tt):
            t0 = tt * TT
            for s in range(NSUB):
                r0 = t0 + s * 128
                i = xslot % 2
                xslot += 1
                norm_transpose(P, nc, src[r0:r0 + 128, :], xs[i], xn[0], ssq[i], rstd[i], hT, s, gain, consts, pbank[6:8])
            for c in range(16):
                W = wf[fslot % 3]
                fslot += 1
                P.dma("gpsimd", lambda e, W=W, c=c: e.dma_start(out=W.t[:], in_=w_v[:, :, c * 128:(c + 1) * 128]),
                      writes=[W.b], sembuf=W.b)
                for half in range(TT // 512):
                    PB = pbank[pslot % 6]
                    pslot += 1
                    for k in range(NKC):
                        P.op("tensor", lambda e, PB=PB, W=W, k=k, half=half: e.matmul(
                            PB.t[:, :], lhsT=W.t[:, k, :], rhs=hT.t[:, k, half * 512:(half + 1) * 512],
                            start=(k == 0), stop=(k == NKC - 1)), reads=[W.b, hT.b], writes=[PB.b], same_ok=True)
                    O = obf[oslot % 4]
                    oslot += 1
                    if c < 8:
                        P.op("scalar", lambda e, O=O, PB=PB: e.activation(out=O.t[:], in_=PB.t[:, :], func=AF.Copy, scale=qscale),
                             reads=[PB.b], writes=[O.b])
                        dstap = sc["qT"][c, :, t0 + half * 512:t0 + (half + 1) * 512]
                    else:
                        P.op("vector", lambda e, O=O, PB=PB: e.tensor_copy(out=O.t[:], in_=PB.t[:, :]),
                             reads=[PB.b], writes=[O.b])
                        dstap = sc["kT"][c - 8, :, NPREV + t0 + half * 512:NPREV + t0 + (half + 1) * 512]
                    P.dma("sync", lambda e, O=O, dstap=dstap: e.dma_start(out=dstap, in_=O.t[:]), reads=[O.b], sembuf=O.b)
            for bi, (c0, ncol, dfn, odt) in enumerate(tm_blocks):
                W = wt[tslot % 2]
                tslot += 1
                P.dma("gpsimd", lambda e, W=W, c0=c0, ncol=ncol: e.dma_start(out=W.t[:, :, 0:ncol], in_=w_v[:, :, c0:c0 + ncol]),
                      writes=[W.b], sembuf=W.b)
                for s in range(NSUB):
                    r0 = t0 + s * 128
                    PB = pbank[pslot % 6]
                    pslot += 1
                    for k in range(NKC):
                        P.op("tensor", lambda e, PB=PB, W=W, k=k, s=s, ncol=ncol: e.matmul(
                            PB.t[:, 0:ncol], lhsT=hT.t[:, k, s * 128:(s + 1) * 128], rhs=W.t[:, k, 0:ncol],
                            start=(k == 0), stop=(k == NKC - 1)), reads=[W.b, hT.b], writes=[PB.b], same_ok=True)
                    O = (obf if odt == BF16 else of32)[oslot % 4]
                    oslot += 1
                    eng = "scalar" if (s % 2 == 0) else "vector"
                    if eng == "scalar":
                        P.op("scalar", lambda e, O=O, PB=PB, ncol=ncol: e.activation(out=O.t[:, 0:ncol], in_=PB.t[:, 0:ncol], func=AF.Copy),
                             reads=[PB.b], writes=[O.b])
                    else:
                        P.op("vector", lambda e, O=O, PB=PB, ncol=ncol: e.tensor_copy(out=O.t[:, 0:ncol], in_=PB.t[:, 0:ncol]),
                             reads=[PB.b], writes=[O.b])
                    P.dma("sync", lambda e, O=O, dstap=dfn(r0), ncol=ncol: e.dma_start(out=dstap, in_=O.t[:, 0:ncol]),
                          reads=[O.b], sembuf=O.b)
        P.emit()


def attn_phase(P, nc, NT, NPREV, sc, cin, consts, xg=None):
    NK = NPREV + NT
    NKB = NK // 128
    NPB = NPREV // 128
    NG = NT // 512
    P.new_phase("attn")
    with ExitStack() as st:
        NHB = 4
        KT = [sb(nc, st, f"KT{i}", [128, NK], BF16) for i in range(NHB)]
        VV = [sb(nc, st, f"VV{i}", [128, NKB, 128], BF16) for i in range(NHB)]
        QT = [sb(nc, st, f"QT{i}", [128, NT], BF16) for i in range(NHB)]
        trineg = sb(nc, st, "trineg", [128, 128], BF16)
        tricomp = sb(nc, st, "tricomp", [128, 128], BF16)
        ones32 = sb(nc, st, "ones32", [128, 128], F32)
        masks = sb(nc, st, "masks", [128, 4, 512], F32)
        negb = sb(nc, st, "negb", [128, 1], F32)
        gsb = sb(nc, st, "gsb", [128, 1], F32)
        NS = 2
        E = [[sb(nc, st, f"E{s_}_{i}", [128, 512], F32) for i in range(3)] for s_ in range(NS)]
        SP = [[sb(nc, st, f"SP{s_}_{i}", [128, 512], BF16) for i in range(2)] for s_ in range(NS)]
        EC = [[sb(nc, st, f"EC{s_}_{i}", [128, 512], F32) for i in range(2)] for s_ in range(NS)]
        W = [[sb(nc, st, f"W{s_}_{i}", [128, 512], BF16) for i in range(3)] for s_ in range(NS)]
        SQ = [sb(nc, st, f"SQ{s_}", [128, 512], F32) for s_ in range(NS)]
        R = [sb(nc, st, f"R{s_}", [128, 512], F32) for s_ in range(NS)]
        Y = [[sb(nc, st, f"Y{s_}_{i}", [128, 512], BF16) for i in range(2)] for s_ in range(NS)]
        Zp = [[ps(nc, st, f"Zp{s_}_{i}", [128, 512]) for i in range(2)] for s_ in range(NS)]
        Cp = [ps(nc, st, f"Cp{s_}", [128, 512]) for s_ in range(NS)]
        OT = [ps(nc, st, f"OT{s_}", [128, 512]) for s_ in range(NS)]

        P.dma("gpsimd", lambda e: e.dma_start(out=trineg.t[:], in_=cin["trineg"][:, :]), writes=[trineg.b], sembuf=trineg.b)
        P.dma("gpsimd", lambda e: e.dma_start(out=tricomp.t[:], in_=cin["tricomp"][:, :]), writes=[tricomp.b], sembuf=tricomp.b)
        P.dma("sync", lambda e: e.dma_start(out=ones32.t[:], in_=cin["ones32"][:, :]), writes=[ones32.b], sembuf=ones32.b)
        P.dma("sync", lambda e: e.dma_start(out=masks.t[:], in_=cin["dmask"].rearrange("r p n -> p r n")), writes=[masks.b], sembuf=masks.b)
        P.dma("sync", lambda e: e.dma_start(out=negb.t[:], in_=cin["negbias"][:, :]), writes=[negb.b], sembuf=negb.b)
        P.dma("sync", lambda e: e.dma_start(out=gsb.t[:], in_=cin["sb_out_norm"][:, :]), writes=[gsb.b], sembuf=gsb.b)

        def load_head(h):
            K_, V_, Q_ = KT[h % NHB], VV[h % NHB], QT[h % NHB]
            if xg is None:
                P.dma("sync", lambda e: e.dma_start(out=K_.t[:], in_=sc["kT"][h, :, :]), writes=[K_.b], sembuf=K_.b)
                P.dma("sync", lambda e: e.dma_start(
                    out=V_.t[:], in_=sc["v"][:, h * 128:(h + 1) * 128].rearrange("(b s) d -> s b d", s=128)),
                    writes=[V_.b], sembuf=V_.b)
            else:
                P.dma("sync", lambda e: e.dma_start(out=K_.t[:, 0:NPREV], in_=xg["kd"][h // 4][(h % 4) * 128:(h % 4 + 1) * 128, :]), writes=[K_.b], sembuf=K_.b)
                P.dma("sync", lambda e: e.dma_start(out=K_.t[:, NPREV:NK], in_=sc["kT"][h, :, :]), writes=[K_.b], sembuf=K_.b)
                HVB = xg["HV"] // 128
                for i in range(2):
                    P.dma("sync", lambda e, i=i: e.dma_start(
                        out=V_.t[:, i * HVB:(i + 1) * HVB, :], in_=xg["vd"][i][0:xg["HV"], h * 128:(h + 1) * 128].rearrange("(b s) d -> s b d", s=128)),
                        writes=[V_.b], sembuf=V_.b)
                P.dma("sync", lambda e: e.dma_start(
                    out=V_.t[:, NPB:NKB, :], in_=sc["v"][:, h * 128:(h + 1) * 128].rearrange("(b s) d -> s b d", s=128)),
                    writes=[V_.b], sembuf=V_.b)
            P.dma("sync", lambda e: e.dma_start(out=Q_.t[:], in_=sc["qT"][h, :, :]), writes=[Q_.b], sembuf=Q_.b)

        class Stream:
            pass

        def mk_stream(sid, h, G):
            S_ = Stream()
            S_.sid, S_.h, S_.G = sid, h, G
            S_.K, S_.V, S_.Q = KT[h % NHB], VV[h % NHB], QT[h % NHB]
            S_.g0 = G * 512
            steps = []
            for r in (3, 2, 1, 0):
                steps.append((NPB + G * 4 + r, r, False))
            for m in range(G * 4 - 1, -1, -1):
                steps.append((NPB + m, None, False))
            for m in range(NPB - 1, -1, -1):
                steps.append((m, None, True))
            S_.steps = steps
            S_.ns = len(steps)
            S_.bufs = {}
            S_.cz = S_.ce = S_.csp = S_.cec = S_.cw = 0
            return S_

        def S1z(S_, i):
            kb, r, isprev = S_.steps[i]
            sid = S_.sid
            Z = Zp[sid][S_.cz % 2]; S_.cz += 1
            K_, Q_, g0 = S_.K, S_.Q, S_.g0
            P.op("tensor", lambda e: e.matmul(Z.t[:, :], lhsT=K_.t[:, kb * 128:(kb + 1) * 128], rhs=Q_.t[:, g0:g0 + 512],
                                              start=True, stop=True), reads=[K_.b, Q_.b], writes=[Z.b], same_ok=True)
            S_.bufs[i] = [None, None, None, Z]

        def S1e(S_, i):
            kb, r, isprev = S_.steps[i]
            sid = S_.sid
            Z = S_.bufs[i][3]
            Ei = E[sid][S_.ce % 3]; S_.ce += 1
            if isprev:
                P.op("scalar", lambda e: e.activation(out=Ei.t[:], in_=Z.t[:, :], func=AF.Exp, bias=negb.t[:, 0:1]),
                     reads=[Z.b, negb.b], writes=[Ei.b])
            else:
                P.op("scalar", lambda e: e.activation(out=Ei.t[:], in_=Z.t[:, :], func=AF.Exp), reads=[Z.b], writes=[Ei.b])
            if r is not None:
                P.op("vector", lambda e: e.tensor_tensor(out=Ei.t[:], in0=Ei.t[:], in1=masks.t[:, r, :], op=ALU.mult),
                     reads=[Ei.b, masks.b], writes=[Ei.b])
            S_.bufs[i][0] = Ei

        def S1sp(S_, i):
            sid = S_.sid
            Ei = S_.bufs[i][0]
            SPi = SP[sid][S_.csp % 2]; S_.csp += 1
            P.op("scalar", lambda e: e.activation(out=SPi.t[:], in_=Ei.t[:], func=AF.Ln, bias=1.0),
                 reads=[Ei.b], writes=[SPi.b])
            S_.bufs[i][1] = SPi

        def S2a(S_, i, part):
            Ei, SPi = S_.bufs[i][0], S_.bufs[i][1]
            sid = S_.sid
            C = Cp[sid]
            if part == 0:
                ECi = EC[sid][S_.cec % 2]; S_.cec += 1
                Wi = W[sid][S_.cw % 3]; S_.cw += 1
            last = (i == S_.ns - 1)
            if part == 0:
                P.op("tensor", lambda e: e.matmul(C.t[:, :], lhsT=trineg.t[:], rhs=SPi.t[:], start=(i == 0), stop=last),
                     reads=[trineg.b, SPi.b], writes=[C.b], same_ok=True)
                S_.bufs[i].append((ECi, Wi))
                return
            ECi, Wi = S_.bufs[i][4]
            P.op("scalar", lambda e: e.activation(out=ECi.t[:], in_=C.t[:, :], func=AF.Exp), reads=[C.b], writes=[ECi.b])
            P.op("vector", lambda e: e.tensor_tensor(out=Wi.t[:], in0=Ei.t[:], in1=ECi.t[:], op=ALU.mult),
                 reads=[Ei.b, ECi.b], writes=[Wi.b])
            S_.bufs[i][2] = Wi

        def S2b(S_, i):
            if i == S_.ns - 1:
                return
            Ei, SPi = S_.bufs[i][0], S_.bufs[i][1]
            C = Cp[S_.sid]
            P.op("tensor", lambda e: e.matmul(C.t[:, :], lhsT=tricomp.t[:], rhs=SPi.t[:], start=False, stop=False),
                 reads=[tricomp.b, SPi.b], writes=[C.b], same_ok=True)

        def S3(S_, i):
            kb, r, isprev = S_.steps[i]
            Wi = S_.bufs[i][2]
            OTg, V_, ns = OT[S_.sid], S_.V, S_.ns
            P.op("tensor", lambda e: e.matmul(OTg.t[:, :], lhsT=V_.t[:, kb, :], rhs=Wi.t[:], start=(i == 0), stop=(i == ns - 1)),
                 reads=[V_.b, Wi.b], writes=[OTg.b], same_ok=True)
            del S_.bufs[i]

        def finish(S_):
            sid, h, g0, G = S_.sid, S_.h, S_.g0, S_.G
            OTg = OT[sid]
            Yg = Y[sid][G % 2]
            Zs = Zp[sid][S_.cz % 2]; S_.cz += 1
            SQ_, R_ = SQ[sid], R[sid]
            P.op("scalar", lambda e: e.activation(out=SQ_.t[:], in_=OTg.t[:, :], func=AF.Square), reads=[OTg.b], writes=[SQ_.b])
            P.op("tensor", lambda e: e.matmul(Zs.t[:, :], lhsT=ones32.t[:], rhs=SQ_.t[:], start=True, stop=True),
                 reads=[ones32.b, SQ_.b], writes=[Zs.b], same_ok=True)
            P.op("vector", lambda e: e.tensor_scalar(out=R_.t[:], in0=Zs.t[:, :], scalar1=1.0 / 128.0, scalar2=EPS,
                                                     op0=ALU.mult, op1=ALU.add), reads=[Zs.b], writes=[R_.b])
            P.op("scalar", lambda e: e.activation(out=R_.t[:], in_=R_.t[:], func=AF.Ln), reads=[R_.b], writes=[R_.b])
            P.op("scalar", lambda e: e.activation(out=R_.t[:], in_=R_.t[:], func=AF.Exp, scale=-0.5), reads=[R_.b], writes=[R_.b])
            P.op("vector", lambda e: e.scalar_tensor_tensor(
                out=Yg.t[:], in0=OTg.t[:, :], scalar=gsb.t[:, 0:1], in1=R_.t[:], op0=ALU.mult, op1=ALU.mult),
                reads=[OTg.b, gsb.b, R_.b], writes=[Yg.b])
            P.dma("sync", lambda e: e.dma_start(out=sc["yT"][h, :, g0:g0 + 512], in_=Yg.t[:]), reads=[Yg.b], sembuf=Yg.b)

        load_head(0)
        load_head(1)
        for hp in range(4):
            if hp + 1 < 4:
                load_head(2 * hp + 2)
                load_head(2 * hp + 3)
            for G in range(NG):
                strs = [mk_stream(0, 2 * hp, G), mk_stream(1, 2 * hp + 1, G)]
                ns = strs[0].ns
                for S_ in strs:
                    S1z(S_, 0)
                if ns > 1:
                    for S_ in strs:
                        S1z(S_, 1)
                for S_ in strs:
                    S1e(S_, 0)
                for i in range(ns):
                    if i + 2 < ns:
                        for S_ in strs:
                            S1z(S_, i + 2)
                    if i >= 1:
                        for S_ in strs:
                            S3(S_, i - 1)
                    for S_ in strs:
                        S1sp(S_, i)
                    for S_ in strs:
                        S2a(S_, i, 0)
                    if i + 1 < ns:
                        for S_ in strs:
                            S1e(S_, i + 1)
                    for S_ in strs:
                        S2a(S_, i, 1)
                    for S_ in strs:
                        S2b(S_, i)
                for S_ in strs:
                    S3(S_, ns - 1)
                for S_ in strs:
                    finish(S_)
        P.emit()


FP32R = False


def mm32(e, out, lhsT, rhs, **kw):
    if FP32R and lhsT.dtype == F32 and rhs.dtype == F32:
        lhsT = lhsT.bitcast(mybir.dt.float32r)
        rhs = rhs.bitcast(mybir.dt.float32r)
    return e.matmul(out, lhsT=lhsT, rhs=rhs, **kw)


class PQ:
    def __init__(self, bank, q, buf):
        self.bank = bank
        self.q = q
        self.b = buf

    def ap(self, rows=128, cols=128):
        return self.bank[0:rows, self.q * 128:self.q * 128 + cols]


def gdn_prep_phase(P, nc, NT, sc, cin, consts, egl_all, xg=None):
    NTILE = NT // 128
    P.new_phase("gprep")
    ident = consts["ident_f32"]
    with ExitStack() as st:
        convw = sb(nc, st, "convw", [128, 4, 3072], F32)
        alog = sb(nc, st, "alog", [128, 8], F32)
        dtb = sb(nc, st, "dtb", [128, 8], F32)
        nega = sb(nc, st, "nega", [128, 8], F32)
        BT = sb(nc, st, "BT", [128, 128], F32)
        BL = sb(nc, st, "BL", [128, 128], F32)
        selC = sb(nc, st, "selC", [128, 2, 128], F32)
        zero3 = sb(nc, st, "zero3", [3, 3072], F32)
        XJ = [[sb(nc, st, f"XJ{i}_{j}", [128, 1024], F32) for j in range(4)] for i in range(2)]
        acc = sb(nc, st, "acc", [128, 1024], F32)
        tmp = sb(nc, st, "tmp", [128, 1024], F32)
        tmp2 = sb(nc, st, "tmp2", [128, 1024], F32)
        acc2 = sb(nc, st, "acc2", [128, 1024], F32)
        qn2 = [sb(nc, st, f"qn{i}", [128, 8, 128], F32) for i in range(2)]
        kn2 = [sb(nc, st, f"kn{i}", [128, 8, 128], F32) for i in range(2)]
        vs2 = [sb(nc, st, f"vs{i}", [128, 8, 128], F32) for i in range(2)]
        qg = sb(nc, st, "qg", [128, 8, 128], F32)
        kbg = sb(nc, st, "kbg", [128, 8, 128], BF16)
        vb = sb(nc, st, "vb", [128, 8, 128], BF16)
        kT = sb(nc, st, "kT", [128, 8, 128], F32)
        qT = sb(nc, st, "qT", [128, 8, 128], F32)
        qgT = sb(nc, st, "qgT", [128, 8, 128], BF16)
        A_all = sb(nc, st, "A_all", [128, 8, 128], F32)
        aqk_all = sb(nc, st, "aqk_all", [64, 2, 8, 64], BF16)
        kend_all = sb(nc, st, "kend_all", [64, 2, 8, 128], BF16)
        DmS = [sb(nc, st, f"DmS{i}", [128, 128], F32) for i in range(3)]
        DmT = [sb(nc, st, f"DmT{i}", [64, 64], F32) for i in range(3)]
        Gbb = sb(nc, st, "Gbb", [128, 8, 128], F32)
        MS01 = sb(nc, st, "MS01", [128, 128], F32)
        ML01 = sb(nc, st, "ML01", [64, 64], F32)
        gcrow_buf = Buf("gcrow")
        abt = sb(nc, st, "abt", [128, 16], F32)
        sm = {n: sb(nc, st, n, [128, 8], F32) for n in ("g", "beta", "gc", "eg", "ekend", "bke", "ssq", "rq", "rk", "t8")}
        smC = {n: sb(nc, st, n, [64, 2, 8], F32) for n in ("ngcC", "ekendC", "tC")}
        gcT = sb(nc, st, "gcT", [8, 128], F32)
        mh8 = sb(nc, st, "mh8", [128, 8], F32)
        banks = [st.enter_context(nc.psum_tensor(f"gp_bank{i}", [128, 512], F32)) for i in range(8)]
        bankbufs = [Buf(f"bank{i}") for i in range(8)]
        pq = [PQ(banks[i % 8], i // 8, bankbufs[i % 8]) for i in range(32)]
        pqi = [0]

        def nextpq():
            p = pq[pqi[0] % 32]
            pqi[0] += 1
            return p

        ld = lambda t, src, eng="sync": P.dma(eng, lambda e: e.dma_start(out=t.t[:], in_=src), writes=[t.b], sembuf=t.b)
        ld(convw, cin["convw_b"].rearrange("p (j c) -> p j c", j=4))
        ld(alog, cin["alog_b"][:, :])
        ld(dtb, cin["dtb_b"][:, :])
        ld(BT, cin["BT"][:, :])
        ld(BL, cin["BL"][:, :])
        ld(selC, cin["selC"].rearrange("p (c n) -> p c n", c=2))
        ld(MS01, cin["MS01"][:, :])
        ld(ML01, cin["ML01"][:, :])
        if xg is None:
            P.op("vector", lambda e: e.memset(zero3.t[:], 0.0), writes=[zero3.b])
        P.op("vector", lambda e: e.memset(mh8.t[:], -0.5), writes=[mh8.b])
        hist = Buf("hist")
        if xg is None:
            P.dma("sync", lambda e: e.dma_start(out=sc["graw"][0:3, :], in_=zero3.t[:]), reads=[zero3.b], writes=[hist], sembuf=zero3.b)
        else:
            flag3 = sb(nc, st, "flag3", [3, 1], F32)
            P.dma("sync", lambda e: e.dma_start(out=flag3.t[:], in_=cin["flag01"][0:3, :]), writes=[flag3.b], sembuf=flag3.b)
            P.dma("sync", lambda e: e.dma_start(out=zero3.t[:], in_=xg["hist_dst"][0:3, :]), writes=[zero3.b], sembuf=zero3.b)
            P.op("vector", lambda e: e.tensor_scalar(out=zero3.t[:], in0=zero3.t[:], scalar1=flag3.t[:, 0:1], scalar2=None, op0=ALU.mult),
                 reads=[zero3.b, flag3.b], writes=[zero3.b])
            P.dma("sync", lambda e: e.dma_start(out=sc["graw"][0:3, :], in_=zero3.t[:]), reads=[zero3.b], writes=[hist], sembuf=zero3.b)
        P.op("scalar", lambda e: e.activation(out=nega.t[:], in_=alog.t[:], func=AF.Exp), reads=[alog.b], writes=[nega.b])
        P.op("vector", lambda e: e.tensor_scalar(out=nega.t[:], in0=nega.t[:], scalar1=-1.0, scalar2=None, op0=ALU.mult),
             reads=[nega.b], writes=[nega.b])
        qs = 1.0 / float(np.sqrt(128.0))
        P.emit_barrier_needed = True
        g, beta, gc, eg, ekend, bke, t8 = (sm[n] for n in ("g", "beta", "gc", "eg", "ekend", "bke", "t8"))
        ngcC, ekendC, tC = smC["ngcC"], smC["ekendC"], smC["tC"]

        def s1(tb, third, dstt):
            r0 = tb * 128
            X = XJ[(tb * 3 + third) % 2]
            c0 = third * 1024
            for j in range(4):
                P.dma("sync", lambda e, X=X, j=j, r0=r0, c0=c0: e.dma_start(out=X[j].t[:], in_=sc["graw"][r0 + j:r0 + j + 128, c0:c0 + 1024]),
                      reads=([hist] if tb == 0 else []), writes=[X[j].b], sembuf=X[j].b)
            P.op("gpsimd", lambda e, X=X, c0=c0: e.tensor_tensor(out=tmp.t[:], in0=X[1].t[:], in1=convw.t[:, 1, c0:c0 + 1024], op=ALU.mult),
                 reads=[X[1].b, convw.b], writes=[tmp.b])
            P.op("gpsimd", lambda e, X=X, c0=c0: e.tensor_tensor(out=tmp2.t[:], in0=X[2].t[:], in1=convw.t[:, 2, c0:c0 + 1024], op=ALU.mult),
                 reads=[X[2].b, convw.b], writes=[tmp2.b])
            P.op("vector", lambda e, X=X, c0=c0: e.tensor_tensor(out=acc.t[:], in0=X[0].t[:], in1=convw.t[:, 0, c0:c0 + 1024], op=ALU.mult),
                 reads=[X[0].b, convw.b], writes=[acc.b])
            P.op("vector", lambda e, X=X, c0=c0: e.tensor_tensor(out=acc2.t[:], in0=X[3].t[:], in1=convw.t[:, 3, c0:c0 + 1024], op=ALU.mult),
                 reads=[X[3].b, convw.b], writes=[acc2.b])
            P.op("vector", lambda e: e.tensor_tensor(out=acc.t[:], in0=acc.t[:], in1=acc2.t[:], op=ALU.add), reads=[acc.b, acc2.b], writes=[acc.b])
            P.op("vector", lambda e: e.tensor_tensor(out=acc.t[:], in0=acc.t[:], in1=tmp.t[:], op=ALU.add), reads=[acc.b, tmp.b], writes=[acc.b])
            P.op("vector", lambda e: e.tensor_tensor(out=acc.t[:], in0=acc.t[:], in1=tmp2.t[:], op=ALU.add), reads=[acc.b, tmp2.b], writes=[acc.b])
            dflat = dstt.t[:].rearrange("p h d -> p (h d)")
            P.op("scalar", lambda e, dflat=dflat: e.activation(out=dflat, in_=acc.t[:], func=AF.Silu), reads=[acc.b], writes=[dstt.b])
            if third < 2:
                rr = sm["rq"] if third == 0 else sm["rk"]
                P.op("gpsimd", lambda e, dflat=dflat: e.tensor_tensor(out=tmp.t[:], in0=dflat, in1=dflat, op=ALU.mult),
                     reads=[dstt.b], writes=[tmp.b])
                P.op("vector", lambda e: e.tensor_reduce(out=sm["ssq"].t[:], in_=tmp.t[:].rearrange("p (h d) -> p h d", h=8),
                                                         axis=AX.X, op=ALU.add), reads=[tmp.b], writes=[sm["ssq"].b])
                P.op("vector", lambda e: e.tensor_scalar(out=sm["ssq"].t[:], in0=sm["ssq"].t[:], scalar1=EPS, scalar2=None, op0=ALU.add),
                     reads=[sm["ssq"].b], writes=[sm["ssq"].b])
                P.op("gpsimd", lambda e, rr=rr: e.tensor_tensor(out=rr.t[:], in0=sm["ssq"].t[:], in1=mh8.t[:], op=ALU.pow),
                     reads=[sm["ssq"].b, mh8.b], writes=[rr.b])
                if third == 0:
                    P.op("vector", lambda e, rr=rr: e.tensor_scalar(out=rr.t[:], in0=rr.t[:], scalar1=qs, scalar2=None, op0=ALU.mult),
                         reads=[rr.b], writes=[rr.b])
                P.op("vector", lambda e, dstt=dstt, rr=rr: e.tensor_tensor(out=dstt.t[:], in0=dstt.t[:], in1=bcast_last(rr.t[:, :], 128), op=ALU.mult),
                     reads=[dstt.b, rr.b], writes=[dstt.b])

        def s2a(tb, qn, kn, vs):
            r0 = tb * 128
            P.dma("sync", lambda e, r0=r0: e.dma_start(out=abt.t[:], in_=sc["ab"][r0:r0 + 128, :]), writes=[abt.b], sembuf=abt.b)
            g, beta, gc, eg, ekend, bke, t8 = (sm[n] for n in ("g", "beta", "gc", "eg", "ekend", "bke", "t8"))
            P.op("vector", lambda e: e.tensor_tensor(out=t8.t[:], in0=abt.t[:, 0:8], in1=dtb.t[:], op=ALU.add), reads=[abt.b, dtb.b], writes=[t8.b])
            P.op("scalar", lambda e: e.activation(out=t8.t[:], in_=t8.t[:], func=AF.Exp), reads=[t8.b], writes=[t8.b])
            P.op("scalar", lambda e: e.activation(out=t8.t[:], in_=t8.t[:], func=AF.Ln, bias=1.0), reads=[t8.b], writes=[t8.b])
            P.op("vector", lambda e: e.tensor_tensor(out=g.t[:], in0=t8.t[:], in1=nega.t[:], op=ALU.mult), reads=[t8.b, nega.b], writes=[g.b])
            P.op("scalar", lambda e: e.activation(out=beta.t[:], in_=abt.t[:, 8:16], func=AF.Exp, scale=-1.0), reads=[abt.b], writes=[beta.b])
            P.op("vector", lambda e: e.tensor_scalar(out=beta.t[:], in0=beta.t[:], scalar1=1.0, scalar2=None, op0=ALU.add), reads=[beta.b], writes=[beta.b])
            P.op("vector", lambda e: e.reciprocal(out=beta.t[:], in_=beta.t[:]), reads=[beta.b], writes=[beta.b])
            p_gc, p_gl, p_gcT = nextpq(), nextpq(), nextpq()
            P.op("tensor", lambda e, p=p_gc: mm32(e, p.ap(128, 8), lhsT=BT.t[:], rhs=g.t[:], start=True, stop=True), reads=[BT.b, g.b], writes=[p_gc.b], same_ok=True)
            P.op("tensor", lambda e, p=p_gl: mm32(e, p.ap(128, 8), lhsT=BL.t[:], rhs=g.t[:], start=True, stop=True), reads=[BL.b, g.b], writes=[p_gl.b], same_ok=True)
            P.op("tensor", lambda e, p=p_gcT: mm32(e, p.ap(8, 128), lhsT=g.t[:], rhs=BT.t[:], start=True, stop=True), reads=[BT.b, g.b], writes=[p_gcT.b], same_ok=True)
            P.op("vector", lambda e, p=p_gc: e.tensor_copy(out=gc.t[:], in_=p.ap(128, 8)), reads=[p_gc.b], writes=[gc.b])
            P.op("scalar", lambda e, p=p_gc: e.activation(out=eg.t[:], in_=p.ap(128, 8), func=AF.Exp), reads=[p_gc.b], writes=[eg.b])
            P.op("vector", lambda e, p=p_gl: e.tensor_tensor(out=ekend.t[:], in0=p.ap(128, 8), in1=gc.t[:], op=ALU.subtract), reads=[p_gl.b, gc.b], writes=[ekend.b])
            P.op("scalar", lambda e: e.activation(out=ekend.t[:], in_=ekend.t[:], func=AF.Exp), reads=[ekend.b], writes=[ekend.b])
            P.op("vector", lambda e, p=p_gcT: e.tensor_copy(out=gcT.t[:], in_=p.ap(8, 128)), reads=[p_gcT.b], writes=[gcT.b])
            P.dma("sync", lambda e, tb=tb: e.dma_start(out=sc["gcrow"][tb], in_=gcT.t[:]), reads=[gcT.b], writes=[gcrow_buf], sembuf=gcrow_buf)

            def ldb(e, tb=tb):
                src = sc["gcrow"][tb]
                bsrc = bass.AP(src.tensor, src.offset, [[0, 128], [128, 8], [1, 128]])
                return e.dma_start(out=Gbb.t[:], in_=bsrc)
            P.dma("sync", ldb, reads=[gcrow_buf], writes=[Gbb.b], sembuf=Gbb.b)
            P.op("vector", lambda e: e.tensor_tensor(out=bke.t[:], in0=beta.t[:], in1=eg.t[:], op=ALU.mult), reads=[beta.b, eg.b], writes=[bke.b])
            ngcC, ekendC, tC = smC["ngcC"], smC["ekendC"], smC["tC"]
            for c in range(2):
                pc1, pc2, pc3 = nextpq(), nextpq(), nextpq()
                P.op("tensor", lambda e, p=pc1, c=c: mm32(e, p.ap(64, 8), lhsT=BT.t[:, c * 64:(c + 1) * 64], rhs=g.t[:], start=True, stop=True),
                     reads=[BT.b, g.b], writes=[pc1.b], same_ok=True)
                P.op("tensor", lambda e, p=pc2, c=c: mm32(e, p.ap(64, 8), lhsT=BL.t[:, c * 64:(c + 1) * 64], rhs=g.t[:], start=True, stop=True),
                     reads=[BL.b, g.b], writes=[pc2.b], same_ok=True)
                P.op("tensor", lambda e, p=pc3, c=c: mm32(e, p.ap(128, 8), lhsT=selC.t[:, c, :], rhs=g.t[:], start=True, stop=True),
                     reads=[selC.b, g.b], writes=[pc3.b], same_ok=True)
                P.op("vector", lambda e, p=pc1, c=c: e.tensor_scalar(out=ngcC.t[:, c, :], in0=p.ap(64, 8), scalar1=-1.0, scalar2=None, op0=ALU.mult),
                     reads=[pc1.b], writes=[ngcC.b])
                P.op("vector", lambda e, p=pc2, c=c: e.tensor_tensor(out=tC.t[:, c, :], in0=p.ap(64, 8), in1=ngcC.t[:, c, :], op=ALU.add),
                     reads=[pc2.b, ngcC.b], writes=[tC.b])
                P.op("scalar", lambda e, c=c: e.activation(out=ekendC.t[:, c, :], in_=tC.t[:, c, :], func=AF.Exp), reads=[tC.b], writes=[ekendC.b])
                P.op("scalar", lambda e, p=pc3, c=c, tb=tb: e.activation(out=egl_all.t[:, tb * 2 + c, :], in_=p.ap(128, 8), func=AF.Exp),
                     reads=[pc3.b], writes=[egl_all.b])
            P.op("vector", lambda e: e.tensor_tensor(out=kbg.t[:], in0=kn.t[:], in1=bcast_last(bke.t[:, :], 128), op=ALU.mult), reads=[kn.b, bke.b], writes=[kbg.b])
            P.op("gpsimd", lambda e: e.tensor_tensor(out=vb.t[:], in0=vs.t[:], in1=bcast_last(beta.t[:, :], 128), op=ALU.mult), reads=[vs.b, beta.b], writes=[vb.b])
            P.op("vector", lambda e: e.tensor_tensor(out=qg.t[:], in0=qn.t[:], in1=bcast_last(eg.t[:, :], 128), op=ALU.mult), reads=[qn.b, eg.b], writes=[qg.b])
            P.dma("sync", lambda e, r0=r0: e.dma_start(out=sc["kbg"][r0:r0 + 128, :], in_=kbg.t[:].rearrange("p h d -> p (h d)")), reads=[kbg.b], sembuf=kbg.b)
            P.dma("sync", lambda e, r0=r0: e.dma_start(out=sc["vb"][r0:r0 + 128, :], in_=vb.t[:].rearrange("p h d -> p (h d)")), reads=[vb.b], sembuf=vb.b)
            for srct, dstT in ((kn, kT), (qn, qT), (qg, qgT)):
                for h in range(8):
                    p = nextpq()
                    P.op("tensor", lambda e, p=p, srct=srct, h=h: e.transpose(out=p.ap(), in_=srct.t[:, h, :], identity=ident.t[:]),
                         reads=[srct.b, ident.b], writes=[p.b], same_ok=True)
                    eng = "scalar" if h % 2 == 0 else "vector"
                    if eng == "scalar":
                        P.op("scalar", lambda e, p=p, dstT=dstT, h=h: e.activation(out=dstT.t[:, h, :], in_=p.ap(), func=AF.Copy), reads=[p.b], writes=[dstT.b])
                    else:
                        P.op("vector", lambda e, p=p, dstT=dstT, h=h: e.tensor_copy(out=dstT.t[:, h, :], in_=p.ap()), reads=[p.b], writes=[dstT.b])
            P.dma("sync", lambda e, tb=tb: e.dma_start(out=sc["qgT"][tb], in_=qgT.t[:]), reads=[qgT.b], sembuf=qgT.b)

        def s2h(tb, h, qn, kn, vs):
            r0 = tb * 128
            pkk = nextpq()
            P.op("tensor", lambda e, p=pkk, h=h: mm32(e, p.ap(), lhsT=kT.t[:, h, :], rhs=kT.t[:, h, :], start=True, stop=True),
                 reads=[kT.b], writes=[pkk.b], same_ok=True)
            DS = DmS[h % 3]
            P.op("vector", lambda e, DS=DS, h=h: e.tensor_scalar(out=DS.t[:], in0=Gbb.t[:, h, :], scalar1=gc.t[:, h:h + 1], scalar2=0.0,
                                                                op0=ALU.subtract, op1=ALU.max),
                 reads=[Gbb.b, gc.b], writes=[DS.b])
            P.op("scalar", lambda e, DS=DS: e.activation(out=DS.t[:], in_=DS.t[:], func=AF.Exp, scale=-1.0), reads=[DS.b], writes=[DS.b])
            P.op("gpsimd", lambda e, DS=DS: e.tensor_tensor(out=DS.t[:], in0=DS.t[:], in1=MS01.t[:], op=ALU.mult), reads=[DS.b, MS01.b], writes=[DS.b])
            P.op("vector", lambda e, p=pkk, DS=DS, h=h: e.scalar_tensor_tensor(out=A_all.t[:, h, :], in0=p.ap(), scalar=beta.t[:, h:h + 1], in1=DS.t[:],
                                                                             op0=ALU.mult, op1=ALU.mult),
                 reads=[pkk.b, DS.b, beta.b], writes=[A_all.b])
            for c in range(2):
                pkq, pkc = nextpq(), nextpq()
                cs = slice(c * 64, (c + 1) * 64)
                P.op("tensor", lambda e, p=pkq, h=h, cs=cs: mm32(e, p.ap(64, 64), lhsT=kT.t[:, h, cs], rhs=qT.t[:, h, cs], start=True, stop=True),
                     reads=[kT.b, qT.b], writes=[pkq.b], same_ok=True)
                DT = DmT[(h * 2 + c) % 3]
                P.op("vector", lambda e, DT=DT, h=h, c=c, cs=cs: e.tensor_scalar(out=DT.t[:], in0=Gbb.t[0:64, h, cs], scalar1=ngcC.t[:, c, h:h + 1], scalar2=0.0,
                                                                                op0=ALU.add, op1=ALU.min),
                     reads=[Gbb.b, ngcC.b], writes=[DT.b])
                P.op("scalar", lambda e, DT=DT: e.activation(out=DT.t[:], in_=DT.t[:], func=AF.Exp), reads=[DT.b], writes=[DT.b])
                P.op("gpsimd", lambda e, DT=DT: e.tensor_tensor(out=DT.t[:], in0=DT.t[:], in1=ML01.t[:], op=ALU.mult), reads=[DT.b, ML01.b], writes=[DT.b])
                P.op("vector", lambda e, p=pkq, DT=DT, h=h, c=c: e.tensor_tensor(out=aqk_all.t[:, c, h, :], in0=p.ap(64, 64), in1=DT.t[:], op=ALU.mult),
                     reads=[pkq.b, DT.b], writes=[aqk_all.b])
                P.op("tensor", lambda e, p=pkc, h=h, cs=cs: e.transpose(out=p.ap(64, 128), in_=kT.t[:, h, cs], identity=ident.t[:]),
                     reads=[kT.b, ident.b], writes=[pkc.b], same_ok=True)
                P.op("vector", lambda e, p=pkc, h=h, c=c: e.tensor_scalar(out=kend_all.t[:, c, h, :], in0=p.ap(64, 128), scalar1=ekendC.t[:, c, h:h + 1],
                                                                         scalar2=None, op0=ALU.mult),
                     reads=[pkc.b, ekendC.b], writes=[kend_all.b])

        def s2z(tb):
            r0 = tb * 128
            for c in range(2):
                cg = tb * 2 + c
                P.dma("sync", lambda e, c=c, cg=cg: e.dma_start(out=sc["A"][cg].rearrange("h i j -> i h j"), in_=A_all.t[c * 64:(c + 1) * 64, :, c * 64:(c + 1) * 64]),
                      reads=[A_all.b], sembuf=A_all.b)
            P.dma("sync", lambda e, tb=tb: e.dma_start(out=sc["aqkT"][tb * 2:tb * 2 + 2].rearrange("c j h i -> j c h i"), in_=aqk_all.t[:]),
                  reads=[aqk_all.b], sembuf=aqk_all.b)
            P.dma("sync", lambda e, tb=tb: e.dma_start(out=sc["kend"][tb * 2:tb * 2 + 2].rearrange("c j h d -> j c h d"), in_=kend_all.t[:]),
                  reads=[kend_all.b], sembuf=kend_all.b)

        def s1_all(tb, third):
            s1(tb, third, (qn2[tb % 2], kn2[tb % 2], vs2[tb % 2])[third])

        for third in range(3):
            s1_all(0, third)
        for tb in range(NTILE):
            q_, k_, v_ = qn2[tb % 2], kn2[tb % 2], vs2[tb % 2]
            nxt = tb + 1 < NTILE
            if nxt:
                s1_all(tb + 1, 0)
            s2a(tb, q_, k_, v_)
            if nxt:
                s1_all(tb + 1, 1)
            for h in range(4):
                s2h(tb, h, q_, k_, v_)
            if nxt:
                s1_all(tb + 1, 2)
            for h in range(4, 8):
                s2h(tb, h, q_, k_, v_)
            s2z(tb)
        P.emit()


def gdn_solve_phase(P, nc, NT, sc, cin):
    NCH = NT // 64
    NSYS = NCH * 8
    NGRP = (NSYS + 127) // 128
    P.new_phase("gsolve")
    A_v = sc["A"].rearrange("c h i j -> (c h) (i j)")
    Tt_v = sc["Tt"].rearrange("c h j i -> (c h) (j i)")
    with ExitStack() as st:
        nset = min(2, NGRP)
        As = [sb(nc, st, f"As{i}", [128, 64, 64], F32) for i in range(nset)]
        Ts = [sb(nc, st, f"Ts{i}", [128, 64, 64], F32) for i in range(nset)]
        Tm = [sb(nc, st, f"Tm{i}", [128, 63, 63], F32) for i in range(nset)]
        Tb = [sb(nc, st, f"Tb{i}", [128, 64, 64], BF16) for i in range(nset)]
        for g0 in range(0, NGRP, nset):
            grp = list(range(g0, min(NGRP, g0 + nset)))
            Rs = {}
            for gi in grp:
                k = gi % nset
                A_, T_ = As[k], Ts[k]
                R = min(128, NSYS - gi * 128)
                Rs[gi] = R
                P.dma("sync", lambda e, A_=A_, gi=gi, R=R: e.dma_start(out=A_.t[0:R].rearrange("p i j -> p (i j)"), in_=A_v[gi * 128:gi * 128 + R, :]),
                      writes=[A_.b], sembuf=A_.b)
                P.dma("sync", lambda e, T_=T_: e.dma_start(out=T_.t[:].rearrange("p i j -> p (i j)"), in_=cin["I64"][:, :]),
                      writes=[T_.b], sembuf=T_.b)
            for i in range(1, 64):
                for gi in grp:
                    k = gi % nset
                    A_, T_, M_, R = As[k], Ts[k], Tm[k], Rs[gi]

                    def mul(e, A_=A_, T_=T_, M_=M_, i=i, R=R):
                        a = A_.t[0:R, i, 0:i]
                        in1 = bass.AP(a.tensor, a.offset, [list(a.ap[0]), [0, i], list(a.ap[1])])
                        in0 = T_.t[0:R, 0:i, 0:i].rearrange("p j c -> p c j")
                        return e.tensor_tensor(out=M_.t[0:R, 0:i, 0:i], in0=in0, in1=in1, op=ALU.mult)
                    P.op("gpsimd" if k == 1 else "vector", mul, reads=[A_.b, T_.b], writes=[M_.b])
                for gi in grp:
                    k = gi % nset
                    T_, M_, R = Ts[k], Tm[k], Rs[gi]
                    P.op("vector", lambda e, T_=T_, M_=M_, i=i, R=R: e.tensor_reduce(out=T_.t[0:R, i, 0:i], in_=M_.t[0:R, 0:i, 0:i], axis=AX.X, op=ALU.add, negate=True),
                         reads=[M_.b], writes=[T_.b])
            for gi in grp:
                k = gi % nset
                B_, T_, R = Tb[k], Ts[k], Rs[gi]
                P.op("vector", lambda e, B_=B_, T_=T_, R=R: e.tensor_copy(out=B_.t[0:R], in_=T_.t[0:R].rearrange("p i j -> p j i")), reads=[T_.b], writes=[B_.b])
                P.dma("sync", lambda e, B_=B_, gi=gi, R=R: e.dma_start(out=Tt_v[gi * 128:gi * 128 + R, :], in_=B_.t[0:R].rearrange("p j i -> p (j i)")),
                      reads=[B_.b], sembuf=B_.b)
        P.emit()


def gdn_scan_phase(P, nc, NT, sc, cin, consts, egl_all, pname="gscan", with_out=True, init_ap=None, final_ap=None):
    NTILE = NT // 128
    P.new_phase(pname)
    ident = consts["ident_f32"]
    with ExitStack() as st:
        kbg = [sb(nc, st, f"kbg{i}", [128, 8, 128], BF16) for i in range(2)]
        vb = [sb(nc, st, f"vb{i}", [128, 8, 128], BF16) for i in range(2)]
        qgT = [sb(nc, st, f"qgT{i}", [128, 8, 128], BF16) for i in range(2)]
        aqk = [sb(nc, st, f"aqk{i}", [64, 2, 8, 64], BF16) for i in range(2)]
        kend = [sb(nc, st, f"kend{i}", [64, 2, 8, 128], BF16) for i in range(2)]
        TtBD = [sb(nc, st, f"TtBD{i}", [128, 8, 128], BF16) for i in range(2)]
        gate = [sb(nc, st, f"gate{i}", [64, 2, 8, 128], F32) for i in range(2)]
        wT = sb(nc, st, "wT", [128, 8, 128], BF16)
        u = sb(nc, st, "u", [64, 2, 8, 128], F32)
        o = sb(nc, st, "o", [64, 2, 8, 128], F32)
        sq = sb(nc, st, "sq", [64, 2, 8, 128], F32)
        S = [sb(nc, st, f"S{h}", [128, 128], F32) for h in range(8)]
        vnew = [sb(nc, st, f"vnew{i}", [64, 128], BF16) for i in range(8)]
        Sb = [sb(nc, st, f"Sb{h}", [128, 128], BF16) for h in range(8)]
        ssq = sb(nc, st, "ssq", [64, 16], F32)
        mh16 = sb(nc, st, "mh16", [64, 16], F32)
        gnb = sb(nc, st, "gnb", [128, 128], F32)
        yTt = [sb(nc, st, f"yTt{i}", [128, 8, 128], BF16) for i in range(2)]
        banks = [st.enter_context(nc.psum_tensor(f"gs_{pname}_bank{i}", [128, 512], F32)) for i in range(8)]
        bankbufs = [Buf(f"bank{i}") for i in range(8)]
        pq = [PQ(banks[i % 8], i // 8, bankbufs[i % 8]) for i in range(32)]
        pqi = [0]

        def nextpq():
            p = pq[pqi[0] % 32]
            pqi[0] += 1
            return p

        P.op("vector", lambda e: e.memset(mh16.t[:], -0.5), writes=[mh16.b])
        P.dma("sync", lambda e: e.dma_start(out=gnb.t[:], in_=cin["gdn_norm_b"][:, :]), writes=[gnb.b], sembuf=gnb.b)
        for i in range(2):
            P.op("gpsimd", lambda e, i=i: e.memset(TtBD[i].t[:], 0.0), writes=[TtBD[i].b])
        for h in range(8):
            if init_ap is None:
                P.op("vector", lambda e, h=h: e.memset(S[h].t[:], 0.0), writes=[S[h].b])
            else:
                if h == 0:
                    flagS = sb(nc, st, "flagS", [128, 1], F32)
                    P.dma("sync", lambda e: e.dma_start(out=flagS.t[:], in_=cin["flag01"][:, :]), writes=[flagS.b], sembuf=flagS.b)
                P.dma("sync", lambda e, h=h: e.dma_start(out=S[h].t[:], in_=init_ap[h]), writes=[S[h].b], sembuf=S[h].b)
                P.op("vector", lambda e, h=h: e.tensor_scalar(out=S[h].t[:], in0=S[h].t[:], scalar1=flagS.t[:, 0:1], scalar2=None, op0=ALU.mult),
                     reads=[S[h].b, flagS.b], writes=[S[h].b])
        for h in range(8):
            P.op("scalar", lambda e, h=h: e.activation(out=Sb[h].t[:], in_=S[h].t[:], func=AF.Copy), reads=[S[h].b], writes=[Sb[h].b])
        for tb in range(NTILE):
            r0 = tb * 128
            k_ = tb % 2
            KB, VB, QG, AQ, KE, TB_, GT, YT = kbg[k_], vb[k_], qgT[k_], aqk[k_], kend[k_], TtBD[k_], gate[k_], yTt[k_]
            P.dma("sync", lambda e, KB=KB, r0=r0: e.dma_start(out=KB.t[:].rearrange("p h d -> p (h d)"), in_=sc["kbg"][r0:r0 + 128, :]), writes=[KB.b], sembuf=KB.b)
            P.dma("sync", lambda e, VB=VB, r0=r0: e.dma_start(out=VB.t[:].rearrange("p h d -> p (h d)"), in_=sc["vb"][r0:r0 + 128, :]), writes=[VB.b], sembuf=VB.b)
            P.dma("sync", lambda e, AQ=AQ, tb=tb: e.dma_start(out=AQ.t[:], in_=sc["aqkT"][tb * 2:tb * 2 + 2].rearrange("c j h i -> j c h i")), writes=[AQ.b], sembuf=AQ.b)
            P.dma("sync", lambda e, KE=KE, tb=tb: e.dma_start(out=KE.t[:], in_=sc["kend"][tb * 2:tb * 2 + 2].rearrange("c j h d -> j c h d")), writes=[KE.b], sembuf=KE.b)
            for c in range(2):
                P.dma("sync", lambda e, TB_=TB_, tb=tb, c=c: e.dma_start(out=TB_.t[c * 64:(c + 1) * 64, :, c * 64:(c + 1) * 64],
                                                                       in_=sc["Tt"][tb * 2 + c].rearrange("h j i -> j h i")), writes=[TB_.b], sembuf=TB_.b)
            if with_out:
                P.dma("sync", lambda e, QG=QG, tb=tb: e.dma_start(out=QG.t[:], in_=sc["qgT"][tb]), writes=[QG.b], sembuf=QG.b)
                P.dma("sync", lambda e, GT=GT, r0=r0: e.dma_start(out=GT.t[:], in_=sc["gate"][r0:r0 + 128, :].rearrange("(c p) (h d) -> p c h d", c=2, h=8)),
                      writes=[GT.b], sembuf=GT.b)
            for h in range(8):
                p = nextpq()
                P.op("tensor", lambda e, p=p, h=h, KB=KB, TB_=TB_: mm32(e, p.ap(), lhsT=KB.t[:, h, :], rhs=TB_.t[:, h, :], start=True, stop=True),
                     reads=[KB.b, TB_.b], writes=[p.b], same_ok=True)
                P.op("scalar", lambda e, p=p, h=h: e.activation(out=wT.t[:, h, :], in_=p.ap(), func=AF.Copy), reads=[p.b], writes=[wT.b])
                for c in range(2):
                    p2 = nextpq()
                    P.op("tensor", lambda e, p=p2, h=h, c=c, VB=VB, TB_=TB_: mm32(e, p.ap(64, 128), lhsT=TB_.t[:, h, c * 64:(c + 1) * 64], rhs=VB.t[:, h, :], start=True, stop=True),
                         reads=[VB.b, TB_.b], writes=[p2.b], same_ok=True)
                    P.op("vector", lambda e, p=p2, h=h, c=c: e.tensor_copy(out=u.t[:, c, h, :], in_=p.ap(64, 128)), reads=[p2.b], writes=[u.b])
            for c in range(2):
                cg = tb * 2 + c
                pws = [nextpq() for _ in range(8)]
                for h in range(8):
                    P.op("tensor", lambda e, p=pws[h], h=h, c=c: mm32(e, p.ap(64, 128), lhsT=wT.t[:, h, c * 64:(c + 1) * 64], rhs=Sb[h].t[:], start=True, stop=True),
                         reads=[wT.b, Sb[h].b], writes=[pws[h].b], same_ok=True)
                for h in range(8):
                    P.op("vector", lambda e, p=pws[h], h=h, c=c: e.tensor_tensor(out=vnew[h].t[:], in0=u.t[:, c, h, :], in1=p.ap(64, 128), op=ALU.subtract),
                         reads=[u.b, pws[h].b], writes=[vnew[h].b])
                for h in range(8):
                    if with_out:
                        po = nextpq()
                        P.op("tensor", lambda e, p=po, h=h, c=c, QG=QG: mm32(e, p.ap(64, 128), lhsT=QG.t[:, h, c * 64:(c + 1) * 64], rhs=Sb[h].t[:], start=True, stop=False),
                             reads=[QG.b, Sb[h].b], writes=[po.b], same_ok=True)
                        P.op("tensor", lambda e, p=po, h=h, c=c, AQ=AQ: mm32(e, p.ap(64, 128), lhsT=AQ.t[:, c, h, :], rhs=vnew[h].t[:], start=False, stop=True),
                             reads=[AQ.b, vnew[h].b], writes=[po.b], same_ok=True)
                        P.op("scalar", lambda e, p=po, h=h, c=c: e.activation(out=o.t[:, c, h, :], in_=p.ap(64, 128), func=AF.Copy), reads=[po.b], writes=[o.b])
                    psu = nextpq()
                    P.op("tensor", lambda e, p=psu, h=h, c=c, KE=KE: mm32(e, p.ap(), lhsT=KE.t[:, c, h, :], rhs=vnew[h].t[:], start=True, stop=True),
                         reads=[KE.b, vnew[h].b], writes=[psu.b], same_ok=True)
                    P.op("vector", lambda e, p=psu, h=h, cg=cg: e.scalar_tensor_tensor(out=S[h].t[:], in0=S[h].t[:], scalar=egl_all.t[:, cg, h:h + 1], in1=p.ap(),
                                                                                     op0=ALU.mult, op1=ALU.add),
                         reads=[S[h].b, psu.b, egl_all.b], writes=[S[h].b])
                    P.op("scalar", lambda e, h=h: e.activation(out=Sb[h].t[:], in_=S[h].t[:], func=AF.Copy), reads=[S[h].b], writes=[Sb[h].b])
            if with_out:
                o16 = o.t[:].rearrange("p c h d -> p (c h) d")
                sq16 = sq.t[:].rearrange("p c h d -> p (c h) d")
                g16 = GT.t[:].rearrange("p c h d -> p (c h) d")
                P.op("gpsimd", lambda e, o16=o16, sq16=sq16: e.tensor_tensor(out=sq16, in0=o16, in1=o16, op=ALU.mult), reads=[o.b], writes=[sq.b])
                P.op("vector", lambda e, sq16=sq16: e.tensor_reduce(out=ssq.t[:], in_=sq16, axis=AX.X, op=ALU.add), reads=[sq.b], writes=[ssq.b])
                P.op("vector", lambda e: e.tensor_scalar(out=ssq.t[:], in0=ssq.t[:], scalar1=1.0 / 128.0, scalar2=EPS, op0=ALU.mult, op1=ALU.add),
                     reads=[ssq.b], writes=[ssq.b])
                P.op("gpsimd", lambda e: e.tensor_tensor(out=ssq.t[:], in0=ssq.t[:], in1=mh16.t[:], op=ALU.pow), reads=[ssq.b, mh16.b], writes=[ssq.b])
                P.op("vector", lambda e, o16=o16: e.tensor_tensor(out=o16, in0=o16, in1=bcast_last(ssq.t[:, :], 128), op=ALU.mult), reads=[o.b, ssq.b], writes=[o.b])
                gn = gnb.t[0:64, :]
                gnb16 = bass.AP(gn.tensor, gn.offset, [list(gn.ap[0]), [0, 16], list(gn.ap[1])])
                P.op("gpsimd", lambda e, o16=o16, gnb16=gnb16: e.tensor_tensor(out=o16, in0=o16, in1=gnb16, op=ALU.mult), reads=[o.b, gnb.b], writes=[o.b])
                P.op("scalar", lambda e, g16=g16: e.activation(out=g16, in_=g16, func=AF.Silu), reads=[GT.b], writes=[GT.b])
                P.op("vector", lambda e, o16=o16, g16=g16: e.tensor_tensor(out=o16, in0=o16, in1=g16, op=ALU.mult), reads=[o.b, GT.b], writes=[o.b])
                for h in range(8):
                    p = nextpq()
                    for c in range(2):
                        P.op("tensor", lambda e, p=p, h=h, c=c: e.transpose(out=p.bank[0:128, p.q * 128 + c * 64:p.q * 128 + (c + 1) * 64], in_=o.t[:, c, h, :],
                                                                           identity=ident.t[0:64, 0:64]),
                             reads=[o.b, ident.b], writes=[p.b], same_ok=True)
                    if h % 2 == 0:
                        P.op("scalar", lambda e, p=p, h=h, YT=YT: e.activation(out=YT.t[:, h, :], in_=p.ap(), func=AF.Copy), reads=[p.b], writes=[YT.b])
                    else:
                        P.op("vector", lambda e, p=p, h=h, YT=YT: e.tensor_copy(out=YT.t[:, h, :], in_=p.ap()), reads=[p.b], writes=[YT.b])
                P.dma("sync", lambda e, YT=YT, r0=r0: e.dma_start(out=sc["yT"][8:16, :, r0:r0 + 128].rearrange("h d t -> d h t"), in_=YT.t[:]),
                      reads=[YT.b], sembuf=YT.b)
        if final_ap is not None:
            for h in range(8):
                P.dma("sync", lambda e, h=h: e.dma_start(out=final_ap[h], in_=S[h].t[:]), reads=[S[h].b], sembuf=S[h].b)
        P.emit()


def wout_phase(P, nc, NT, sc, w_out):
    NTILE = NT // 128
    P.new_phase("wout")
    with ExitStack() as st:
        wo = sb(nc, st, "wo", [128, 16, D], BF16)
        yt = [sb(nc, st, f"yt{i}", [128, 16, 128], BF16) for i in range(2)]
        xr = [sb(nc, st, f"xr{i}", [128, D], F32) for i in range(2)]
        xo = [sb(nc, st, f"xo{i}", [128, D], F32) for i in range(2)]
        pbank = [ps(nc, st, f"pb{i}", [128, 512]) for i in range(8)]
        w_v = w_out.rearrange("(c p) n -> p c n", p=128)
        for q4 in range(4):
            P.dma("gpsimd", lambda e, q4=q4: e.dma_start(out=wo.t[:, q4 * 4:(q4 + 1) * 4, :], in_=w_v[:, q4 * 4:(q4 + 1) * 4, :]), writes=[wo.b], sembuf=wo.b)
        for tb in range(NTILE):
            r0 = tb * 128
            YT, XR, XO = yt[tb % 2], xr[tb % 2], xo[tb % 2]
            P.dma("sync", lambda e, YT=YT, r0=r0: e.dma_start(out=YT.t[:], in_=sc["yT"][:, :, r0:r0 + 128].rearrange("c d t -> d c t")), writes=[YT.b], sembuf=YT.b)
            P.dma("sync", lambda e, XR=XR, r0=r0: e.dma_start(out=XR.t[:], in_=sc["x1"][r0:r0 + 128, :]), writes=[XR.b], sembuf=XR.b)
            for n in range(4):
                PB = pbank[(tb * 4 + n) % 8]
                for c in range(16):
                    P.op("tensor", lambda e, PB=PB, YT=YT, c=c, n=n: e.matmul(PB.t[:, :], lhsT=YT.t[:, c, :], rhs=wo.t[:, c, n * 512:(n + 1) * 512],
                                                                           start=(c == 0), stop=(c == 15)),
                         reads=[YT.b, wo.b], writes=[PB.b], same_ok=True)
                P.op("vector", lambda e, PB=PB, XR=XR, XO=XO, n=n: e.tensor_tensor(out=XO.t[:, n * 512:(n + 1) * 512], in0=PB.t[:, :], in1=XR.t[:, n * 512:(n + 1) * 512], op=ALU.add),
                     reads=[PB.b, XR.b], writes=[XO.b])
            P.dma("sync", lambda e, XO=XO, r0=r0: e.dma_start(out=sc["x2"][r0:r0 + 128, :], in_=XO.t[:]), reads=[XO.b], sembuf=XO.b)
        P.emit()


def fnorm_phase(P, nc, NT, src, dst, gain_b_d, consts):
    NTILE = NT // 128
    P.new_phase("fnorm")
    mhalf = consts["mhalf"]
    with ExitStack() as st:
        gb = sb(nc, st, "gb", [128, D], F32)
        xs = [sb(nc, st, f"xs{i}", [128, D], F32) for i in range(3)]
        xq = sb(nc, st, "xq", [128, D], F32)
        ssq = [sb(nc, st, f"ssq{i}", [128, 1], F32) for i in range(3)]
        P.dma("sync", lambda e: e.dma_start(out=gb.t[:], in_=gain_b_d[:, :]), writes=[gb.b], sembuf=gb.b)
        for tb in range(NTILE):
            r0 = tb * 128
            X, SS = xs[tb % 3], ssq[tb % 3]
            P.dma("sync", lambda e, X=X, r0=r0: e.dma_start(out=X.t[:], in_=src[r0:r0 + 128, :]), writes=[X.b], sembuf=X.b)
            P.op("scalar", lambda e, X=X, SS=SS: e.activation(out=xq.t[:], in_=X.t[:], func=AF.Square, accum_out=SS.t[:]), reads=[X.b], writes=[xq.b, SS.b])
            P.op("vector", lambda e, SS=SS: e.tensor_scalar(out=SS.t[:], in0=SS.t[:], scalar1=1.0 / D, scalar2=EPS, op0=ALU.mult, op1=ALU.add), reads=[SS.b], writes=[SS.b])
            P.op("gpsimd", lambda e, SS=SS: e.tensor_tensor(out=SS.t[:], in0=SS.t[:], in1=mhalf.t[:], op=ALU.pow), reads=[SS.b, mhalf.b], writes=[SS.b])
            P.op("vector", lambda e, X=X, SS=SS: e.scalar_tensor_tensor(out=X.t[:], in0=X.t[:], scalar=SS.t[:, 0:1], in1=gb.t[:], op0=ALU.mult, op1=ALU.mult),
                 reads=[X.b, SS.b, gb.b], writes=[X.b])
            P.dma("sync", lambda e, X=X, r0=r0: e.dma_start(out=dst[r0:r0 + 128, :], in_=X.t[:]), reads=[X.b], sembuf=X.b)
        P.emit()


PAIRS = [[0, 1], [2, 3], [4, 5], [6, 7]]


def xchg_phase(P, nc, name, ccsem, cccount, items, pre_copy=None):
    P.new_phase(name)
    if pre_copy is not None:
        dummy = Buf("cp")
        for (src_ap, dst_ap) in pre_copy:
            P.dma("gpsimd", lambda e, src_ap=src_ap, dst_ap=dst_ap: e.dma_start(out=dst_ap, in_=src_ap), writes=[dummy], sembuf=dummy)
    prev = [None]
    for (src, dst) in items:
        cccount[0] += 1
        n = cccount[0]

        def fn(e, src=src, dst=dst, n=n):
            e.collective_compute("AllGather", ALU.bypass, replica_groups=PAIRS,
                                 ins=[src.ap().opt()], outs=[dst.ap().opt()]).then_inc(ccsem)
            return e.wait_ge(ccsem, n)
        b = Buf("cc")
        reads = [dummy] if pre_copy is not None else []
        P.op("gpsimd", fn, reads=reads, writes=[b])
    P.emit()


class LazyIn(dict):
    def __init__(self, nc, shapes):
        super().__init__()
        self.nc = nc
        self.shapes = shapes

    def __missing__(self, name):
        ap = self.nc.dram_tensor(name, list(self.shapes[name]), F32, kind="ExternalInput").ap()
        self[name] = ap
        return ap


def input_shapes(NT):
    sh = {"x": [NT, D], "w_in": [D, 7184], "w_out": [D, D]}
    for nm in ("ffn1_norm", "mix_norm", "ffn2_norm", "final_norm_t"):
        sh[nm] = [128, NKC]
    for pre in ("ffn1", "ffn2"):
        sh[pre + "_w_gate"] = [D, DFF]
        sh[pre + "_w_up"] = [D, DFF]
        sh[pre + "_w_down"] = [DFF, D]
    sh.update({"ident_f32": [128, 128], "trineg": [128, 128], "tricomp": [128, 128], "negones": [128, 128], "ones32": [128, 128],
               "dmask": [4, 128, 512], "negbias": [128, 1], "sb_out_norm": [128, 1],
               "final_norm_b": [128, D], "convw_b": [128, 4 * 3072], "alog_b": [128, 8], "dtb_b": [128, 8],
               "BT": [128, 128], "BL": [128, 128], "selC": [128, 256], "selH": [8, 1024], "MUs": [128, 128],
               "ML64": [64, 64], "MS01": [128, 128], "ML01": [64, 64], "I64": [128, 4096], "gdn_norm_b": [128, 128], "flag01": [128, 1]})
    return sh


def const_inputs(j):
    c = {}
    c["ident_f32"] = np.eye(128, dtype=np.float32)
    jj = np.arange(128)[:, None]
    ss = np.arange(128)[None, :]
    c["trineg"] = np.where(jj >= ss, -1.0, 0.0).astype(np.float32)
    c["negones"] = -np.ones((128, 128), np.float32)
    c["tricomp"] = np.where(jj < ss, -1.0, 0.0).astype(np.float32)
    c["ones32"] = np.ones((128, 128), np.float32)
    t = np.arange(512)[None, None, :]
    r = np.arange(4)[:, None, None]
    sp = np.arange(128)[None, :, None]
    c["dmask"] = ((r * 128 + sp) < t).astype(np.float32)
    c["negbias"] = np.full((128, 1), 0.0 if j == 1 else -1.0e4, np.float32)
    c["flag01"] = np.full((128, 1), 1.0 if j == 1 else 0.0, np.float32)
    ch = np.arange(128) // 64
    same = ch[:, None] == ch[None, :]
    ii = np.arange(128)
    c["BT"] = (same & (ii[:, None] <= ii[None, :])).astype(np.float32)
    c["BL"] = same.astype(np.float32)
    selC = np.zeros((128, 2, 128), np.float32)
    selC[:64, 0, :] = 1.0
    selC[64:, 1, :] = 1.0
    c["selC"] = selC.reshape(128, 256)
    selH = np.zeros((8, 8, 128), np.float32)
    for h in range(8):
        selH[h, h, :] = 1.0
    c["selH"] = selH.reshape(8, 1024)
    c["MUs"] = np.where(same & (ii[None, :] < ii[:, None]), 0.0, 1.0e4).astype(np.float32)
    i64 = np.arange(64)
    c["ML64"] = np.where(i64[:, None] <= i64[None, :], 0.0, -1.0e4).astype(np.float32)
    c["MS01"] = (same & (ii[None, :] < ii[:, None])).astype(np.float32)
    c["ML01"] = (i64[:, None] <= i64[None, :]).astype(np.float32)
    c["I64"] = np.tile(np.eye(64, dtype=np.float32).reshape(1, 4096), (128, 1))
    return c


def build_program(NT=2048, NPREV=0, phases=("ffn1", "proj", "attn", "gdn", "wout", "ffn2", "fnorm"), dbg=(), exchange=False):
    nc = bass.Bass("TRN2", target_bir_lowering=False)
    cin = LazyIn(nc, input_shapes(NT))

    def dsc(name, shape, dtype=F32, ap=True):
        kind = "ExternalOutput" if name in dbg else "Internal"
        t = nc.dram_tensor(name, list(shape), dtype, kind=kind)
        return t.ap() if ap else t

    NK = NPREV + NT
    NCH = NT // 64
    out = nc.dram_tensor("out", [NT, D], F32, kind="ExternalOutput").ap()
    sc = {}
    sc["x1"] = dsc("x1", [NT, D]) if "ffn1" in phases else cin["x"]
    sc["x2"] = dsc("x2", [NT, D])
    sc["x3"] = dsc("x3", [NT, D])
    sc["qT"] = dsc("qT", [8, 128, NT], BF16)
    xg = None
    if exchange:
        sc["kT"] = dsc("kT", [8, 128, NT], BF16)
        sc["v"] = dsc("v", [NT, 1024], BF16)
        HK = 4 * 128
        HV = NT // 2
        ks_t = [dsc(f"ksrc{i}", [HK, NT], BF16, ap=False) for i in range(2)]
        kd_t = [dsc(f"kdst{i}", [2 * HK, NT], BF16, ap=False) for i in range(2)]
        vs_t = [dsc(f"vsrc{i}", [HV, 1024], BF16, ap=False) for i in range(2)]
        vd_t = [dsc(f"vdst{i}", [2 * HV, 1024], BF16, ap=False) for i in range(2)]
        hs_t = dsc("hist_src", [3, 3072], F32, ap=False)
        hd_t = dsc("hist_dst", [6, 3072], F32, ap=False)
        ss_t = dsc("st_src", [8 * 128, 128], F32, ap=False)
        sd_t = dsc("st_dst", [2 * 8 * 128, 128], F32, ap=False)
        xg = {"kd": [t.ap() for t in kd_t], "vd": [t.ap() for t in vd_t], "hist_dst": hd_t.ap(), "HV": HV}
    else:
        sc["kT"] = dsc("kT", [8, 128, NK], BF16)
        sc["v"] = dsc("v", [NK, 1024], BF16)
    sc["graw"] = dsc("graw", [3 + NT, 3072])
    sc["ab"] = dsc("ab", [NT, 16])
    sc["gate"] = dsc("gate", [NT, 1024])
    sc["yT"] = dsc("yT", [16, 128, NT], BF16)
    sc["gcrow"] = dsc("gcrow", [NT // 128, 8, 128])
    sc["kbg"] = dsc("kbg", [NT, 1024], BF16)
    sc["vb"] = dsc("vb", [NT, 1024], BF16)
    sc["qgT"] = dsc("qgT", [NT // 128, 128, 8, 128], BF16)
    sc["aqkT"] = dsc("aqkT", [NCH, 64, 8, 64], BF16)
    sc["kend"] = dsc("kend", [NCH, 64, 8, 128], BF16)
    sc["A"] = dsc("A", [NCH, 8, 64, 64])
    sc["Tt"] = dsc("Tt", [NCH, 8, 64, 64], BF16)

    with ExitStack() as stack:
        P = Prog(nc, stack)
        ccsem = stack.enter_context(nc.semaphore("ccsem"))
        cccount = [0]
        consts = {}
        consts["ident_f32"] = sb(nc, stack, "ident_f32", [128, 128], F32)
        consts["mhalf"] = sb(nc, stack, "mhalf", [128, 1], F32)
        P.new_phase("const")
        c = consts["ident_f32"]
        P.dma("sync", lambda e: e.dma_start(out=c.t[:], in_=cin["ident_f32"][:, :]), writes=[c.b], sembuf=c.b)
        m = consts["mhalf"]
        P.op("vector", lambda e: e.memset(m.t[:], -0.5), writes=[m.b])
        P.emit()

        if "ffn1" in phases:
            ffn_phase(P, nc, "ffn1", NT, cin["x"], sc["x1"], cin["ffn1_norm"], cin["ffn1_w_gate"], cin["ffn1_w_up"],
                      cin["ffn1_w_down"], consts)
        if "proj" in phases:
            proj_phase(P, nc, NT, 0 if exchange else NPREV, sc["x1"], cin["mix_norm"], cin["w_in"], sc, consts)
        if exchange:
            kflat = sc["kT"].rearrange("h d n -> (h d) n")
            pre = [(kflat[i * HK:(i + 1) * HK, :], ks_t[i].ap()) for i in range(2)]
            pre += [(sc["v"][i * HV:(i + 1) * HV, :], vs_t[i].ap()) for i in range(2)]
            pre += [(sc["graw"][NT:NT + 3, :], hs_t.ap())]
            xchg_phase(P, nc, "xchg1", ccsem, cccount,
                       [(ks_t[0], kd_t[0]), (ks_t[1], kd_t[1]), (vs_t[0], vd_t[0]), (vs_t[1], vd_t[1]), (hs_t, hd_t)], pre_copy=pre)
        if "attn" in phases:
            attn_phase(P, nc, NT, NPREV, sc, cin, consts, xg=xg)
        if "gdn" in phases:
            egl_all = sb(nc, stack, "egl_all", [128, NCH, 8], F32)
            gdn_prep_phase(P, nc, NT, sc, cin, consts, egl_all, xg=xg)
            if "nosolve" not in phases:
                gdn_solve_phase(P, nc, NT, sc, cin)
            if "noscan" not in phases:
                if exchange:
                    ss_v = ss_t.ap().rearrange("(h k) v -> h k v", h=8)
                    sd_v = sd_t.ap().rearrange("(r h k) v -> r h k v", r=2, h=8)
                    gdn_scan_phase(P, nc, NT, sc, cin, consts, egl_all, pname="gscanA", with_out=False, final_ap=ss_v)
                    xchg_phase(P, nc, "xchg2", ccsem, cccount, [(ss_t, sd_t)])
                    gdn_scan_phase(P, nc, NT, sc, cin, consts, egl_all, pname="gscanB", with_out=True, init_ap=sd_v[0])
                else:
                    gdn_scan_phase(P, nc, NT, sc, cin, consts, egl_all)
        if "wout" in phases:
            wout_phase(P, nc, NT, sc, cin["w_out"])
        if "ffn2" in phases:
            ffn_phase(P, nc, "ffn2", NT, sc["x2"], sc["x3"], cin["ffn2_norm"], cin["ffn2_w_gate"], cin["ffn2_w_up"],
                      cin["ffn2_w_down"], consts)
        if "fnorm" in phases:
            fnorm_phase(P, nc, NT, sc["x3"], out, cin["final_norm_b"], consts)
    return nc


MODE = "T"
_CACHE = {}


def kernel(**inputs):
    f32 = lambda a: np.ascontiguousarray(np.asarray(a, dtype=np.float32))
    x = f32(inputs["x"])
    B, S, _ = x.shape
    NT = S if MODE == "A" else S // 2
    NPREV = 0 if MODE == "A" else S // 2
    key = (MODE, NT, NPREV)
    if key not in _CACHE:
        _CACHE[key] = build_program(NT=NT, NPREV=NPREV, exchange=(MODE == "T"))
    nc = _CACHE[key]

    def nt(v):
        return np.ascontiguousarray(f32(v).reshape(NKC, 128).T)

    shared = {}
    shared["ffn1_norm"] = nt(inputs["ffn1_norm"][0])
    shared["mix_norm"] = nt(inputs["mix_norm"][0])
    shared["ffn2_norm"] = nt(inputs["ffn2_norm"][0])
    shared["final_norm_b"] = np.ascontiguousarray(np.tile(f32(inputs["final_norm"]).reshape(1, D), (128, 1)))
    for pre in ("ffn1", "ffn2"):
        for w in ("w_gate", "w_up", "w_down"):
            shared[f"{pre}_{w}"] = f32(inputs[f"{pre}_{w}"][0])
    shared["w_in"] = f32(inputs["w_in"][0])
    shared["w_out"] = f32(inputs["w_out"][0])
    shared["sb_out_norm"] = f32(inputs["sb_out_norm"][0]).reshape(128, 1)
    shared["convw_b"] = np.ascontiguousarray(np.tile(f32(inputs["conv_w"][0]).reshape(1, -1), (128, 1)))
    shared["alog_b"] = np.ascontiguousarray(np.tile(f32(inputs["a_log"][0]).reshape(1, 8), (128, 1)))
    shared["dtb_b"] = np.ascontiguousarray(np.tile(f32(inputs["dt_bias"][0]).reshape(1, 8), (128, 1)))
    shared["gdn_norm_b"] = np.ascontiguousarray(np.tile(f32(inputs["gdn_out_norm"][0]).reshape(1, 128), (128, 1)))
    consts = [const_inputs(0), const_inputs(1)]
    in_maps = []
    for c in range(8):
        b, j = c // 2, c % 2
        m = dict(shared)
        m.update(consts[j])
        if MODE == "A":
            m["x"] = x[b]
        else:
            m["x"] = np.ascontiguousarray(x[b, j * NT:(j + 1) * NT])
        in_maps.append(m)
    res = run_bass_kernel_spmd(nc, in_maps, core_ids=list(range(8)))
    out = np.empty((B, S, D), np.float32)
    for c in range(8):
        b, j = c // 2, c % 2
        if MODE == "A":
            if j == 0:
                out[b] = res.results[c]["out"]
        else:
            out[b, j * NT:(j + 1) * NT] = res.results[c]["out"]
    return out
```

```python
from contextlib import ExitStack
import numpy as np
import concourse.bass as bass
import concourse.mybir as mybir
from concourse.bass_utils import run_bass_kernel_spmd

F32 = mybir.dt.float32
BF16 = mybir.dt.bfloat16
AF = mybir.ActivationFunctionType
ALU = mybir.AluOpType
AX = mybir.AxisListType

D = 2048
DFF = 5504
NF = DFF // 128
NKC = D // 128
SEQ = 4096
EPS = 1e-6
ENGS = ("tensor", "scalar", "vector", "gpsimd", "sync")


class Buf:
    __slots__ = ("name", "w", "r", "rd", "dsem")

    def __init__(self, name):
        self.name = name
        self.w = None
        self.r = {}
        self.rd = []
        self.dsem = None


class Op:
    __slots__ = ("eng", "fn", "deps", "sig", "sem", "val", "inc", "is_dma", "ph")

    def __init__(self, eng, fn, is_dma):
        self.eng = eng
        self.fn = fn
        self.deps = []
        self.sig = False
        self.sem = None
        self.val = None
        self.inc = 1
        self.is_dma = is_dma


class Prog:
    def __init__(self, nc, stack, n_dma_sems=40):
        self.nc = nc
        self.stack = stack
        self.dma_sems = []
        for i in range(n_dma_sems):
            h = stack.enter_context(nc.semaphore(f"dq{i}"))
            self.dma_sems.append([h, 0])
        self.n_dma_sems = n_dma_sems

    def new_phase(self, name):
        self.pname = name
        self.ops = {e: [] for e in ENGS}
        self.phase_dma_ops = []
        self.prog_sems = {}
        for e in ENGS[:4]:
            self.prog_sems[e] = self.stack.enter_context(self.nc.semaphore(f"pg_{name}_{e}"))
        self.free_dma = {"sync": list(range(12, self.n_dma_sems)), "gpsimd": list(range(0, 12)),
                         "scalar": []}

    def op(self, eng, fn, reads=(), writes=(), same_ok=False):
        o = Op(eng, fn, False)
        self._deps(o, reads, writes, same_ok)
        self.ops[eng].append(o)
        return o

    def dma(self, eng, fn, reads=(), writes=(), sembuf=None):
        o = Op(eng, fn, True)
        self._deps(o, reads, writes, False)
        if sembuf.dsem is None or sembuf.dsem[0] != self.pname:
            sembuf.dsem = (self.pname, self.free_dma[eng].pop())
        ent = self.dma_sems[sembuf.dsem[1]]
        ent[1] += 16
        o.sem = ent[0]
        o.val = ent[1]
        o.inc = 16
        o.sig = True
        self.ops[eng].append(o)
        self.phase_dma_ops.append(o)
        return o

    def _deps(self, o, reads, writes, same_ok):
        o.ph = self.pname
        deps = []
        for b in reads:
            if b.w is not None:
                deps.append(b.w)
        for b in writes:
            if b.w is not None:
                deps.append(b.w)
            deps.extend(b.r.values())
            deps.extend(b.rd)
        for d in deps:
            if d is o or d.ph != o.ph:
                continue
            if same_ok and (not d.is_dma) and d.eng == o.eng:
                continue
            o.deps.append(d)
        for b in reads:
            if o.is_dma:
                b.rd.append(o)
            else:
                b.r[o.eng] = o
        for b in writes:
            b.w = o
            b.r = {}
            b.rd = []

    def emit(self):
        nc = self.nc
        for e in ENGS:
            for o in self.ops[e]:
                for d in o.deps:
                    d.sig = True
        for e in ENGS[:4]:
            cnt = 0
            for o in self.ops[e]:
                if o.is_dma:
                    continue
                if o.sig:
                    cnt += 1
                    o.sem = self.prog_sems[e]
                    o.val = cnt
        finals = {e: {} for e in ENGS}
        for o in self.phase_dma_ops:
            k = id(o.sem)
            cur = finals[o.eng].get(k)
            if cur is None or cur[1] < o.val:
                finals[o.eng][k] = (o.sem, o.val)
        ops = self.ops

        def replay(e, eng):
            waited = {}
            for o in ops[e]:
                need = {}
                for d in o.deps:
                    k = id(d.sem)
                    if waited.get(k, 0) >= d.val:
                        continue
                    if k not in need or need[k][1] < d.val:
                        need[k] = (d.sem, d.val)
                for k, (s, v) in need.items():
                    eng.wait_ge(s, v)
                    waited[k] = v
                ins = o.fn(eng)
                if o.sig:
                    ins.then_inc(o.sem, o.inc)
            for k, (s, v) in finals[e].items():
                if waited.get(k, 0) < v:
                    eng.wait_ge(s, v)

        with nc.Block() as block:
            if ops["sync"]:
                @block.sync
                def _(eng):
                    replay("sync", eng)
            if ops["gpsimd"]:
                @block.gpsimd
                def _(eng):
                    replay("gpsimd", eng)
            if ops["tensor"]:
                @block.tensor
                def _(eng):
                    replay("tensor", eng)
            if ops["scalar"]:
                @block.scalar
                def _(eng):
                    replay("scalar", eng)
            if ops["vector"]:
                @block.vector
                def _(eng):
                    replay("vector", eng)
        self.ops = None
        self.phase_dma_ops = None


class Tile:
    def __init__(self, t, name, nbuf=1):
        self.t = t
        self.b = Buf(name)


_UID = [0]


def sb(nc, stack, name, shape, dtype):
    _UID[0] += 1
    t = stack.enter_context(nc.sbuf_tensor(f"s{_UID[0]}_{name}", list(shape), dtype))
    return Tile(t, name)


def ps(nc, stack, name, shape, dtype=F32):
    _UID[0] += 1
    t = stack.enter_context(nc.psum_tensor(f"p{_UID[0]}_{name}", list(shape), dtype))
    return Tile(t, name)


def bcast_last(ap, n):
    a = ap.ap
    return bass.AP(ap.tensor, ap.offset, [list(a[0]), list(a[1]), [0, n]])


def ffn_phase(P, nc, name, NT, src, dst, gain_d, wg, wu, wd, consts):
    TT = min(1024, NT)
    n_tt = NT // TT
    NSUB = TT // 128
    P.new_phase(name)
    with ExitStack() as st:
        hT = sb(nc, st, "hT", [128, NKC, TT], BF16)
        act = sb(nc, st, "act", [128, NF, TT], BF16)
        NWS = 3
        wgu = [sb(nc, st, f"wgu{i}", [128, 2, NKC, 128], BF16) for i in range(NWS)]
        FG = 4
        NDS = 3
        wds = [sb(nc, st, f"wds{i}", [128, FG, 512], BF16) for i in range(NDS)]
        xs = [sb(nc, st, f"xs{i}", [128, D], F32) for i in range(2)]
        xn = [sb(nc, st, f"xn{i}", [128, D], F32) for i in range(1)]
        sg = [sb(nc, st, f"sg{i}", [128, 512], F32) for i in range(2)]
        xres = [sb(nc, st, f"xres{i}", [128, 512], F32) for i in range(4)]
        ost = [sb(nc, st, f"ost{i}", [128, 512], F32) for i in range(4)]
        ssq = [sb(nc, st, f"ssq{i}", [128, 1], F32) for i in range(2)]
        rstd = [sb(nc, st, f"rstd{i}", [128, 1], F32) for i in range(2)]
        gain = sb(nc, st, "gain", [128, NKC], F32)
        pbank = [ps(nc, st, f"pb{i}", [128, 512]) for i in range(8)]
        ident = consts["ident_f32"]
        mhalf = consts["mhalf"]

        P.dma("sync", lambda e: e.dma_start(out=gain.t[:], in_=gain_d[:, :]), writes=[gain.b], sembuf=gain.b)

        wg_v = wg.rearrange("(c p) n -> p c n", p=128)
        wu_v = wu.rearrange("(c p) n -> p c n", p=128)
        wslot = 0
        dslot = 0
        xslot = 0
        rslot = 0
        for tt in range(n_tt):
            t0 = tt * TT
            for s in range(NSUB):
                r0 = t0 + s * 128
                X = xs[xslot % 2]
                SS = ssq[xslot % 2]
                RS = rstd[xslot % 2]
                XN = xn[0]
                xslot += 1
                P.dma("sync", lambda e, X=X, r0=r0: e.dma_start(out=X.t[:], in_=src[r0:r0 + 128, :]),
                      writes=[X.b], sembuf=X.b)
                P.op("scalar", lambda e, X=X, XN=XN, SS=SS: e.activation(out=XN.t[:], in_=X.t[:], func=AF.Square,
                                                                        accum_out=SS.t[:]),
                     reads=[X.b], writes=[XN.b, SS.b])
                P.op("vector", lambda e, SS=SS: e.tensor_scalar(out=SS.t[:], in0=SS.t[:], scalar1=1.0 / D, scalar2=EPS,
                                                               op0=ALU.mult, op1=ALU.add),
                     reads=[SS.b], writes=[SS.b])
                P.op("gpsimd", lambda e, SS=SS, RS=RS: e.tensor_tensor(out=RS.t[:], in0=SS.t[:], in1=mhalf.t[:], op=ALU.pow),
                     reads=[SS.b, mhalf.b], writes=[RS.b])
                P.op("vector", lambda e, X=X, XN=XN, RS=RS: e.tensor_scalar(out=XN.t[:], in0=X.t[:], scalar1=RS.t[:, 0:1],
                                                                           scalar2=None, op0=ALU.mult),
                     reads=[X.b, RS.b], writes=[XN.b])
                for cg in range(4):
                    PB = pbank[6 + (cg % 2)]
                    for ci in range(4):
                        c = cg * 4 + ci
                        P.op("tensor", lambda e, PB=PB, XN=XN, c=c, ci=ci: e.transpose(
                            out=PB.t[:, ci * 128:(ci + 1) * 128], in_=XN.t[:, c * 128:(c + 1) * 128], identity=ident.t[:]),
                            reads=[XN.b, ident.b], writes=[PB.b], same_ok=True)
                    P.op("vector", lambda e, PB=PB, cg=cg, s=s: e.tensor_tensor(
                        out=hT.t[:, cg * 4:(cg + 1) * 4, s * 128:(s + 1) * 128],
                        in0=PB.t[:, :].rearrange("p (c n) -> p c n", c=4),
                        in1=bcast_last(gain.t[:, cg * 4:(cg + 1) * 4], 128), op=ALU.mult),
                        reads=[PB.b, gain.b], writes=[hT.b])
            for f in range(NF):
                W = wgu[wslot % NWS]
                wslot += 1
                P.dma("gpsimd", lambda e, W=W, f=f: e.dma_start(out=W.t[:, 0, :, :], in_=wg_v[:, :, f * 128:(f + 1) * 128]),
                      writes=[W.b], sembuf=W.b)
                P.dma("gpsimd", lambda e, W=W, f=f: e.dma_start(out=W.t[:, 1, :, :], in_=wu_v[:, :, f * 128:(f + 1) * 128]),
                      writes=[W.b], sembuf=W.b)
                for half in range(TT // 512):
                    pset = (f * 2 + half) % 3
                    PG = pbank[2 * pset]
                    PU = pbank[2 * pset + 1]
                    for gi, PBK in ((0, PG), (1, PU)):
                        for k in range(NKC):
                            P.op("tensor", lambda e, PBK=PBK, W=W, gi=gi, k=k, half=half: e.matmul(
                                PBK.t[:, :], lhsT=W.t[:, gi, k, :], rhs=hT.t[:, k, half * 512:(half + 1) * 512],
                                start=(k == 0), stop=(k == NKC - 1)),
                                reads=[W.b, hT.b], writes=[PBK.b], same_ok=True)
                    SG = sg[(f * 2 + half) % 2]
                    P.op("scalar", lambda e, SG=SG, PG=PG: e.activation(out=SG.t[:], in_=PG.t[:, :], func=AF.Silu),
                         reads=[PG.b], writes=[SG.b])
                    P.op("vector", lambda e, SG=SG, PU=PU, f=f, half=half: e.tensor_tensor(
                        out=act.t[:, f, half * 512:(half + 1) * 512], in0=PU.t[:, :], in1=SG.t[:], op=ALU.mult),
                        reads=[SG.b, PU.b], writes=[act.b])
            for n in range(D // 512):
                f = 0
                while f < NF:
                    g = min(FG, NF - f)
                    WD = wds[dslot % NDS]
                    dslot += 1
                    P.dma("gpsimd", lambda e, WD=WD, f=f, g=g, n=n: e.dma_start(
                        out=WD.t[:, 0:g, :],
                        in_=wd[f * 128:(f + g) * 128, n * 512:(n + 1) * 512].rearrange("(g p) n -> p g n", p=128)),
                        writes=[WD.b], sembuf=WD.b)
                    for gi in range(g):
                        ff = f + gi
                        for s in range(NSUB):
                            P.op("tensor", lambda e, s=s, WD=WD, gi=gi, ff=ff: e.matmul(
                                pbank[s].t[:, :], lhsT=act.t[:, ff, s * 128:(s + 1) * 128], rhs=WD.t[:, gi, :],
                                start=(ff == 0), stop=(ff == NF - 1)),
                                reads=[act.b, WD.b], writes=[pbank[s].b], same_ok=True)
                    f += g
                for s in range(NSUB):
                    r0 = t0 + s * 128
                    XR = xres[rslot % 4]
                    OS = ost[rslot % 4]
                    rslot += 1
                    P.dma("sync", lambda e, XR=XR, r0=r0, n=n: e.dma_start(out=XR.t[:], in_=src[r0:r0 + 128, n * 512:(n + 1) * 512]),
                          writes=[XR.b], sembuf=XR.b)
                    P.op("vector", lambda e, OS=OS, XR=XR, s=s: e.scalar_tensor_tensor(
                        out=OS.t[:], in0=pbank[s].t[:, :], scalar=0.5, in1=XR.t[:], op0=ALU.mult, op1=ALU.add),
                        reads=[pbank[s].b, XR.b], writes=[OS.b])
                    P.dma("sync", lambda e, OS=OS, r0=r0, n=n: e.dma_start(out=dst[r0:r0 + 128, n * 512:(n + 1) * 512], in_=OS.t[:]),
                          reads=[OS.b], sembuf=OS.b)
        P.emit()


def norm_transpose(P, nc, src_rows, X, XN, SS, RS, hT, s, gain, consts, pbanks):
    ident = consts["ident_f32"]
    mhalf = consts["mhalf"]
    P.dma("sync", lambda e: e.dma_start(out=X.t[:], in_=src_rows), writes=[X.b], sembuf=X.b)
    P.op("scalar", lambda e: e.activation(out=XN.t[:], in_=X.t[:], func=AF.Square, accum_out=SS.t[:]),
         reads=[X.b], writes=[XN.b, SS.b])
    P.op("vector", lambda e: e.tensor_scalar(out=SS.t[:], in0=SS.t[:], scalar1=1.0 / D, scalar2=EPS,
                                             op0=ALU.mult, op1=ALU.add), reads=[SS.b], writes=[SS.b])
    P.op("gpsimd", lambda e: e.tensor_tensor(out=RS.t[:], in0=SS.t[:], in1=mhalf.t[:], op=ALU.pow),
         reads=[SS.b, mhalf.b], writes=[RS.b])
    P.op("vector", lambda e: e.tensor_scalar(out=XN.t[:], in0=X.t[:], scalar1=RS.t[:, 0:1], scalar2=None, op0=ALU.mult),
         reads=[X.b, RS.b], writes=[XN.b])
    for cg in range(4):
        PB = pbanks[cg % 2]
        for ci in range(4):
            c = cg * 4 + ci
            P.op("tensor", lambda e, PB=PB, c=c, ci=ci: e.transpose(
                out=PB.t[:, ci * 128:(ci + 1) * 128], in_=XN.t[:, c * 128:(c + 1) * 128], identity=ident.t[:]),
                reads=[XN.b, ident.b], writes=[PB.b], same_ok=True)
        P.op("vector", lambda e, PB=PB, cg=cg: e.tensor_tensor(
            out=hT.t[:, cg * 4:(cg + 1) * 4, s * 128:(s + 1) * 128],
            in0=PB.t[:, :].rearrange("p (c n) -> p c n", c=4),
            in1=bcast_last(gain.t[:, cg * 4:(cg + 1) * 4], 128), op=ALU.mult),
            reads=[PB.b, gain.b], writes=[hT.b])


def proj_phase(P, nc, NT, NPREV, src, gain_d, w_in, sc, consts):
    TT = min(1024, NT)
    n_tt = NT // TT
    NSUB = TT // 128
    P.new_phase("proj")
    qscale = 1.0 / float(np.sqrt(128.0))
    with ExitStack() as st:
        hT = sb(nc, st, "hT", [128, NKC, TT], BF16)
        wf = [sb(nc, st, f"wf{i}", [128, NKC, 128], BF16) for i in range(3)]
        wt = [sb(nc, st, f"wt{i}", [128, NKC, 512], BF16) for i in range(3)]
        xs = [sb(nc, st, f"xs{i}", [128, D], F32) for i in range(2)]
        xn = [sb(nc, st, f"xn{i}", [128, D], F32) for i in range(1)]
        ssq = [sb(nc, st, f"ssq{i}", [128, 1], F32) for i in range(2)]
        rstd = [sb(nc, st, f"rstd{i}", [128, 1], F32) for i in range(2)]
        gain = sb(nc, st, "gain", [128, NKC], F32)
        obf = [sb(nc, st, f"obf{i}", [128, 512], BF16) for i in range(4)]
        of32 = [sb(nc, st, f"of32{i}", [128, 512], F32) for i in range(4)]
        pbank = [ps(nc, st, f"pb{i}", [128, 512]) for i in range(8)]
        P.dma("sync", lambda e: e.dma_start(out=gain.t[:], in_=gain_d[:, :]), writes=[gain.b], sembuf=gain.b)
        w_v = w_in.rearrange("(c p) n -> p c n", p=128)
        xslot = 0
        fslot = 0
        tslot = 0
        oslot = 0
        pslot = 0
        tm_blocks = []
        for j in range(2):
            tm_blocks.append((2048 + 512 * j, 512, (lambda r0, j=j: sc["v"][NPREV + r0:NPREV + r0 + 128, 512 * j:512 * (j + 1)]), BF16))
        for j in range(6):
            tm_blocks.append((3072 + 512 * j, 512, (lambda r0, j=j: sc["graw"][3 + r0:3 + r0 + 128, 512 * j:512 * (j + 1)]), F32))
        tm_blocks.append((6144, 16, (lambda r0: sc["ab"][r0:r0 + 128, :]), F32))
        for j in range(2):
            tm_blocks.append((6160 + 512 * j, 512, (lambda r0, j=j: sc["gate"][r0:r0 + 128, 512 * j:512 * (j + 1)]), F32))
        for tt in range(n_tt):
            t0 = tt * TT
            for s in range(NSUB):
                r0 = t0 + s * 128
                i = xslot % 2
                xslot += 1
                norm_transpose(P, nc, src[r0:r0 + 128, :], xs[i], xn[0], ssq[i], rstd[i], hT, s, gain, consts, pbank[6:8])
            for c in range(16):
                W = wf[fslot % 3]
                fslot += 1
                P.dma("gpsimd", lambda e, W=W, c=c: e.dma_start(out=W.t[:], in_=w_v[:, :, c * 128:(c + 1) * 128]),
                      writes=[W.b], sembuf=W.b)
                for half in range(TT // 512):
                    PB = pbank[pslot % 6]
                    pslot += 1
                    for k in range(NKC):
                        P.op("tensor", lambda e, PB=PB, W=W, k=k, half=half: e.matmul(
                            PB.t[:, :], lhsT=W.t[:, k, :], rhs=hT.t[:, k, half * 512:(half + 1) * 512],
                            start=(k == 0), stop=(k == NKC - 1)), reads=[W.b, hT.b], writes=[PB.b], same_ok=True)
                    O = obf[oslot % 4]
                    oslot += 1
                    if c < 8:
                        P.op("scalar", lambda e, O=O, PB=PB: e.activation(out=O.t[:], in_=PB.t[:, :], func=AF.Copy, scale=qscale),
                             reads=[PB.b], writes=[O.b])
                        dstap = sc["qT"][c, :, t0 + half * 512:t0 + (half + 1) * 512]
                    else:
                        P.op("vector", lambda e, O=O, PB=PB: e.tensor_copy(out=O.t[:], in_=PB.t[:, :]),
                             reads=[PB.b], writes=[O.b])
                        dstap = sc["kT"][c - 8, :, NPREV + t0 + half * 512:NPREV + t0 + (half + 1) * 512]
                    P.dma("sync", lambda e, O=O, dstap=dstap: e.dma_start(out=dstap, in_=O.t[:]), reads=[O.b], sembuf=O.b)
            for bi, (c0, ncol, dfn, odt) in enumerate(tm_blocks):
                W = wt[tslot % 3]
                tslot += 1
                P.dma("gpsimd", lambda e, W=W, c0=c0, ncol=ncol: e.dma_start(out=W.t[:, :, 0:ncol], in_=w_v[:, :, c0:c0 + ncol]),
                      writes=[W.b], sembuf=W.b)
                for s in range(NSUB):
                    r0 = t0 + s * 128
                    PB = pbank[pslot % 6]
                    pslot += 1
                    for k in range(NKC):
                        P.op("tensor", lambda e, PB=PB, W=W, k=k, s=s, ncol=ncol: e.matmul(
                            PB.t[:, 0:ncol], lhsT=hT.t[:, k, s * 128:(s + 1) * 128], rhs=W.t[:, k, 0:ncol],
                            start=(k == 0), stop=(k == NKC - 1)), reads=[W.b, hT.b], writes=[PB.b], same_ok=True)
                    O = (obf if odt == BF16 else of32)[oslot % 4]
                    oslot += 1
                    eng = "scalar" if (s % 2 == 0) else "vector"
                    if eng == "scalar":
                        P.op("scalar", lambda e, O=O, PB=PB, ncol=ncol: e.activation(out=O.t[:, 0:ncol], in_=PB.t[:, 0:ncol], func=AF.Copy),
                             reads=[PB.b], writes=[O.b])
                    else:
                        P.op("vector", lambda e, O=O, PB=PB, ncol=ncol: e.tensor_copy(out=O.t[:, 0:ncol], in_=PB.t[:, 0:ncol]),
                             reads=[PB.b], writes=[O.b])
                    P.dma("sync", lambda e, O=O, dstap=dfn(r0), ncol=ncol: e.dma_start(out=dstap, in_=O.t[:, 0:ncol]),
                          reads=[O.b], sembuf=O.b)
        P.emit()


def attn_phase(P, nc, NT, NPREV, sc, cin, consts, xg=None):
    NK = NPREV + NT
    NKB = NK // 128
    NPB = NPREV // 128
    NG = NT // 512
    P.new_phase("attn")
    with ExitStack() as st:
        NHB = 4
        KT = [sb(nc, st, f"KT{i}", [128, NK], BF16) for i in range(NHB)]
        VV = [sb(nc, st, f"VV{i}", [128, NKB, 128], BF16) for i in range(NHB)]
        QT = [sb(nc, st, f"QT{i}", [128, NT], BF16) for i in range(NHB)]
        trineg = sb(nc, st, "trineg", [128, 128], BF16)
        tricomp = sb(nc, st, "tricomp", [128, 128], BF16)
        ones32 = sb(nc, st, "ones32", [128, 128], F32)
        masks = sb(nc, st, "masks", [128, 4, 512], F32)
        negb = sb(nc, st, "negb", [128, 1], F32)
        gsb = sb(nc, st, "gsb", [128, 1], F32)
        NS = 2
        E = [[sb(nc, st, f"E{s_}_{i}", [128, 512], F32) for i in range(3)] for s_ in range(NS)]
        SP = [[sb(nc, st, f"SP{s_}_{i}", [128, 512], BF16) for i in range(3)] for s_ in range(NS)]
        EC = [[sb(nc, st, f"EC{s_}_{i}", [128, 512], F32) for i in range(3)] for s_ in range(NS)]
        W = [[sb(nc, st, f"W{s_}_{i}", [128, 512], BF16) for i in range(3)] for s_ in range(NS)]
        SQ = [sb(nc, st, f"SQ{s_}", [128, 512], F32) for s_ in range(NS)]
        R = [sb(nc, st, f"R{s_}", [128, 512], F32) for s_ in range(NS)]
        Y = [[sb(nc, st, f"Y{s_}_{i}", [128, 512], BF16) for i in range(2)] for s_ in range(NS)]
        Zp = [[ps(nc, st, f"Zp{s_}_{i}", [128, 512]) for i in range(2)] for s_ in range(NS)]
        Cp = [ps(nc, st, f"Cp{s_}", [128, 512]) for s_ in range(NS)]
        OT = [ps(nc, st, f"OT{s_}", [128, 512]) for s_ in range(NS)]

        P.dma("gpsimd", lambda e: e.dma_start(out=trineg.t[:], in_=cin["trineg"][:, :]), writes=[trineg.b], sembuf=trineg.b)
        P.dma("gpsimd", lambda e: e.dma_start(out=tricomp.t[:], in_=cin["tricomp"][:, :]), writes=[tricomp.b], sembuf=tricomp.b)
        P.dma("sync", lambda e: e.dma_start(out=ones32.t[:], in_=cin["ones32"][:, :]), writes=[ones32.b], sembuf=ones32.b)
        P.dma("sync", lambda e: e.dma_start(out=masks.t[:], in_=cin["dmask"].rearrange("r p n -> p r n")), writes=[masks.b], sembuf=masks.b)
        P.dma("sync", lambda e: e.dma_start(out=negb.t[:], in_=cin["negbias"][:, :]), writes=[negb.b], sembuf=negb.b)
        P.dma("sync", lambda e: e.dma_start(out=gsb.t[:], in_=cin["sb_out_norm"][:, :]), writes=[gsb.b], sembuf=gsb.b)

        def load_head(h):
            K_, V_, Q_ = KT[h % NHB], VV[h % NHB], QT[h % NHB]
            if xg is None:
                P.dma("sync", lambda e: e.dma_start(out=K_.t[:], in_=sc["kT"][h, :, :]), writes=[K_.b], sembuf=K_.b)
                P.dma("sync", lambda e: e.dma_start(
                    out=V_.t[:], in_=sc["v"][:, h * 128:(h + 1) * 128].rearrange("(b s) d -> s b d", s=128)),
                    writes=[V_.b], sembuf=V_.b)
            else:
                P.dma("sync", lambda e: e.dma_start(out=K_.t[:, 0:NPREV], in_=xg["kd"][h // 4][(h % 4) * 128:(h % 4 + 1) * 128, :]), writes=[K_.b], sembuf=K_.b)
                P.dma("sync", lambda e: e.dma_start(out=K_.t[:, NPREV:NK], in_=sc["kT"][h, :, :]), writes=[K_.b], sembuf=K_.b)
                HVB = xg["HV"] // 128
                for i in range(2):
                    P.dma("sync", lambda e, i=i: e.dma_start(
                        out=V_.t[:, i * HVB:(i + 1) * HVB, :], in_=xg["vd"][i][0:xg["HV"], h * 128:(h + 1) * 128].rearrange("(b s) d -> s b d", s=128)),
                        writes=[V_.b], sembuf=V_.b)
                P.dma("sync", lambda e: e.dma_start(
                    out=V_.t[:, NPB:NKB, :], in_=sc["v"][:, h * 128:(h + 1) * 128].rearrange("(b s) d -> s b d", s=128)),
                    writes=[V_.b], sembuf=V_.b)
            P.dma("sync", lambda e: e.dma_start(out=Q_.t[:], in_=sc["qT"][h, :, :]), writes=[Q_.b], sembuf=Q_.b)

        class Stream:
            pass

        def mk_stream(sid, h, G):
            S_ = Stream()
            S_.sid, S_.h, S_.G = sid, h, G
            S_.K, S_.V, S_.Q = KT[h % NHB], VV[h % NHB], QT[h % NHB]
            S_.g0 = G * 512
            steps = []
            for r in (3, 2, 1, 0):
                steps.append((NPB + G * 4 + r, r, False))
            for m in range(G * 4 - 1, -1, -1):
                steps.append((NPB + m, None, False))
            for m in range(NPB - 1, -1, -1):
                steps.append((m, None, True))
            S_.steps = steps
            S_.ns = len(steps)
            S_.bufs = {}
            S_.cz = S_.ce = S_.csp = S_.cec = S_.cw = 0
            return S_

        def S1z(S_, i):
            kb, r, isprev = S_.steps[i]
            sid = S_.sid
            Z = Zp[sid][S_.cz % 2]; S_.cz += 1
            K_, Q_, g0 = S_.K, S_.Q, S_.g0
            P.op("tensor", lambda e: e.matmul(Z.t[:, :], lhsT=K_.t[:, kb * 128:(kb + 1) * 128], rhs=Q_.t[:, g0:g0 + 512],
                                              start=True, stop=True), reads=[K_.b, Q_.b], writes=[Z.b], same_ok=True)
            S_.bufs[i] = [None, None, None, Z]

        def S1e(S_, i):
            kb, r, isprev = S_.steps[i]
            sid = S_.sid
            Z = S_.bufs[i][3]
            Ei = E[sid][S_.ce % 3]; S_.ce += 1
            if isprev:
                P.op("scalar", lambda e: e.activation(out=Ei.t[:], in_=Z.t[:, :], func=AF.Exp, bias=negb.t[:, 0:1]),
                     reads=[Z.b, negb.b], writes=[Ei.b])
            else:
                P.op("scalar", lambda e: e.activation(out=Ei.t[:], in_=Z.t[:, :], func=AF.Exp), reads=[Z.b], writes=[Ei.b])
            if r is not None:
                P.op("vector", lambda e: e.tensor_tensor(out=Ei.t[:], in0=Ei.t[:], in1=masks.t[:, r, :], op=ALU.mult),
                     reads=[Ei.b, masks.b], writes=[Ei.b])
            S_.bufs[i][0] = Ei

        def S1sp(S_, i):
            sid = S_.sid
            Ei = S_.bufs[i][0]
            SPi = SP[sid][S_.csp % 3]; S_.csp += 1
            P.op("scalar", lambda e: e.activation(out=SPi.t[:], in_=Ei.t[:], func=AF.Ln, bias=1.0),
                 reads=[Ei.b], writes=[SPi.b])
            S_.bufs[i][1] = SPi

        def S2a(S_, i, part):
            Ei, SPi = S_.bufs[i][0], S_.bufs[i][1]
            sid = S_.sid
            C = Cp[sid]
            if part == 0:
                ECi = EC[sid][S_.cec % 3]; S_.cec += 1
                Wi = W[sid][S_.cw % 3]; S_.cw += 1
            last = (i == S_.ns - 1)
            if part == 0:
                P.op("tensor", lambda e: e.matmul(C.t[:, :], lhsT=trineg.t[:], rhs=SPi.t[:], start=(i == 0), stop=last),
                     reads=[trineg.b, SPi.b], writes=[C.b], same_ok=True)
                S_.bufs[i].append((ECi, Wi))
                return
            ECi, Wi = S_.bufs[i][4]
            P.op("scalar", lambda e: e.activation(out=ECi.t[:], in_=C.t[:, :], func=AF.Exp), reads=[C.b], writes=[ECi.b])
            P.op("vector", lambda e: e.tensor_tensor(out=Wi.t[:], in0=Ei.t[:], in1=ECi.t[:], op=ALU.mult),
                 reads=[Ei.b, ECi.b], writes=[Wi.b])
            S_.bufs[i][2] = Wi

        def S2b(S_, i):
            if i == S_.ns - 1:
                return
            Ei, SPi = S_.bufs[i][0], S_.bufs[i][1]
            C = Cp[S_.sid]
            P.op("tensor", lambda e: e.matmul(C.t[:, :], lhsT=tricomp.t[:], rhs=SPi.t[:], start=False, stop=False),
                 reads=[tricomp.b, SPi.b], writes=[C.b], same_ok=True)

        def S3(S_, i):
            kb, r, isprev = S_.steps[i]
            Wi = S_.bufs[i][2]
            OTg, V_, ns = OT[S_.sid], S_.V, S_.ns
            P.op("tensor", lambda e: e.matmul(OTg.t[:, :], lhsT=V_.t[:, kb, :], rhs=Wi.t[:], start=(i == 0), stop=(i == ns - 1)),
                 reads=[V_.b, Wi.b], writes=[OTg.b], same_ok=True)
            del S_.bufs[i]

        def finish(S_):
            sid, h, g0, G = S_.sid, S_.h, S_.g0, S_.G
            OTg = OT[sid]
            Yg = Y[sid][G % 2]
            Zs = Zp[sid][S_.cz % 2]; S_.cz += 1
            SQ_, R_ = SQ[sid], R[sid]
            P.op("scalar", lambda e: e.activation(out=SQ_.t[:], in_=OTg.t[:, :], func=AF.Square), reads=[OTg.b], writes=[SQ_.b])
            P.op("tensor", lambda e: e.matmul(Zs.t[:, :], lhsT=ones32.t[:], rhs=SQ_.t[:], start=True, stop=True),
                 reads=[ones32.b, SQ_.b], writes=[Zs.b], same_ok=True)
            P.op("vector", lambda e: e.tensor_scalar(out=R_.t[:], in0=Zs.t[:, :], scalar1=1.0 / 128.0, scalar2=EPS,
                                                     op0=ALU.mult, op1=ALU.add), reads=[Zs.b], writes=[R_.b])
            P.op("scalar", lambda e: e.activation(out=R_.t[:], in_=R_.t[:], func=AF.Ln), reads=[R_.b], writes=[R_.b])
            P.op("scalar", lambda e: e.activation(out=R_.t[:], in_=R_.t[:], func=AF.Exp, scale=-0.5), reads=[R_.b], writes=[R_.b])
            P.op("vector", lambda e: e.scalar_tensor_tensor(
                out=Yg.t[:], in0=OTg.t[:, :], scalar=gsb.t[:, 0:1], in1=R_.t[:], op0=ALU.mult, op1=ALU.mult),
                reads=[OTg.b, gsb.b, R_.b], writes=[Yg.b])
            P.dma("sync", lambda e: e.dma_start(out=sc["yT"][h, :, g0:g0 + 512], in_=Yg.t[:]), reads=[Yg.b], sembuf=Yg.b)

        load_head(0)
        load_head(1)
        for hp in range(4):
            if hp + 1 < 4:
                load_head(2 * hp + 2)
                load_head(2 * hp + 3)
            for G in range(NG):
                strs = [mk_stream(0, 2 * hp, G), mk_stream(1, 2 * hp + 1, G)]
                ns = strs[0].ns
                for S_ in strs:
                    S1z(S_, 0)
                if ns > 1:
                    for S_ in strs:
                        S1z(S_, 1)
                for S_ in strs:
                    S1e(S_, 0)
                for i in range(ns):
                    if i + 2 < ns:
                        for S_ in strs:
                            S1z(S_, i + 2)
                    if i >= 1:
                        for S_ in strs:
                            S3(S_, i - 1)
                    for S_ in strs:
                        S1sp(S_, i)
                    for S_ in strs:
                        S2a(S_, i, 0)
                    if i + 1 < ns:
                        for S_ in strs:
                            S1e(S_, i + 1)
                    for S_ in strs:
                        S2a(S_, i, 1)
                    for S_ in strs:
                        S2b(S_, i)
                for S_ in strs:
                    S3(S_, ns - 1)
                for S_ in strs:
                    finish(S_)
        P.emit()


FP32R = False


def mm32(e, out, lhsT, rhs, **kw):
    if FP32R and lhsT.dtype == F32 and rhs.dtype == F32:
        lhsT = lhsT.bitcast(mybir.dt.float32r)
        rhs = rhs.bitcast(mybir.dt.float32r)
    return e.matmul(out, lhsT=lhsT, rhs=rhs, **kw)


class PQ:
    def __init__(self, bank, q, buf):
        self.bank = bank
        self.q = q
        self.b = buf

    def ap(self, rows=128, cols=128):
        return self.bank[0:rows, self.q * 128:self.q * 128 + cols]


def gdn_prep_phase(P, nc, NT, sc, cin, consts, egl_all, xg=None):
    NTILE = NT // 128
    P.new_phase("gprep")
    ident = consts["ident_f32"]
    with ExitStack() as st:
        convw = sb(nc, st, "convw", [128, 4, 3072], F32)
        alog = sb(nc, st, "alog", [128, 8], F32)
        dtb = sb(nc, st, "dtb", [128, 8], F32)
        nega = sb(nc, st, "nega", [128, 8], F32)
        BT = sb(nc, st, "BT", [128, 128], F32)
        BL = sb(nc, st, "BL", [128, 128], F32)
        selC = sb(nc, st, "selC", [128, 2, 128], F32)
        zero3 = sb(nc, st, "zero3", [3, 3072], F32)
        XJ = [[sb(nc, st, f"XJ{i}_{j}", [128, 1024], F32) for j in range(4)] for i in range(2)]
        acc = sb(nc, st, "acc", [128, 1024], F32)
        tmp = sb(nc, st, "tmp", [128, 1024], F32)
        tmp2 = sb(nc, st, "tmp2", [128, 1024], F32)
        acc2 = sb(nc, st, "acc2", [128, 1024], F32)
        qn2 = [sb(nc, st, f"qn{i}", [128, 8, 128], F32) for i in range(2)]
        kn2 = [sb(nc, st, f"kn{i}", [128, 8, 128], F32) for i in range(2)]
        vs2 = [sb(nc, st, f"vs{i}", [128, 8, 128], F32) for i in range(2)]
        qg = sb(nc, st, "qg", [128, 8, 128], F32)
        kbg = sb(nc, st, "kbg", [128, 8, 128], BF16)
        vb = sb(nc, st, "vb", [128, 8, 128], BF16)
        kT = sb(nc, st, "kT", [128, 8, 128], F32)
        qT = sb(nc, st, "qT", [128, 8, 128], F32)
        qgT = sb(nc, st, "qgT", [128, 8, 128], BF16)
        A_all = sb(nc, st, "A_all", [128, 8, 128], F32)
        aqk_all = sb(nc, st, "aqk_all", [64, 2, 8, 64], BF16)
        kend_all = sb(nc, st, "kend_all", [64, 2, 8, 128], BF16)
        DmS = [sb(nc, st, f"DmS{i}", [128, 128], F32) for i in range(3)]
        DmT = [sb(nc, st, f"DmT{i}", [64, 64], F32) for i in range(3)]
        Gbb = sb(nc, st, "Gbb", [128, 8, 128], F32)
        MS01 = sb(nc, st, "MS01", [128, 128], F32)
        ML01 = sb(nc, st, "ML01", [64, 64], F32)
        gcrow_buf = Buf("gcrow")
        abt = sb(nc, st, "abt", [128, 16], F32)
        sm = {n: sb(nc, st, n, [128, 8], F32) for n in ("g", "beta", "gc", "eg", "ekend", "bke", "ssq", "rq", "rk", "t8")}
        smC = {n: sb(nc, st, n, [64, 2, 8], F32) for n in ("ngcC", "ekendC", "tC")}
        gcT = sb(nc, st, "gcT", [8, 128], F32)
        mh8 = sb(nc, st, "mh8", [128, 8], F32)
        banks = [st.enter_context(nc.psum_tensor(f"gp_bank{i}", [128, 512], F32)) for i in range(8)]
        bankbufs = [Buf(f"bank{i}") for i in range(8)]
        pq = [PQ(banks[i % 8], i // 8, bankbufs[i % 8]) for i in range(32)]
        pqi = [0]

        def nextpq():
            p = pq[pqi[0] % 32]
            pqi[0] += 1
            return p

        ld = lambda t, src, eng="sync": P.dma(eng, lambda e: e.dma_start(out=t.t[:], in_=src), writes=[t.b], sembuf=t.b)
        ld(convw, cin["convw_b"].rearrange("p (j c) -> p j c", j=4))
        ld(alog, cin["alog_b"][:, :])
        ld(dtb, cin["dtb_b"][:, :])
        ld(BT, cin["BT"][:, :])
        ld(BL, cin["BL"][:, :])
        ld(selC, cin["selC"].rearrange("p (c n) -> p c n", c=2))
        ld(MS01, cin["MS01"][:, :])
        ld(ML01, cin["ML01"][:, :])
        if xg is None:
            P.op("vector", lambda e: e.memset(zero3.t[:], 0.0), writes=[zero3.b])
        P.op("vector", lambda e: e.memset(mh8.t[:], -0.5), writes=[mh8.b])
        hist = Buf("hist")
        if xg is None:
            P.dma("sync", lambda e: e.dma_start(out=sc["graw"][0:3, :], in_=zero3.t[:]), reads=[zero3.b], writes=[hist], sembuf=zero3.b)
        else:
            flag3 = sb(nc, st, "flag3", [3, 1], F32)
            P.dma("sync", lambda e: e.dma_start(out=flag3.t[:], in_=cin["flag01"][0:3, :]), writes=[flag3.b], sembuf=flag3.b)
            P.dma("sync", lambda e: e.dma_start(out=zero3.t[:], in_=xg["hist_dst"][0:3, :]), writes=[zero3.b], sembuf=zero3.b)
            P.op("vector", lambda e: e.tensor_scalar(out=zero3.t[:], in0=zero3.t[:], scalar1=flag3.t[:, 0:1], scalar2=None, op0=ALU.mult),
                 reads=[zero3.b, flag3.b], writes=[zero3.b])
            P.dma("sync", lambda e: e.dma_start(out=sc["graw"][0:3, :], in_=zero3.t[:]), reads=[zero3.b], writes=[hist], sembuf=zero3.b)
        P.op("scalar", lambda e: e.activation(out=nega.t[:], in_=alog.t[:], func=AF.Exp), reads=[alog.b], writes=[nega.b])
        P.op("vector", lambda e: e.tensor_scalar(out=nega.t[:], in0=nega.t[:], scalar1=-1.0, scalar2=None, op0=ALU.mult),
             reads=[nega.b], writes=[nega.b])
        qs = 1.0 / float(np.sqrt(128.0))
        P.emit_barrier_needed = True
        g, beta, gc, eg, ekend, bke, t8 = (sm[n] for n in ("g", "beta", "gc", "eg", "ekend", "bke", "t8"))
        ngcC, ekendC, tC = smC["ngcC"], smC["ekendC"], smC["tC"]

        def s1(tb, third, dstt):
            r0 = tb * 128
            X = XJ[(tb * 3 + third) % 2]
            c0 = third * 1024
            for j in range(4):
                P.dma("sync", lambda e, X=X, j=j, r0=r0, c0=c0: e.dma_start(out=X[j].t[:], in_=sc["graw"][r0 + j:r0 + j + 128, c0:c0 + 1024]),
                      reads=([hist] if tb == 0 else []), writes=[X[j].b], sembuf=X[j].b)
            P.op("gpsimd", lambda e, X=X, c0=c0: e.tensor_tensor(out=tmp.t[:], in0=X[1].t[:], in1=convw.t[:, 1, c0:c0 + 1024], op=ALU.mult),
                 reads=[X[1].b, convw.b], writes=[tmp.b])
            P.op("gpsimd", lambda e, X=X, c0=c0: e.tensor_tensor(out=tmp2.t[:], in0=X[2].t[:], in1=convw.t[:, 2, c0:c0 + 1024], op=ALU.mult),
                 reads=[X[2].b, convw.b], writes=[tmp2.b])
            P.op("vector", lambda e, X=X, c0=c0: e.tensor_tensor(out=acc.t[:], in0=X[0].t[:], in1=convw.t[:, 0, c0:c0 + 1024], op=ALU.mult),
                 reads=[X[0].b, convw.b], writes=[acc.b])
            P.op("vector", lambda e, X=X, c0=c0: e.tensor_tensor(out=acc2.t[:], in0=X[3].t[:], in1=convw.t[:, 3, c0:c0 + 1024], op=ALU.mult),
                 reads=[X[3].b, convw.b], writes=[acc2.b])
            P.op("vector", lambda e: e.tensor_tensor(out=acc.t[:], in0=acc.t[:], in1=acc2.t[:], op=ALU.add), reads=[acc.b, acc2.b], writes=[acc.b])
            P.op("vector", lambda e: e.tensor_tensor(out=acc.t[:], in0=acc.t[:], in1=tmp.t[:], op=ALU.add), reads=[acc.b, tmp.b], writes=[acc.b])
            P.op("vector", lambda e: e.tensor_tensor(out=acc.t[:], in0=acc.t[:], in1=tmp2.t[:], op=ALU.add), reads=[acc.b, tmp2.b], writes=[acc.b])
            dflat = dstt.t[:].rearrange("p h d -> p (h d)")
            P.op("scalar", lambda e, dflat=dflat: e.activation(out=dflat, in_=acc.t[:], func=AF.Silu), reads=[acc.b], writes=[dstt.b])
            if third < 2:
                rr = sm["rq"] if third == 0 else sm["rk"]
                P.op("gpsimd", lambda e, dflat=dflat: e.tensor_tensor(out=tmp.t[:], in0=dflat, in1=dflat, op=ALU.mult),
                     reads=[dstt.b], writes=[tmp.b])
                P.op("vector", lambda e: e.tensor_reduce(out=sm["ssq"].t[:], in_=tmp.t[:].rearrange("p (h d) -> p h d", h=8),
                                                         axis=AX.X, op=ALU.add), reads=[tmp.b], writes=[sm["ssq"].b])
                P.op("vector", lambda e: e.tensor_scalar(out=sm["ssq"].t[:], in0=sm["ssq"].t[:], scalar1=EPS, scalar2=None, op0=ALU.add),
                     reads=[sm["ssq"].b], writes=[sm["ssq"].b])
                P.op("gpsimd", lambda e, rr=rr: e.tensor_tensor(out=rr.t[:], in0=sm["ssq"].t[:], in1=mh8.t[:], op=ALU.pow),
                     reads=[sm["ssq"].b, mh8.b], writes=[rr.b])
                if third == 0:
                    P.op("vector", lambda e, rr=rr: e.tensor_scalar(out=rr.t[:], in0=rr.t[:], scalar1=qs, scalar2=None, op0=ALU.mult),
                         reads=[rr.b], writes=[rr.b])
                P.op("vector", lambda e, dstt=dstt, rr=rr: e.tensor_tensor(out=dstt.t[:], in0=dstt.t[:], in1=bcast_last(rr.t[:, :], 128), op=ALU.mult),
                     reads=[dstt.b, rr.b], writes=[dstt.b])

        def s2a(tb, qn, kn, vs):
            r0 = tb * 128
            P.dma("sync", lambda e, r0=r0: e.dma_start(out=abt.t[:], in_=sc["ab"][r0:r0 + 128, :]), writes=[abt.b], sembuf=abt.b)
            g, beta, gc, eg, ekend, bke, t8 = (sm[n] for n in ("g", "beta", "gc", "eg", "ekend", "bke", "t8"))
            P.op("vector", lambda e: e.tensor_tensor(out=t8.t[:], in0=abt.t[:, 0:8], in1=dtb.t[:], op=ALU.add), reads=[abt.b, dtb.b], writes=[t8.b])
            P.op("scalar", lambda e: e.activation(out=t8.t[:], in_=t8.t[:], func=AF.Exp), reads=[t8.b], writes=[t8.b])
            P.op("scalar", lambda e: e.activation(out=t8.t[:], in_=t8.t[:], func=AF.Ln, bias=1.0), reads=[t8.b], writes=[t8.b])
            P.op("vector", lambda e: e.tensor_tensor(out=g.t[:], in0=t8.t[:], in1=nega.t[:], op=ALU.mult), reads=[t8.b, nega.b], writes=[g.b])
            P.op("scalar", lambda e: e.activation(out=beta.t[:], in_=abt.t[:, 8:16], func=AF.Exp, scale=-1.0), reads=[abt.b], writes=[beta.b])
            P.op("vector", lambda e: e.tensor_scalar(out=beta.t[:], in0=beta.t[:], scalar1=1.0, scalar2=None, op0=ALU.add), reads=[beta.b], writes=[beta.b])
            P.op("vector", lambda e: e.reciprocal(out=beta.t[:], in_=beta.t[:]), reads=[beta.b], writes=[beta.b])
            p_gc, p_gl, p_gcT = nextpq(), nextpq(), nextpq()
            P.op("tensor", lambda e, p=p_gc: mm32(e, p.ap(128, 8), lhsT=BT.t[:], rhs=g.t[:], start=True, stop=True), reads=[BT.b, g.b], writes=[p_gc.b], same_ok=True)
            P.op("tensor", lambda e, p=p_gl: mm32(e, p.ap(128, 8), lhsT=BL.t[:], rhs=g.t[:], start=True, stop=True), reads=[BL.b, g.b], writes=[p_gl.b], same_ok=True)
            P.op("tensor", lambda e, p=p_gcT: mm32(e, p.ap(8, 128), lhsT=g.t[:], rhs=BT.t[:], start=True, stop=True), reads=[BT.b, g.b], writes=[p_gcT.b], same_ok=True)
            P.op("vector", lambda e, p=p_gc: e.tensor_copy(out=gc.t[:], in_=p.ap(128, 8)), reads=[p_gc.b], writes=[gc.b])
            P.op("scalar", lambda e, p=p_gc: e.activation(out=eg.t[:], in_=p.ap(128, 8), func=AF.Exp), reads=[p_gc.b], writes=[eg.b])
            P.op("vector", lambda e, p=p_gl: e.tensor_tensor(out=ekend.t[:], in0=p.ap(128, 8), in1=gc.t[:], op=ALU.subtract), reads=[p_gl.b, gc.b], writes=[ekend.b])
            P.op("scalar", lambda e: e.activation(out=ekend.t[:], in_=ekend.t[:], func=AF.Exp), reads=[ekend.b], writes=[ekend.b])
            P.op("vector", lambda e, p=p_gcT: e.tensor_copy(out=gcT.t[:], in_=p.ap(8, 128)), reads=[p_gcT.b], writes=[gcT.b])
            P.dma("sync", lambda e, tb=tb: e.dma_start(out=sc["gcrow"][tb], in_=gcT.t[:]), reads=[gcT.b], writes=[gcrow_buf], sembuf=gcrow_buf)

            def ldb(e, tb=tb):
                src = sc["gcrow"][tb]
                bsrc = bass.AP(src.tensor, src.offset, [[0, 128], [128, 8], [1, 128]])
                return e.dma_start(out=Gbb.t[:], in_=bsrc)
            P.dma("sync", ldb, reads=[gcrow_buf], writes=[Gbb.b], sembuf=Gbb.b)
            P.op("vector", lambda e: e.tensor_tensor(out=bke.t[:], in0=beta.t[:], in1=eg.t[:], op=ALU.mult), reads=[beta.b, eg.b], writes=[bke.b])
            ngcC, ekendC, tC = smC["ngcC"], smC["ekendC"], smC["tC"]
            for c in range(2):
                pc1, pc2, pc3 = nextpq(), nextpq(), nextpq()
                P.op("tensor", lambda e, p=pc1, c=c: mm32(e, p.ap(64, 8), lhsT=BT.t[:, c * 64:(c + 1) * 64], rhs=g.t[:], start=True, stop=True),
                     reads=[BT.b, g.b], writes=[pc1.b], same_ok=True)
                P.op("tensor", lambda e, p=pc2, c=c: mm32(e, p.ap(64, 8), lhsT=BL.t[:, c * 64:(c + 1) * 64], rhs=g.t[:], start=True, stop=True),
                     reads=[BL.b, g.b], writes=[pc2.b], same_ok=True)
                P.op("tensor", lambda e, p=pc3, c=c: mm32(e, p.ap(128, 8), lhsT=selC.t[:, c, :], rhs=g.t[:], start=True, stop=True),
                     reads=[selC.b, g.b], writes=[pc3.b], same_ok=True)
                P.op("vector", lambda e, p=pc1, c=c: e.tensor_scalar(out=ngcC.t[:, c, :], in0=p.ap(64, 8), scalar1=-1.0, scalar2=None, op0=ALU.mult),
                     reads=[pc1.b], writes=[ngcC.b])
                P.op("vector", lambda e, p=pc2, c=c: e.tensor_tensor(out=tC.t[:, c, :], in0=p.ap(64, 8), in1=ngcC.t[:, c, :], op=ALU.add),
                     reads=[pc2.b, ngcC.b], writes=[tC.b])
                P.op("scalar", lambda e, c=c: e.activation(out=ekendC.t[:, c, :], in_=tC.t[:, c, :], func=AF.Exp), reads=[tC.b], writes=[ekendC.b])
                P.op("scalar", lambda e, p=pc3, c=c, tb=tb: e.activation(out=egl_all.t[:, tb * 2 + c, :], in_=p.ap(128, 8), func=AF.Exp),
                     reads=[pc3.b], writes=[egl_all.b])
            P.op("vector", lambda e: e.tensor_tensor(out=kbg.t[:], in0=kn.t[:], in1=bcast_last(bke.t[:, :], 128), op=ALU.mult), reads=[kn.b, bke.b], writes=[kbg.b])
            P.op("gpsimd", lambda e: e.tensor_tensor(out=vb.t[:], in0=vs.t[:], in1=bcast_last(beta.t[:, :], 128), op=ALU.mult), reads=[vs.b, beta.b], writes=[vb.b])
            P.op("vector", lambda e: e.tensor_tensor(out=qg.t[:], in0=qn.t[:], in1=bcast_last(eg.t[:, :], 128), op=ALU.mult), reads=[qn.b, eg.b], writes=[qg.b])
            P.dma("sync", lambda e, r0=r0: e.dma_start(out=sc["kbg"][r0:r0 + 128, :], in_=kbg.t[:].rearrange("p h d -> p (h d)")), reads=[kbg.b], sembuf=kbg.b)
            P.dma("sync", lambda e, r0=r0: e.dma_start(out=sc["vb"][r0:r0 + 128, :], in_=vb.t[:].rearrange("p h d -> p (h d)")), reads=[vb.b], sembuf=vb.b)
            for srct, dstT in ((kn, kT), (qn, qT), (qg, qgT)):
                for h in range(8):
                    p = nextpq()
                    P.op("tensor", lambda e, p=p, srct=srct, h=h: e.transpose(out=p.ap(), in_=srct.t[:, h, :], identity=ident.t[:]),
                         reads=[srct.b, ident.b], writes=[p.b], same_ok=True)
                    eng = "scalar" if h % 2 == 0 else "vector"
                    if eng == "scalar":
                        P.op("scalar", lambda e, p=p, dstT=dstT, h=h: e.activation(out=dstT.t[:, h, :], in_=p.ap(), func=AF.Copy), reads=[p.b], writes=[dstT.b])
                    else:
                        P.op("vector", lambda e, p=p, dstT=dstT, h=h: e.tensor_copy(out=dstT.t[:, h, :], in_=p.ap()), reads=[p.b], writes=[dstT.b])
            P.dma("sync", lambda e, tb=tb: e.dma_start(out=sc["qgT"][tb], in_=qgT.t[:]), reads=[qgT.b], sembuf=qgT.b)

        def s2h(tb, h, qn, kn, vs):
            r0 = tb * 128
            pkk = nextpq()
            P.op("tensor", lambda e, p=pkk, h=h: mm32(e, p.ap(), lhsT=kT.t[:, h, :], rhs=kT.t[:, h, :], start=True, stop=True),
                 reads=[kT.b], writes=[pkk.b], same_ok=True)
            DS = DmS[h % 3]
            P.op("vector", lambda e, DS=DS, h=h: e.tensor_scalar(out=DS.t[:], in0=Gbb.t[:, h, :], scalar1=gc.t[:, h:h + 1], scalar2=0.0,
                                                                op0=ALU.subtract, op1=ALU.max),
                 reads=[Gbb.b, gc.b], writes=[DS.b])
            P.op("scalar", lambda e, DS=DS: e.activation(out=DS.t[:], in_=DS.t[:], func=AF.Exp, scale=-1.0), reads=[DS.b], writes=[DS.b])
            P.op("gpsimd", lambda e, DS=DS: e.tensor_tensor(out=DS.t[:], in0=DS.t[:], in1=MS01.t[:], op=ALU.mult), reads=[DS.b, MS01.b], writes=[DS.b])
            P.op("vector", lambda e, p=pkk, DS=DS, h=h: e.scalar_tensor_tensor(out=A_all.t[:, h, :], in0=p.ap(), scalar=beta.t[:, h:h + 1], in1=DS.t[:],
                                                                             op0=ALU.mult, op1=ALU.mult),
                 reads=[pkk.b, DS.b, beta.b], writes=[A_all.b])
            for c in range(2):
                pkq, pkc = nextpq(), nextpq()
                cs = slice(c * 64, (c + 1) * 64)
                P.op("tensor", lambda e, p=pkq, h=h, cs=cs: mm32(e, p.ap(64, 64), lhsT=kT.t[:, h, cs], rhs=qT.t[:, h, cs], start=True, stop=True),
                     reads=[kT.b, qT.b], writes=[pkq.b], same_ok=True)
                DT = DmT[(h * 2 + c) % 3]
                P.op("vector", lambda e, DT=DT, h=h, c=c, cs=cs: e.tensor_scalar(out=DT.t[:], in0=Gbb.t[0:64, h, cs], scalar1=ngcC.t[:, c, h:h + 1], scalar2=0.0,
                                                                                op0=ALU.add, op1=ALU.min),
                     reads=[Gbb.b, ngcC.b], writes=[DT.b])
                P.op("scalar", lambda e, DT=DT: e.activation(out=DT.t[:], in_=DT.t[:], func=AF.Exp), reads=[DT.b], writes=[DT.b])
                P.op("gpsimd", lambda e, DT=DT: e.tensor_tensor(out=DT.t[:], in0=DT.t[:], in1=ML01.t[:], op=ALU.mult), reads=[DT.b, ML01.b], writes=[DT.b])
                P.op("vector", lambda e, p=pkq, DT=DT, h=h, c=c: e.tensor_tensor(out=aqk_all.t[:, c, h, :], in0=p.ap(64, 64), in1=DT.t[:], op=ALU.mult),
                     reads=[pkq.b, DT.b], writes=[aqk_all.b])
                P.op("tensor", lambda e, p=pkc, h=h, cs=cs: e.transpose(out=p.ap(64, 128), in_=kT.t[:, h, cs], identity=ident.t[:]),
                     reads=[kT.b, ident.b], writes=[pkc.b], same_ok=True)
                P.op("vector", lambda e, p=pkc, h=h, c=c: e.tensor_scalar(out=kend_all.t[:, c, h, :], in0=p.ap(64, 128), scalar1=ekendC.t[:, c, h:h + 1],
                                                                         scalar2=None, op0=ALU.mult),
                     reads=[pkc.b, ekendC.b], writes=[kend_all.b])

        def s2z(tb):
            r0 = tb * 128
            for c in range(2):
                cg = tb * 2 + c
                P.dma("sync", lambda e, c=c, cg=cg: e.dma_start(out=sc["A"][cg].rearrange("h i j -> i h j"), in_=A_all.t[c * 64:(c + 1) * 64, :, c * 64:(c + 1) * 64]),
                      reads=[A_all.b], sembuf=A_all.b)
            P.dma("sync", lambda e, tb=tb: e.dma_start(out=sc["aqkT"][tb * 2:tb * 2 + 2].rearrange("c j h i -> j c h i"), in_=aqk_all.t[:]),
                  reads=[aqk_all.b], sembuf=aqk_all.b)
            P.dma("sync", lambda e, tb=tb: e.dma_start(out=sc["kend"][tb * 2:tb * 2 + 2].rearrange("c j h d -> j c h d"), in_=kend_all.t[:]),
                  reads=[kend_all.b], sembuf=kend_all.b)

        def s1_all(tb, third):
            s1(tb, third, (qn2[tb % 2], kn2[tb % 2], vs2[tb % 2])[third])

        for third in range(3):
            s1_all(0, third)
        for tb in range(NTILE):
            q_, k_, v_ = qn2[tb % 2], kn2[tb % 2], vs2[tb % 2]
            nxt = tb + 1 < NTILE
            if nxt:
                s1_all(tb + 1, 0)
            s2a(tb, q_, k_, v_)
            if nxt:
                s1_all(tb + 1, 1)
            for h in range(4):
                s2h(tb, h, q_, k_, v_)
            if nxt:
                s1_all(tb + 1, 2)
            for h in range(4, 8):
                s2h(tb, h, q_, k_, v_)
            s2z(tb)
        P.emit()


def gdn_solve_phase(P, nc, NT, sc, cin):
    NCH = NT // 64
    NSYS = NCH * 8
    NGRP = (NSYS + 127) // 128
    P.new_phase("gsolve")
    A_v = sc["A"].rearrange("c h i j -> (c h) (i j)")
    Tt_v = sc["Tt"].rearrange("c h j i -> (c h) (j i)")
    with ExitStack() as st:
        nset = min(2, NGRP)
        As = [sb(nc, st, f"As{i}", [128, 64, 64], F32) for i in range(nset)]
        Ts = [sb(nc, st, f"Ts{i}", [128, 64, 64], F32) for i in range(nset)]
        Tm = [sb(nc, st, f"Tm{i}", [128, 63, 63], F32) for i in range(nset)]
        Tb = [sb(nc, st, f"Tb{i}", [128, 64, 64], BF16) for i in range(nset)]
        for g0 in range(0, NGRP, nset):
            grp = list(range(g0, min(NGRP, g0 + nset)))
            Rs = {}
            for gi in grp:
                k = gi % nset
                A_, T_ = As[k], Ts[k]
                R = min(128, NSYS - gi * 128)
                Rs[gi] = R
                P.dma("sync", lambda e, A_=A_, gi=gi, R=R: e.dma_start(out=A_.t[0:R].rearrange("p i j -> p (i j)"), in_=A_v[gi * 128:gi * 128 + R, :]),
                      writes=[A_.b], sembuf=A_.b)
                P.dma("sync", lambda e, T_=T_: e.dma_start(out=T_.t[:].rearrange("p i j -> p (i j)"), in_=cin["I64"][:, :]),
                      writes=[T_.b], sembuf=T_.b)
            for i in range(1, 64):
                for gi in grp:
                    k = gi % nset
                    A_, T_, M_, R = As[k], Ts[k], Tm[k], Rs[gi]

                    def mul(e, A_=A_, T_=T_, M_=M_, i=i, R=R):
                        a = A_.t[0:R, i, 0:i]
                        in1 = bass.AP(a.tensor, a.offset, [list(a.ap[0]), [0, i], list(a.ap[1])])
                        in0 = T_.t[0:R, 0:i, 0:i].rearrange("p j c -> p c j")
                        return e.tensor_tensor(out=M_.t[0:R, 0:i, 0:i], in0=in0, in1=in1, op=ALU.mult)
                    P.op("gpsimd" if k == 1 else "vector", mul, reads=[A_.b, T_.b], writes=[M_.b])
                for gi in grp:
                    k = gi % nset
                    T_, M_, R = Ts[k], Tm[k], Rs[gi]
                    P.op("vector", lambda e, T_=T_, M_=M_, i=i, R=R: e.tensor_reduce(out=T_.t[0:R, i, 0:i], in_=M_.t[0:R, 0:i, 0:i], axis=AX.X, op=ALU.add, negate=True),
                         reads=[M_.b], writes=[T_.b])
            for gi in grp:
                k = gi % nset
                B_, T_, R = Tb[k], Ts[k], Rs[gi]
                P.op("vector", lambda e, B_=B_, T_=T_, R=R: e.tensor_copy(out=B_.t[0:R], in_=T_.t[0:R].rearrange("p i j -> p j i")), reads=[T_.b], writes=[B_.b])
                P.dma("sync", lambda e, B_=B_, gi=gi, R=R: e.dma_start(out=Tt_v[gi * 128:gi * 128 + R, :], in_=B_.t[0:R].rearrange("p j i -> p (j i)")),
                      reads=[B_.b], sembuf=B_.b)
        P.emit()


def gdn_scan_phase(P, nc, NT, sc, cin, consts, egl_all, pname="gscan", with_out=True, init_ap=None, final_ap=None):
    NTILE = NT // 128
    P.new_phase(pname)
    ident = consts["ident_f32"]
    with ExitStack() as st:
        kbg = [sb(nc, st, f"kbg{i}", [128, 8, 128], BF16) for i in range(2)]
        vb = [sb(nc, st, f"vb{i}", [128, 8, 128], BF16) for i in range(2)]
        qgT = [sb(nc, st, f"qgT{i}", [128, 8, 128], BF16) for i in range(2)]
        aqk = [sb(nc, st, f"aqk{i}", [64, 2, 8, 64], BF16) for i in range(2)]
        kend = [sb(nc, st, f"kend{i}", [64, 2, 8, 128], BF16) for i in range(2)]
        TtBD = [sb(nc, st, f"TtBD{i}", [128, 8, 128], BF16) for i in range(2)]
        gate = [sb(nc, st, f"gate{i}", [64, 2, 8, 128], F32) for i in range(2)]
        wT = sb(nc, st, "wT", [128, 8, 128], BF16)
        u = sb(nc, st, "u", [64, 2, 8, 128], F32)
        o = sb(nc, st, "o", [64, 2, 8, 128], F32)
        sq = sb(nc, st, "sq", [64, 2, 8, 128], F32)
        S = [sb(nc, st, f"S{h}", [128, 128], F32) for h in range(8)]
        vnew = [sb(nc, st, f"vnew{i}", [64, 128], BF16) for i in range(8)]
        Sb = [sb(nc, st, f"Sb{h}", [128, 128], BF16) for h in range(8)]
        ssq = sb(nc, st, "ssq", [64, 16], F32)
        mh16 = sb(nc, st, "mh16", [64, 16], F32)
        gnb = sb(nc, st, "gnb", [128, 128], F32)
        yTt = [sb(nc, st, f"yTt{i}", [128, 8, 128], BF16) for i in range(2)]
        banks = [st.enter_context(nc.psum_tensor(f"gs_{pname}_bank{i}", [128, 512], F32)) for i in range(8)]
        bankbufs = [Buf(f"bank{i}") for i in range(8)]
        pq = [PQ(banks[i % 8], i // 8, bankbufs[i % 8]) for i in range(32)]
        pqi = [0]

        def nextpq():
            p = pq[pqi[0] % 32]
            pqi[0] += 1
            return p

        P.op("vector", lambda e: e.memset(mh16.t[:], -0.5), writes=[mh16.b])
        P.dma("sync", lambda e: e.dma_start(out=gnb.t[:], in_=cin["gdn_norm_b"][:, :]), writes=[gnb.b], sembuf=gnb.b)
        for i in range(2):
            P.op("gpsimd", lambda e, i=i: e.memset(TtBD[i].t[:], 0.0), writes=[TtBD[i].b])
        for h in range(8):
            if init_ap is None:
                P.op("vector", lambda e, h=h: e.memset(S[h].t[:], 0.0), writes=[S[h].b])
            else:
                if h == 0:
                    flagS = sb(nc, st, "flagS", [128, 1], F32)
                    P.dma("sync", lambda e: e.dma_start(out=flagS.t[:], in_=cin["flag01"][:, :]), writes=[flagS.b], sembuf=flagS.b)
                P.dma("sync", lambda e, h=h: e.dma_start(out=S[h].t[:], in_=init_ap[h]), writes=[S[h].b], sembuf=S[h].b)
                P.op("vector", lambda e, h=h: e.tensor_scalar(out=S[h].t[:], in0=S[h].t[:], scalar1=flagS.t[:, 0:1], scalar2=None, op0=ALU.mult),
                     reads=[S[h].b, flagS.b], writes=[S[h].b])
        for h in range(8):
            P.op("scalar", lambda e, h=h: e.activation(out=Sb[h].t[:], in_=S[h].t[:], func=AF.Copy), reads=[S[h].b], writes=[Sb[h].b])
        for tb in range(NTILE):
            r0 = tb * 128
            k_ = tb % 2
            KB, VB, QG, AQ, KE, TB_, GT, YT = kbg[k_], vb[k_], qgT[k_], aqk[k_], kend[k_], TtBD[k_], gate[k_], yTt[k_]
            P.dma("sync", lambda e, KB=KB, r0=r0: e.dma_start(out=KB.t[:].rearrange("p h d -> p (h d)"), in_=sc["kbg"][r0:r0 + 128, :]), writes=[KB.b], sembuf=KB.b)
            P.dma("sync", lambda e, VB=VB, r0=r0: e.dma_start(out=VB.t[:].rearrange("p h d -> p (h d)"), in_=sc["vb"][r0:r0 + 128, :]), writes=[VB.b], sembuf=VB.b)
            P.dma("sync", lambda e, AQ=AQ, tb=tb: e.dma_start(out=AQ.t[:], in_=sc["aqkT"][tb * 2:tb * 2 + 2].rearrange("c j h i -> j c h i")), writes=[AQ.b], sembuf=AQ.b)
            P.dma("sync", lambda e, KE=KE, tb=tb: e.dma_start(out=KE.t[:], in_=sc["kend"][tb * 2:tb * 2 + 2].rearrange("c j h d -> j c h d")), writes=[KE.b], sembuf=KE.b)
            for c in range(2):
                P.dma("sync", lambda e, TB_=TB_, tb=tb, c=c: e.dma_start(out=TB_.t[c * 64:(c + 1) * 64, :, c * 64:(c + 1) * 64],
                                                                       in_=sc["Tt"][tb * 2 + c].rearrange("h j i -> j h i")), writes=[TB_.b], sembuf=TB_.b)
            if with_out:
                P.dma("sync", lambda e, QG=QG, tb=tb: e.dma_start(out=QG.t[:], in_=sc["qgT"][tb]), writes=[QG.b], sembuf=QG.b)
                P.dma("sync", lambda e, GT=GT, r0=r0: e.dma_start(out=GT.t[:], in_=sc["gate"][r0:r0 + 128, :].rearrange("(c p) (h d) -> p c h d", c=2, h=8)),
                      writes=[GT.b], sembuf=GT.b)
            for h in range(8):
                p = nextpq()
                P.op("tensor", lambda e, p=p, h=h, KB=KB, TB_=TB_: mm32(e, p.ap(), lhsT=KB.t[:, h, :], rhs=TB_.t[:, h, :], start=True, stop=True),
                     reads=[KB.b, TB_.b], writes=[p.b], same_ok=True)
                P.op("scalar", lambda e, p=p, h=h: e.activation(out=wT.t[:, h, :], in_=p.ap(), func=AF.Copy), reads=[p.b], writes=[wT.b])
                for c in range(2):
                    p2 = nextpq()
                    P.op("tensor", lambda e, p=p2, h=h, c=c, VB=VB, TB_=TB_: mm32(e, p.ap(64, 128), lhsT=TB_.t[:, h, c * 64:(c + 1) * 64], rhs=VB.t[:, h, :], start=True, stop=True),
                         reads=[VB.b, TB_.b], writes=[p2.b], same_ok=True)
                    P.op("vector", lambda e, p=p2, h=h, c=c: e.tensor_copy(out=u.t[:, c, h, :], in_=p.ap(64, 128)), reads=[p2.b], writes=[u.b])
            for c in range(2):
                cg = tb * 2 + c
                pws = [nextpq() for _ in range(8)]
                for h in range(8):
                    P.op("tensor", lambda e, p=pws[h], h=h, c=c: mm32(e, p.ap(64, 128), lhsT=wT.t[:, h, c * 64:(c + 1) * 64], rhs=Sb[h].t[:], start=True, stop=True),
                         reads=[wT.b, Sb[h].b], writes=[pws[h].b], same_ok=True)
                for h in range(8):
                    P.op("vector", lambda e, p=pws[h], h=h, c=c: e.tensor_tensor(out=vnew[h].t[:], in0=u.t[:, c, h, :], in1=p.ap(64, 128), op=ALU.subtract),
                         reads=[u.b, pws[h].b], writes=[vnew[h].b])
                for h in range(8):
                    if with_out:
                        po = nextpq()
                        P.op("tensor", lambda e, p=po, h=h, c=c, QG=QG: mm32(e, p.ap(64, 128), lhsT=QG.t[:, h, c * 64:(c + 1) * 64], rhs=Sb[h].t[:], start=True, stop=False),
                             reads=[QG.b, Sb[h].b], writes=[po.b], same_ok=True)
                        P.op("tensor", lambda e, p=po, h=h, c=c, AQ=AQ: mm32(e, p.ap(64, 128), lhsT=AQ.t[:, c, h, :], rhs=vnew[h].t[:], start=False, stop=True),
                             reads=[AQ.b, vnew[h].b], writes=[po.b], same_ok=True)
                        P.op("scalar", lambda e, p=po, h=h, c=c: e.activation(out=o.t[:, c, h, :], in_=p.ap(64, 128), func=AF.Copy), reads=[po.b], writes=[o.b])
                    psu = nextpq()
                    P.op("tensor", lambda e, p=psu, h=h, c=c, KE=KE: mm32(e, p.ap(), lhsT=KE.t[:, c, h, :], rhs=vnew[h].t[:], start=True, stop=True),
                         reads=[KE.b, vnew[h].b], writes=[psu.b], same_ok=True)
                    P.op("vector", lambda e, p=psu, h=h, cg=cg: e.scalar_tensor_tensor(out=S[h].t[:], in0=S[h].t[:], scalar=egl_all.t[:, cg, h:h + 1], in1=p.ap(),
                                                                                     op0=ALU.mult, op1=ALU.add),
                         reads=[S[h].b, psu.b, egl_all.b], writes=[S[h].b])
                    P.op("scalar", lambda e, h=h: e.activation(out=Sb[h].t[:], in_=S[h].t[:], func=AF.Copy), reads=[S[h].b], writes=[Sb[h].b])
            if with_out:
                o16 = o.t[:].rearrange("p c h d -> p (c h) d")
                sq16 = sq.t[:].rearrange("p c h d -> p (c h) d")
                g16 = GT.t[:].rearrange("p c h d -> p (c h) d")
                P.op("gpsimd", lambda e, o16=o16, sq16=sq16: e.tensor_tensor(out=sq16, in0=o16, in1=o16, op=ALU.mult), reads=[o.b], writes=[sq.b])
                P.op("vector", lambda e, sq16=sq16: e.tensor_reduce(out=ssq.t[:], in_=sq16, axis=AX.X, op=ALU.add), reads=[sq.b], writes=[ssq.b])
                P.op("vector", lambda e: e.tensor_scalar(out=ssq.t[:], in0=ssq.t[:], scalar1=1.0 / 128.0, scalar2=EPS, op0=ALU.mult, op1=ALU.add),
                     reads=[ssq.b], writes=[ssq.b])
                P.op("gpsimd", lambda e: e.tensor_tensor(out=ssq.t[:], in0=ssq.t[:], in1=mh16.t[:], op=ALU.pow), reads=[ssq.b, mh16.b], writes=[ssq.b])
                P.op("vector", lambda e, o16=o16: e.tensor_tensor(out=o16, in0=o16, in1=bcast_last(ssq.t[:, :], 128), op=ALU.mult), reads=[o.b, ssq.b], writes=[o.b])
                gn = gnb.t[0:64, :]
                gnb16 = bass.AP(gn.tensor, gn.offset, [list(gn.ap[0]), [0, 16], list(gn.ap[1])])
                P.op("gpsimd", lambda e, o16=o16, gnb16=gnb16: e.tensor_tensor(out=o16, in0=o16, in1=gnb16, op=ALU.mult), reads=[o.b, gnb.b], writes=[o.b])
                P.op("scalar", lambda e, g16=g16: e.activation(out=g16, in_=g16, func=AF.Silu), reads=[GT.b], writes=[GT.b])
                P.op("vector", lambda e, o16=o16, g16=g16: e.tensor_tensor(out=o16, in0=o16, in1=g16, op=ALU.mult), reads=[o.b, GT.b], writes=[o.b])
                for h in range(8):
                    p = nextpq()
                    for c in range(2):
                        P.op("tensor", lambda e, p=p, h=h, c=c: e.transpose(out=p.bank[0:128, p.q * 128 + c * 64:p.q * 128 + (c + 1) * 64], in_=o.t[:, c, h, :],
                                                                           identity=ident.t[0:64, 0:64]),
                             reads=[o.b, ident.b], writes=[p.b], same_ok=True)
                    if h % 2 == 0:
                        P.op("scalar", lambda e, p=p, h=h, YT=YT: e.activation(out=YT.t[:, h, :], in_=p.ap(), func=AF.Copy), reads=[p.b], writes=[YT.b])
                    else:
                        P.op("vector", lambda e, p=p, h=h, YT=YT: e.tensor_copy(out=YT.t[:, h, :], in_=p.ap()), reads=[p.b], writes=[YT.b])
                P.dma("sync", lambda e, YT=YT, r0=r0: e.dma_start(out=sc["yT"][8:16, :, r0:r0 + 128].rearrange("h d t -> d h t"), in_=YT.t[:]),
                      reads=[YT.b], sembuf=YT.b)
        if final_ap is not None:
            for h in range(8):
                P.dma("sync", lambda e, h=h: e.dma_start(out=final_ap[h], in_=S[h].t[:]), reads=[S[h].b], sembuf=S[h].b)
        P.emit()


def wout_phase(P, nc, NT, sc, w_out):
    NTILE = NT // 128
    P.new_phase("wout")
    with ExitStack() as st:
        wo = sb(nc, st, "wo", [128, 16, D], BF16)
        yt = [sb(nc, st, f"yt{i}", [128, 16, 128], BF16) for i in range(2)]
        xr = [sb(nc, st, f"xr{i}", [128, D], F32) for i in range(2)]
        xo = [sb(nc, st, f"xo{i}", [128, D], F32) for i in range(2)]
        pbank = [ps(nc, st, f"pb{i}", [128, 512]) for i in range(8)]
        w_v = w_out.rearrange("(c p) n -> p c n", p=128)
        for q4 in range(4):
            P.dma("gpsimd", lambda e, q4=q4: e.dma_start(out=wo.t[:, q4 * 4:(q4 + 1) * 4, :], in_=w_v[:, q4 * 4:(q4 + 1) * 4, :]), writes=[wo.b], sembuf=wo.b)
        for tb in range(NTILE):
            r0 = tb * 128
            YT, XR, XO = yt[tb % 2], xr[tb % 2], xo[tb % 2]
            P.dma("sync", lambda e, YT=YT, r0=r0: e.dma_start(out=YT.t[:], in_=sc["yT"][:, :, r0:r0 + 128].rearrange("c d t -> d c t")), writes=[YT.b], sembuf=YT.b)
            P.dma("sync", lambda e, XR=XR, r0=r0: e.dma_start(out=XR.t[:], in_=sc["x1"][r0:r0 + 128, :]), writes=[XR.b], sembuf=XR.b)
            for n in range(4):
                PB = pbank[(tb * 4 + n) % 8]
                for c in range(16):
                    P.op("tensor", lambda e, PB=PB, YT=YT, c=c, n=n: e.matmul(PB.t[:, :], lhsT=YT.t[:, c, :], rhs=wo.t[:, c, n * 512:(n + 1) * 512],
                                                                           start=(c == 0), stop=(c == 15)),
                         reads=[YT.b, wo.b], writes=[PB.b], same_ok=True)
                P.op("vector", lambda e, PB=PB, XR=XR, XO=XO, n=n: e.tensor_tensor(out=XO.t[:, n * 512:(n + 1) * 512], in0=PB.t[:, :], in1=XR.t[:, n * 512:(n + 1) * 512], op=ALU.add),
                     reads=[PB.b, XR.b], writes=[XO.b])
            P.dma("sync", lambda e, XO=XO, r0=r0: e.dma_start(out=sc["x2"][r0:r0 + 128, :], in_=XO.t[:]), reads=[XO.b], sembuf=XO.b)
        P.emit()


def fnorm_phase(P, nc, NT, src, dst, gain_b_d, consts):
    NTILE = NT // 128
    P.new_phase("fnorm")
    mhalf = consts["mhalf"]
    with ExitStack() as st:
        gb = sb(nc, st, "gb", [128, D], F32)
        xs = [sb(nc, st, f"xs{i}", [128, D], F32) for i in range(3)]
        xq = sb(nc, st, "xq", [128, D], F32)
        ssq = [sb(nc, st, f"ssq{i}", [128, 1], F32) for i in range(3)]
        P.dma("sync", lambda e: e.dma_start(out=gb.t[:], in_=gain_b_d[:, :]), writes=[gb.b], sembuf=gb.b)
        for tb in range(NTILE):
            r0 = tb * 128
            X, SS = xs[tb % 3], ssq[tb % 3]
            P.dma("sync", lambda e, X=X, r0=r0: e.dma_start(out=X.t[:], in_=src[r0:r0 + 128, :]), writes=[X.b], sembuf=X.b)
            P.op("scalar", lambda e, X=X, SS=SS: e.activation(out=xq.t[:], in_=X.t[:], func=AF.Square, accum_out=SS.t[:]), reads=[X.b], writes=[xq.b, SS.b])
            P.op("vector", lambda e, SS=SS: e.tensor_scalar(out=SS.t[:], in0=SS.t[:], scalar1=1.0 / D, scalar2=EPS, op0=ALU.mult, op1=ALU.add), reads=[SS.b], writes=[SS.b])
            P.op("gpsimd", lambda e, SS=SS: e.tensor_tensor(out=SS.t[:], in0=SS.t[:], in1=mhalf.t[:], op=ALU.pow), reads=[SS.b, mhalf.b], writes=[SS.b])
            P.op("vector", lambda e, X=X, SS=SS: e.scalar_tensor_tensor(out=X.t[:], in0=X.t[:], scalar=SS.t[:, 0:1], in1=gb.t[:], op0=ALU.mult, op1=ALU.mult),
                 reads=[X.b, SS.b, gb.b], writes=[X.b])
            P.dma("sync", lambda e, X=X, r0=r0: e.dma_start(out=dst[r0:r0 + 128, :], in_=X.t[:]), reads=[X.b], sembuf=X.b)
        P.emit()


PAIRS = [[0, 1], [2, 3], [4, 5], [6, 7]]


def xchg_phase(P, nc, name, ccsem, cccount, items, pre_copy=None):
    P.new_phase(name)
    if pre_copy is not None:
        dummy = Buf("cp")
        for (src_ap, dst_ap) in pre_copy:
            P.dma("gpsimd", lambda e, src_ap=src_ap, dst_ap=dst_ap: e.dma_start(out=dst_ap, in_=src_ap), writes=[dummy], sembuf=dummy)
    prev = [None]
    for (src, dst) in items:
        cccount[0] += 1
        n = cccount[0]

        def fn(e, src=src, dst=dst, n=n):
            e.collective_compute("AllGather", ALU.bypass, replica_groups=PAIRS,
                                 ins=[src.ap().opt()], outs=[dst.ap().opt()]).then_inc(ccsem)
            return e.wait_ge(ccsem, n)
        b = Buf("cc")
        reads = [dummy] if pre_copy is not None else []
        P.op("gpsimd", fn, reads=reads, writes=[b])
    P.emit()


class LazyIn(dict):
    def __init__(self, nc, shapes):
        super().__init__()
        self.nc = nc
        self.shapes = shapes

    def __missing__(self, name):
        ap = self.nc.dram_tensor(name, list(self.shapes[name]), F32, kind="ExternalInput").ap()
        self[name] = ap
        return ap


def input_shapes(NT):
    sh = {"x": [NT, D], "w_in": [D, 7184], "w_out": [D, D]}
    for nm in ("ffn1_norm", "mix_norm", "ffn2_norm", "final_norm_t"):
        sh[nm] = [128, NKC]
    for pre in ("ffn1", "ffn2"):
        sh[pre + "_w_gate"] = [D, DFF]
        sh[pre + "_w_up"] = [D, DFF]
        sh[pre + "_w_down"] = [DFF, D]
    sh.update({"ident_f32": [128, 128], "trineg": [128, 128], "tricomp": [128, 128], "negones": [128, 128], "ones32": [128, 128],
               "dmask": [4, 128, 512], "negbias": [128, 1], "sb_out_norm": [128, 1],
               "final_norm_b": [128, D], "convw_b": [128, 4 * 3072], "alog_b": [128, 8], "dtb_b": [128, 8],
               "BT": [128, 128], "BL": [128, 128], "selC": [128, 256], "selH": [8, 1024], "MUs": [128, 128],
               "ML64": [64, 64], "MS01": [128, 128], "ML01": [64, 64], "I64": [128, 4096], "gdn_norm_b": [128, 128], "flag01": [128, 1]})
    return sh


def const_inputs(j):
    c = {}
    c["ident_f32"] = np.eye(128, dtype=np.float32)
    jj = np.arange(128)[:, None]
    ss = np.arange(128)[None, :]
    c["trineg"] = np.where(jj >= ss, -1.0, 0.0).astype(np.float32)
    c["negones"] = -np.ones((128, 128), np.float32)
    c["tricomp"] = np.where(jj < ss, -1.0, 0.0).astype(np.float32)
    c["ones32"] = np.ones((128, 128), np.float32)
    t = np.arange(512)[None, None, :]
    r = np.arange(4)[:, None, None]
    sp = np.arange(128)[None, :, None]
    c["dmask"] = ((r * 128 + sp) < t).astype(np.float32)
    c["negbias"] = np.full((128, 1), 0.0 if j == 1 else -1.0e4, np.float32)
    c["flag01"] = np.full((128, 1), 1.0 if j == 1 else 0.0, np.float32)
    ch = np.arange(128) // 64
    same = ch[:, None] == ch[None, :]
    ii = np.arange(128)
    c["BT"] = (same & (ii[:, None] <= ii[None, :])).astype(np.float32)
    c["BL"] = same.astype(np.float32)
    selC = np.zeros((128, 2, 128), np.float32)
    selC[:64, 0, :] = 1.0
    selC[64:, 1, :] = 1.0
    c["selC"] = selC.reshape(128, 256)
    selH = np.zeros((8, 8, 128), np.float32)
    for h in range(8):
        selH[h, h, :] = 1.0
    c["selH"] = selH.reshape(8, 1024)
    c["MUs"] = np.where(same & (ii[None, :] < ii[:, None]), 0.0, 1.0e4).astype(np.float32)
    i64 = np.arange(64)
    c["ML64"] = np.where(i64[:, None] <= i64[None, :], 0.0, -1.0e4).astype(np.float32)
    c["MS01"] = (same & (ii[None, :] < ii[:, None])).astype(np.float32)
    c["ML01"] = (i64[:, None] <= i64[None, :]).astype(np.float32)
    c["I64"] = np.tile(np.eye(64, dtype=np.float32).reshape(1, 4096), (128, 1))
    return c


def build_program(NT=2048, NPREV=0, phases=("ffn1", "proj", "attn", "gdn", "wout", "ffn2", "fnorm"), dbg=(), exchange=False):
    nc = bass.Bass("TRN2", target_bir_lowering=False)
    cin = LazyIn(nc, input_shapes(NT))

    def dsc(name, shape, dtype=F32, ap=True):
        kind = "ExternalOutput" if name in dbg else "Internal"
        t = nc.dram_tensor(name, list(shape), dtype, kind=kind)
        return t.ap() if ap else t

    NK = NPREV + NT
    NCH = NT // 64
    out = nc.dram_tensor("out", [NT, D], F32, kind="ExternalOutput").ap()
    sc = {}
    sc["x1"] = dsc("x1", [NT, D]) if "ffn1" in phases else cin["x"]
    sc["x2"] = dsc("x2", [NT, D])
    sc["x3"] = dsc("x3", [NT, D])
    sc["qT"] = dsc("qT", [8, 128, NT], BF16)
    xg = None
    if exchange:
        sc["kT"] = dsc("kT", [8, 128, NT], BF16)
        sc["v"] = dsc("v", [NT, 1024], BF16)
        HK = 4 * 128
        HV = NT // 2
        ks_t = [dsc(f"ksrc{i}", [HK, NT], BF16, ap=False) for i in range(2)]
        kd_t = [dsc(f"kdst{i}", [2 * HK, NT], BF16, ap=False) for i in range(2)]
        vs_t = [dsc(f"vsrc{i}", [HV, 1024], BF16, ap=False) for i in range(2)]
        vd_t = [dsc(f"vdst{i}", [2 * HV, 1024], BF16, ap=False) for i in range(2)]
        hs_t = dsc("hist_src", [3, 3072], F32, ap=False)
        hd_t = dsc("hist_dst", [6, 3072], F32, ap=False)
        ss_t = dsc("st_src", [8 * 128, 128], F32, ap=False)
        sd_t = dsc("st_dst", [2 * 8 * 128, 128], F32, ap=False)
        xg = {"kd": [t.ap() for t in kd_t], "vd": [t.ap() for t in vd_t], "hist_dst": hd_t.ap(), "HV": HV}
    else:
        sc["kT"] = dsc("kT", [8, 128, NK], BF16)
        sc["v"] = dsc("v", [NK, 1024], BF16)
    sc["graw"] = dsc("graw", [3 + NT, 3072])
    sc["ab"] = dsc("ab", [NT, 16])
    sc["gate"] = dsc("gate", [NT, 1024])
    sc["yT"] = dsc("yT", [16, 128, NT], BF16)
    sc["gcrow"] = dsc("gcrow", [NT // 128, 8, 128])
    sc["kbg"] = dsc("kbg", [NT, 1024], BF16)
    sc["vb"] = dsc("vb", [NT, 1024], BF16)
    sc["qgT"] = dsc("qgT", [NT // 128, 128, 8, 128], BF16)
    sc["aqkT"] = dsc("aqkT", [NCH, 64, 8, 64], BF16)
    sc["kend"] = dsc("kend", [NCH, 64, 8, 128], BF16)
    sc["A"] = dsc("A", [NCH, 8, 64, 64])
    sc["Tt"] = dsc("Tt", [NCH, 8, 64, 64], BF16)

    with ExitStack() as stack:
        P = Prog(nc, stack)
        ccsem = stack.enter_context(nc.semaphore("ccsem"))
        cccount = [0]
        consts = {}
        consts["ident_f32"] = sb(nc, stack, "ident_f32", [128, 128], F32)
        consts["mhalf"] = sb(nc, stack, "mhalf", [128, 1], F32)
        P.new_phase("const")
        c = consts["ident_f32"]
        P.dma("sync", lambda e: e.dma_start(out=c.t[:], in_=cin["ident_f32"][:, :]), writes=[c.b], sembuf=c.b)
        m = consts["mhalf"]
        P.op("vector", lambda e: e.memset(m.t[:], -0.5), writes=[m.b])
        P.emit()

        if "ffn1" in phases:
            ffn_phase(P, nc, "ffn1", NT, cin["x"], sc["x1"], cin["ffn1_norm"], cin["ffn1_w_gate"], cin["ffn1_w_up"],
                      cin["ffn1_w_down"], consts)
        if "proj" in phases:
            proj_phase(P, nc, NT, 0 if exchange else NPREV, sc["x1"], cin["mix_norm"], cin["w_in"], sc, consts)
        if exchange:
            kflat = sc["kT"].rearrange("h d n -> (h d) n")
            pre = [(kflat[i * HK:(i + 1) * HK, :], ks_t[i].ap()) for i in range(2)]
            pre += [(sc["v"][i * HV:(i + 1) * HV, :], vs_t[i].ap()) for i in range(2)]
            pre += [(sc["graw"][NT:NT + 3, :], hs_t.ap())]
            xchg_phase(P, nc, "xchg1", ccsem, cccount,
                       [(ks_t[0], kd_t[0]), (ks_t[1], kd_t[1]), (vs_t[0], vd_t[0]), (vs_t[1], vd_t[1]), (hs_t, hd_t)], pre_copy=pre)
        if "attn" in phases:
            attn_phase(P, nc, NT, NPREV, sc, cin, consts, xg=xg)
        if "gdn" in phases:
            egl_all = sb(nc, stack, "egl_all", [128, NCH, 8], F32)
            gdn_prep_phase(P, nc, NT, sc, cin, consts, egl_all, xg=xg)
            if "nosolve" not in phases:
                gdn_solve_phase(P, nc, NT, sc, cin)
            if "noscan" not in phases:
                if exchange:
                    ss_v = ss_t.ap().rearrange("(h k) v -> h k v", h=8)
                    sd_v = sd_t.ap().rearrange("(r h k) v -> r h k v", r=2, h=8)
                    gdn_scan_phase(P, nc, NT, sc, cin, consts, egl_all, pname="gscanA", with_out=False, final_ap=ss_v)
                    xchg_phase(P, nc, "xchg2", ccsem, cccount, [(ss_t, sd_t)])
                    gdn_scan_phase(P, nc, NT, sc, cin, consts, egl_all, pname="gscanB", with_out=True, init_ap=sd_v[0])
                else:
                    gdn_scan_phase(P, nc, NT, sc, cin, consts, egl_all)
        if "wout" in phases:
            wout_phase(P, nc, NT, sc, cin["w_out"])
        if "ffn2" in phases:
            ffn_phase(P, nc, "ffn2", NT, sc["x2"], sc["x3"], cin["ffn2_norm"], cin["ffn2_w_gate"], cin["ffn2_w_up"],
                      cin["ffn2_w_down"], consts)
        if "fnorm" in phases:
            fnorm_phase(P, nc, NT, sc["x3"], out, cin["final_norm_b"], consts)
    return nc


MODE = "T"
_CACHE = {}


def kernel(**inputs):
    f32 = lambda a: np.ascontiguousarray(np.asarray(a, dtype=np.float32))
    x = f32(inputs["x"])
    B, S, _ = x.shape
    NT = S if MODE == "A" else S // 2
    NPREV = 0 if MODE == "A" else S // 2
    key = (MODE, NT, NPREV)
    if key not in _CACHE:
        _CACHE[key] = build_program(NT=NT, NPREV=NPREV, exchange=(MODE == "T"))
    nc = _CACHE[key]

    def nt(v):
        return np.ascontiguousarray(f32(v).reshape(NKC, 128).T)

    shared = {}
    shared["ffn1_norm"] = nt(inputs["ffn1_norm"][0])
    shared["mix_norm"] = nt(inputs["mix_norm"][0])
    shared["ffn2_norm"] = nt(inputs["ffn2_norm"][0])
    shared["final_norm_b"] = np.ascontiguousarray(np.tile(f32(inputs["final_norm"]).reshape(1, D), (128, 1)))
    for pre in ("ffn1", "ffn2"):
        for w in ("w_gate", "w_up", "w_down"):
            shared[f"{pre}_{w}"] = f32(inputs[f"{pre}_{w}"][0])
    shared["w_in"] = f32(inputs["w_in"][0])
    shared["w_out"] = f32(inputs["w_out"][0])
    shared["sb_out_norm"] = f32(inputs["sb_out_norm"][0]).reshape(128, 1)
    shared["convw_b"] = np.ascontiguousarray(np.tile(f32(inputs["conv_w"][0]).reshape(1, -1), (128, 1)))
    shared["alog_b"] = np.ascontiguousarray(np.tile(f32(inputs["a_log"][0]).reshape(1, 8), (128, 1)))
    shared["dtb_b"] = np.ascontiguousarray(np.tile(f32(inputs["dt_bias"][0]).reshape(1, 8), (128, 1)))
    shared["gdn_norm_b"] = np.ascontiguousarray(np.tile(f32(inputs["gdn_out_norm"][0]).reshape(1, 128), (128, 1)))
    consts = [const_inputs(0), const_inputs(1)]
    in_maps = []
    for c in range(8):
        b, j = c // 2, c % 2
        m = dict(shared)
        m.update(consts[j])
        if MODE == "A":
            m["x"] = x[b]
        else:
            m["x"] = np.ascontiguousarray(x[b, j * NT:(j + 1) * NT])
        in_maps.append(m)
    res = run_bass_kernel_spmd(nc, in_maps, core_ids=list(range(8)))
    out = np.empty((B, S, D), np.float32)
    for c in range(8):
        b, j = c // 2, c % 2
        if MODE == "A":
            if j == 0:
                out[b] = res.results[c]["out"]
        else:
            out[b, j * NT:(j + 1) * NT] = res.results[c]["out"]
    return out
```
